# Optimizing a Trainium2 kernel written in Bass

```python
import jax, jax.numpy as jnp
from jax import lax
import numpy as np

D_MODEL = 4096
BATCH = 4
SEQ = 4096
DEPTH = 4

CTX_LEN = 256
GRID_W = 64
HEAD_DIM = 128
NA_HEADS = 16
NA_KH = 8
NA_KW = 16
NA_QCB = 16
WA_HEADS = 16
WA_KV_HEADS = 4
WA_WINDOW = 128
WA_BLOCK = 128
FNET_GROUPS = 16
FNET_GROUP_DIM = 128
FNET_WIDTH = FNET_GROUPS * FNET_GROUP_DIM
SC_WIDTH = 2048
NA_WIDTH = NA_HEADS * HEAD_DIM
WA_Q_WIDTH = WA_HEADS * HEAD_DIM
WA_KV_WIDTH = WA_KV_HEADS * HEAD_DIM
MIX_WIDTH = NA_WIDTH + WA_Q_WIDTH
EVEN_PROJ = 3 * NA_WIDTH + WA_Q_WIDTH + 2 * WA_KV_WIDTH
ODD_PROJ = FNET_WIDTH + 3 * SC_WIDTH
EVEN_SPLITS = [NA_WIDTH, 2 * NA_WIDTH, 3 * NA_WIDTH, 3 * NA_WIDTH + WA_Q_WIDTH,
               3 * NA_WIDTH + WA_Q_WIDTH + WA_KV_WIDTH]
ODD_SPLITS = [FNET_WIDTH, FNET_WIDTH + SC_WIDTH, FNET_WIDTH + 2 * SC_WIDTH]
FFN_HIDDEN = 5632
CONV_W = 3
ROPE_THETA = 10000.0
RMS_EPS = 1e-6
NEG_INF = -1e30
MOD_INIT = 0.5
N_EVEN = (DEPTH + 1) // 2
N_ODD = DEPTH // 2

kernel_name = "hybrid_na_wgqa_fnet_shortconv_dit_trunk"


def rmsnorm(x, g):
    xf = x.astype(jnp.float32)
    y = xf * lax.rsqrt(jnp.mean(xf * xf, axis=-1, keepdims=True) + RMS_EPS)
    return (y * g.astype(jnp.float32)).astype(x.dtype)


def modulate(h, shift, scale):
    return h * (1 + scale) + shift


def dwconv3(u, w):
    up = jnp.pad(u, ((0, 0), (1, 1), (0, 0)))
    return up[:, :-2] * w[0] + up[:, 1:-1] * w[1] + up[:, 2:] * w[2]


def axial_rope_tables(S):
    t = jnp.arange(S)
    row = (t // GRID_W).astype(jnp.float32)
    col = (t % GRID_W).astype(jnp.float32)
    nf = HEAD_DIM // 4
    inv = ROPE_THETA ** (-jnp.arange(nf, dtype=jnp.float32) / nf)
    ang = jnp.stack([row[:, None] * inv, col[:, None] * inv], axis=1)
    return jnp.cos(ang), jnp.sin(ang)


def apply_axial_rope(x, cos, sin):
    B, S, H, hd = x.shape
    xa = x.reshape(B, S, H, 2, 2, hd // 4)
    x1, x2 = xa[..., 0, :], xa[..., 1, :]
    c = cos[None, :, None].astype(x.dtype)
    s = sin[None, :, None].astype(x.dtype)
    out = jnp.stack([x1 * c - x2 * s, x2 * c + x1 * s], axis=-2)
    return out.reshape(B, S, H, hd)


def ctx_attn(q, k, v, sink=None):
    B, L, H, hd = q.shape
    kvh = k.shape[2]
    g = H // kvh
    qg = q.reshape(B, L, kvh, g, hd)
    s = jnp.einsum('bqkgd,blkd->bkgql', qg, k).astype(jnp.float32) * (hd ** -0.5)
    if sink is not None:
        s_sink = jnp.broadcast_to(sink.astype(jnp.float32).reshape(1, kvh, g, 1, 1), s.shape[:-1] + (1,))
        s = jnp.concatenate([s, s_sink], axis=-1)
    p = jax.nn.softmax(s, axis=-1).astype(v.dtype)
    if sink is not None:
        p = p[..., :-1]
    o = jnp.einsum('bkgql,blkd->bqkgd', p, v)
    return o.reshape(B, L, H, hd)


def neighborhood_attn(q, k, v, kc, vc, rpb):
    B, S, H, hd = q.shape
    rows = S // GRID_W
    kh = min(NA_KH, rows)
    ncb = GRID_W // NA_QCB
    kcb = NA_QCB + NA_KW
    qg = q.reshape(B, rows, GRID_W, H, hd)
    kg = k.reshape(B, rows, GRID_W, H, hd)
    vg = v.reshape(B, rows, GRID_W, H, hd)
    qcol = jnp.arange(GRID_W).reshape(ncb, NA_QCB)
    cstart = jnp.clip(qcol - NA_KW // 2, 0, GRID_W - NA_KW)
    bstart = jnp.clip(jnp.arange(ncb) * NA_QCB - NA_KW // 2, 0, GRID_W - kcb)
    kcol = bstart[:, None] + jnp.arange(kcb)
    col_ok = (kcol[:, None, :] >= cstart[:, :, None]) & (kcol[:, None, :] < cstart[:, :, None] + NA_KW)
    col_rel = jnp.clip(kcol[:, None, :] - qcol[:, :, None], -(NA_KW - 1), NA_KW - 1) + NA_KW - 1
    scale = hd ** -0.5
    nloc = kh * kcb

    def row_block(r):
        rs = jnp.clip(r - kh // 2, 0, rows - kh)
        kr = lax.dynamic_slice_in_dim(kg, rs, kh, axis=1)[:, :, kcol]
        vr = lax.dynamic_slice_in_dim(vg, rs, kh, axis=1)[:, :, kcol]
        qr = lax.dynamic_index_in_dim(qg, r, axis=1, keepdims=False).reshape(B, ncb, NA_QCB, H, hd)
        row_rel = rs + jnp.arange(kh) - r + NA_KH - 1
        bias = rpb[:, row_rel[:, None, None, None], col_rel[None]]
        bias = bias.transpose(0, 2, 3, 1, 4).astype(jnp.float32)
        s_loc = jnp.einsum('bnqhd,bjnkhd->bhnqjk', qr, kr).astype(jnp.float32) * scale + bias
        s_loc = jnp.where(col_ok[:, :, None, :], s_loc, NEG_INF)
        s_ctx = jnp.einsum('bnqhd,blhd->bhnql', qr, kc).astype(jnp.float32) * scale
        s = jnp.concatenate([s_loc.reshape(B, H, ncb, NA_QCB, nloc), s_ctx], axis=-1)
        p = jax.nn.softmax(s, axis=-1).astype(v.dtype)
        p_loc = p[..., :nloc].reshape(B, H, ncb, NA_QCB, kh, kcb)
        o = (jnp.einsum('bhnqjk,bjnkhd->bnqhd', p_loc, vr)
             + jnp.einsum('bhnql,blhd->bnqhd', p[..., nloc:], vc))
        return o.reshape(B, GRID_W, H, hd)

    out = lax.map(row_block, jnp.arange(rows))
    return out.transpose(1, 0, 2, 3, 4).reshape(B, S, H, hd)


def window_gqa(q, k, v, kc, vc, sink):
    B, S, H, hd = q.shape
    kvh = k.shape[2]
    g = H // kvh
    nb = S // WA_BLOCK
    L = kc.shape[1]
    qb = q.reshape(B, nb, WA_BLOCK, kvh, g, hd)

    def band(t):
        tp = jnp.pad(t, ((0, 0), (WA_BLOCK, WA_BLOCK), (0, 0), (0, 0)))
        return jnp.concatenate(
            [tp[:, j * WA_BLOCK: j * WA_BLOCK + S].reshape(B, nb, WA_BLOCK, kvh, hd) for j in range(3)], axis=2)

    kband, vband = band(k), band(v)
    nl = 3 * WA_BLOCK
    ql = jnp.arange(WA_BLOCK)
    kl = jnp.arange(nl)
    off = kl[None, :] - WA_BLOCK - ql[:, None]
    kpos = jnp.arange(nb)[:, None] * WA_BLOCK - WA_BLOCK + kl[None, :]
    ok = (jnp.abs(off)[None] <= WA_WINDOW) & ((kpos >= 0) & (kpos < S))[:, None, :]
    scale = hd ** -0.5
    s_loc = jnp.einsum('bnqkgd,bnjkd->bkgnqj', qb, kband).astype(jnp.float32) * scale
    s_loc = jnp.where(ok, s_loc, NEG_INF)
    s_ctx = jnp.einsum('bnqkgd,blkd->bkgnql', qb, kc).astype(jnp.float32) * scale
    s_sink = jnp.broadcast_to(sink.astype(jnp.float32).reshape(1, kvh, g, 1, 1, 1), s_loc.shape[:-1] + (1,))
    p = jax.nn.softmax(jnp.concatenate([s_loc, s_ctx, s_sink], axis=-1), axis=-1).astype(v.dtype)
    o = (jnp.einsum('bkgnqj,bnjkd->bnqkgd', p[..., :nl], vband)
         + jnp.einsum('bkgnql,blkd->bnqkgd', p[..., nl:nl + L], vc))
    return o.reshape(B, S, H, hd)


def even_mixer(hl, hc, w_in, rpb, sink, w_out, cos, sin, ctx_live):
    B, S, _ = hl.shape

    def heads(p):
        T = p.shape[1]
        qa, ka, va, qb, kb, vb = jnp.split(p, EVEN_SPLITS, axis=-1)
        return (qa.reshape(B, T, NA_HEADS, HEAD_DIM), ka.reshape(B, T, NA_HEADS, HEAD_DIM),
                va.reshape(B, T, NA_HEADS, HEAD_DIM), qb.reshape(B, T, WA_HEADS, HEAD_DIM),
                kb.reshape(B, T, WA_KV_HEADS, HEAD_DIM), vb.reshape(B, T, WA_KV_HEADS, HEAD_DIM))

    qa_l, ka_l, va_l, qb_l, kb_l, vb_l = heads(hl @ w_in)
    qa_c, ka_c, va_c, qb_c, kb_c, vb_c = heads(hc @ w_in)
    qb_l = apply_axial_rope(qb_l, cos, sin)
    kb_l = apply_axial_rope(kb_l, cos, sin)
    oa = neighborhood_attn(qa_l, ka_l, va_l, ka_c, va_c, rpb)
    ob = window_gqa(qb_l, kb_l, vb_l, kb_c, vb_c, sink)
    yl = jnp.concatenate([oa.reshape(B, S, NA_WIDTH), ob.reshape(B, S, WA_Q_WIDTH)], axis=-1) @ w_out
    yc = None
    if ctx_live:
        L = hc.shape[1]
        oac = ctx_attn(qa_c, ka_c, va_c)
        obc = ctx_attn(qb_c, kb_c, vb_c, sink)
        yc = jnp.concatenate([oac.reshape(B, L, NA_WIDTH), obc.reshape(B, L, WA_Q_WIDTH)], axis=-1) @ w_out
    return yl, yc


def odd_mixer(h, w_in, conv_w, w_out):
    B, T, _ = h.shape
    u, bg, cg, hv = jnp.split(h @ w_in, ODD_SPLITS, axis=-1)
    uf = u.reshape(B, T, FNET_GROUPS, FNET_GROUP_DIM).astype(jnp.float32)
    f = jnp.fft.fft2(uf, axes=(1, 3), norm="ortho").real.astype(h.dtype).reshape(B, T, FNET_WIDTH)
    sc = bg * dwconv3(cg * hv, conv_w)
    return jnp.concatenate([f, sc], axis=-1) @ w_out


def conv_ffn(h, w_up, conv_w, w_down):
    u = dwconv3(h @ w_up, conv_w)
    gate, val = jnp.split(u, 2, axis=-1)
    return (jax.nn.silu(gate) * val) @ w_down


def setup_inputs(seed: int = 0) -> dict:
    key = jax.random.key(seed)
    ks = jax.random.split(key, 20)
    D = D_MODEL

    def nrm(k, shape, s):
        return jax.random.normal(k, shape, jnp.float32) * s

    return {
        "x": nrm(ks[0], (BATCH, SEQ, D), 1.0),
        "c": nrm(ks[1], (BATCH, D), 1.0),
        "ctx": nrm(ks[2], (BATCH, CTX_LEN, D), 1.0),
        "c_ctx": nrm(ks[3], (D,), 1.0),
        "w_mod": nrm(ks[4], (DEPTH, D, 6 * D), MOD_INIT * D ** -0.5),
        "b_mod": nrm(ks[5], (DEPTH, 6 * D), 0.02),
        "g_mix_pre": 1.0 + nrm(ks[6], (DEPTH, D), 0.05),
        "g_mix_post": 1.0 + nrm(ks[7], (DEPTH, D), 0.05),
        "g_ffn_pre": 1.0 + nrm(ks[8], (DEPTH, D), 0.05),
        "g_ffn_post": 1.0 + nrm(ks[9], (DEPTH, D), 0.05),
        "even_w_in": nrm(ks[10], (N_EVEN, D, EVEN_PROJ), D ** -0.5),
        "even_rpb": nrm(ks[11], (N_EVEN, NA_HEADS, 2 * NA_KH - 1, 2 * NA_KW - 1), 0.2),
        "even_sink": nrm(ks[12], (N_EVEN, WA_HEADS), 0.5),
        "even_w_out": nrm(ks[13], (N_EVEN, MIX_WIDTH, D), MIX_WIDTH ** -0.5),
        "odd_w_in": nrm(ks[14], (N_ODD, D, ODD_PROJ), D ** -0.5),
        "odd_conv": nrm(ks[15], (N_ODD, CONV_W, SC_WIDTH), CONV_W ** -0.5),
        "odd_w_out": nrm(ks[16], (N_ODD, MIX_WIDTH, D), MIX_WIDTH ** -0.5),
        "ffn_w_up": nrm(ks[17], (DEPTH, D, 2 * FFN_HIDDEN), D ** -0.5),
        "ffn_conv": nrm(ks[18], (DEPTH, CONV_W, 2 * FFN_HIDDEN), CONV_W ** -0.5),
        "ffn_w_down": nrm(ks[19], (DEPTH, FFN_HIDDEN, D), FFN_HIDDEN ** -0.5),
    }


def reference(x, c, ctx, c_ctx, w_mod, b_mod, g_mix_pre, g_mix_post, g_ffn_pre, g_ffn_post,
              even_w_in, even_rpb, even_sink, even_w_out, odd_w_in, odd_conv, odd_w_out,
              ffn_w_up, ffn_conv, ffn_w_down):
    S = x.shape[1]
    cos, sin = axial_rope_tables(S)
    xc = ctx
    silu_c = jax.nn.silu(c)
    silu_cc = jax.nn.silu(c_ctx)
    for i in range(DEPTH):
        ctx_live = any(j % 2 == 0 for j in range(i + 1, DEPTH))
        need_hc = (i % 2 == 0) or ctx_live
        sh_m, sc_m, gt_m, sh_f, sc_f, gt_f = jnp.split((silu_c @ w_mod[i] + b_mod[i])[:, None, :], 6, axis=-1)
        hl = modulate(rmsnorm(x, g_mix_pre[i]), sh_m, sc_m)
        hc = None
        if need_hc:
            csh_m, csc_m, cgt_m, csh_f, csc_f, cgt_f = jnp.split(silu_cc @ w_mod[i] + b_mod[i], 6, axis=-1)
            hc = modulate(rmsnorm(xc, g_mix_pre[i]), csh_m, csc_m)
        if i % 2 == 0:
            e = i // 2
            yl, yc = even_mixer(hl, hc, even_w_in[e], even_rpb[e], even_sink[e], even_w_out[e],
                                cos, sin, ctx_live)
        else:
            o = i // 2
            yl = odd_mixer(hl, odd_w_in[o], odd_conv[o], odd_w_out[o])
            yc = odd_mixer(hc, odd_w_in[o], odd_conv[o], odd_w_out[o]) if ctx_live else None
        x = x + gt_m * rmsnorm(yl, g_mix_post[i])
        hl = modulate(rmsnorm(x, g_ffn_pre[i]), sh_f, sc_f)
        x = x + gt_f * rmsnorm(conv_ffn(hl, ffn_w_up[i], ffn_conv[i], ffn_w_down[i]), g_ffn_post[i])
        if ctx_live:
            xc = xc + cgt_m * rmsnorm(yc, g_mix_post[i])
            hc = modulate(rmsnorm(xc, g_ffn_pre[i]), csh_f, csc_f)
            xc = xc + cgt_f * rmsnorm(conv_ffn(hc, ffn_w_up[i], ffn_conv[i], ffn_w_down[i]), g_ffn_post[i])
    return x
```

```python
import contextlib
import numpy as np
import ml_dtypes
import concourse.bass as bass
import concourse.mybir as mybir
from concourse.bass_utils import run_bass_kernel_spmd

F32 = mybir.dt.float32
BF16 = mybir.dt.bfloat16
AF = mybir.ActivationFunctionType
ALU = mybir.AluOpType

D = 4096
S = 4096
L = 256
NTOK = S + L
DEPTH = 4
KC = 32
HID = 5632
NCORES = 4
NEG = -30000.0
SCALE = 128 ** -0.5
EPS = 1e-6

ENGS = ("pe", "act", "dve", "pool", "sp")


class Prog:
    ND = 40

    def __init__(self, nc, es):
        self.nc = nc
        self.esem = {e: es.enter_context(nc.semaphore("c_" + e)) for e in ENGS}
        self.ecnt = {e: 0 for e in ENGS}
        self.dsem = [es.enter_context(nc.semaphore("d%d" % i)) for i in range(self.ND)]
        self.dcnt = [0] * self.ND
        self.dnext = 0
        self.waited = {e: {} for e in ENGS}
        self.nops = 0
        self._reset()

    def _reset(self):
        self.streams = {e: [] for e in ENGS}
        self.lastw = {}
        self.readers = {}

    def op(self, eng, fn, reads=(), writes=(), dma=False):
        deps = []
        for k in reads:
            t = self.lastw.get(k)
            if t is not None:
                deps.append(t)
        for k in writes:
            t = self.lastw.get(k)
            if t is not None:
                deps.append(t)
            r = self.readers.get(k)
            if r:
                deps.extend(r.values())
        if dma:
            s = self.dnext
            self.dnext = (self.dnext + 1) % self.ND
            if self.dcnt[s] > 0:
                deps.append(("d%d" % s, self.dsem[s], self.dcnt[s]))
            self.dcnt[s] += 16
            tok = ("d%d" % s, self.dsem[s], self.dcnt[s])
            inc = (self.dsem[s], 16)
        else:
            self.ecnt[eng] += 1
            tok = (eng, self.esem[eng], self.ecnt[eng])
            inc = (self.esem[eng], 1)
        need = {}
        w = self.waited[eng]
        for (sid, sem, val) in deps:
            if sid == "pe" and eng == "pe" and not dma:
                continue
            if w.get(sid, 0) >= val:
                continue
            if sid not in need or need[sid][1] < val:
                need[sid] = (sem, val)
        for sid, (sem, val) in need.items():
            w[sid] = val
        self.streams[eng].append((list(need.values()), fn, inc))
        for k in writes:
            self.lastw[k] = tok
            self.readers[k] = {}
        for k in reads:
            if k in writes:
                continue
            r = self.readers.setdefault(k, {})
            old = r.get(tok[0])
            if old is None or old[2] < tok[2]:
                r[tok[0]] = tok
        self.nops += 1
        return tok

    def emit(self, final=False):
        nc = self.nc
        pre = self.pre if hasattr(self, "pre") else []
        fin = []
        for s in range(self.ND):
            if self.dcnt[s] > 0:
                fin.append(("d%d" % s, self.dsem[s], self.dcnt[s]))
        for e in ENGS:
            if self.ecnt[e] > 0:
                fin.append((e, self.esem[e], self.ecnt[e]))
        streams = self.streams

        def run(e, name):
            for sem, val in pre:
                e.wait_ge(sem, val)
            for waits, fn, inc in streams[name]:
                for sem, val in waits:
                    e.wait_ge(sem, val)
                ins = fn(e)
                ins.then_inc(inc[0], inc[1])
            if final and name == "sp":
                for sid, sem, val in fin:
                    e.wait_ge(sem, val)

        with nc.Block() as block:
            @block.tensor
            def _(e):
                run(e, "pe")

            @block.scalar
            def _(e):
                run(e, "act")

            @block.vector
            def _(e):
                run(e, "dve")

            @block.gpsimd
            def _(e):
                run(e, "pool")

            @block.sync
            def _(e):
                run(e, "sp")
        self.pre = [(sem, val) for (sid, sem, val) in fin]
        for e in ENGS:
            for sid, sem, val in fin:
                self.waited[e][sid] = val
        self._reset()


class Rot:
    def __init__(self, n):
        self.n = n
        self.i = 0

    def next(self):
        v = self.i
        self.i = (self.i + 1) % self.n
        return v


def gemm(P, ps, wb, nslot, slotrot, W, kcn, groups, acts, epi, bankrot, MG=256, tag="w"):
    Wv = W.rearrange("(c p) m -> p c m", p=128)
    nq = 4
    qs = [(q * kcn // nq, (q + 1) * kcn // nq) for q in range(nq)]

    def qof(c):
        for qi, (a, b) in enumerate(qs):
            if a <= c < b:
                return qi

    for gi, (col0, mode) in enumerate(groups):
        slot = slotrot.next()
        for qi, (a, b) in enumerate(qs):
            P.op("pool", lambda e, slot=slot, a=a, b=b, col0=col0: e.dma_start(
                out=wb[:, slot, a:b, :], in_=Wv[:, a:b, col0:col0 + MG]),
                writes=[(tag, slot, qi)], dma=True)
        if mode == "f":
            for m in range(MG // 128):
                for j, A in enumerate(acts):
                    b = bankrot.next()
                    T = A["T"]
                    for c in range(kcn):
                        P.op("pe", lambda e, b=b, slot=slot, c=c, m=m, A=A, T=T: e.matmul(
                            ps[b][:, :T], lhsT=wb[:, slot, c, m * 128:(m + 1) * 128], rhs=A["ap"][:, c, :T],
                            start=(c == 0), stop=(c == kcn - 1)),
                            reads=[(tag, slot, qof(c)), A["key"] + (c,)], writes=[("ps", b)])
                    epi(col0 // 128 + m, j, A, b)
        else:
            for j, A in enumerate(acts):
                T = A["T"]
                for s in range(T // 128):
                    b = bankrot.next()
                    for c in range(kcn):
                        P.op("pe", lambda e, b=b, slot=slot, c=c, s=s, A=A: e.matmul(
                            ps[b][:, :MG], lhsT=A["ap"][:, c, s * 128:(s + 1) * 128], rhs=wb[:, slot, c, :],
                            start=(c == 0), stop=(c == kcn - 1)),
                            reads=[(tag, slot, qof(c)), A["key"] + (c,)], writes=[("ps", b)])
                    epi(col0, j, A, b, s)


def build_program():
    nc = bass.Bass("TRN2", target_bir_lowering=False)

    _uid = [0]

    def SBT(name, shape, dt):
        _uid[0] += 1
        return nc.sbuf_tensor("%s_%d" % (name, _uid[0]), shape, dt)

    def din(name, shape, dt=F32):
        return nc.dram_tensor(name, list(shape), dt, kind="ExternalInput").ap()

    def dscr(name, shape, dt):
        if _DEBUG[0]:
            return nc.dram_tensor(name, list(shape), dt, kind="ExternalOutput").ap()
        return nc.dram_tensor(name, list(shape), dt).ap()

    xin = din("xin", [D, NTOK])
    cvec = din("cvec", [128, KC, 2])
    w_mod = din("w_mod", [DEPTH, D, 6 * D])
    bmod = din("bmod", [128, DEPTH, 192])
    gvec = din("gvec", [128, 4, DEPTH, KC])
    even_w_in = din("even_w_in", [2, D, 9216])
    even_w_out = din("even_w_out", [2, D, D])
    odd_w_in = din("odd_w_in", [2, D, 8192])
    odd_w_out = din("odd_w_out", [2, D, D])
    ffn_w_up = din("ffn_w_up", [DEPTH, D, 2 * HID])
    ffn_w_down = din("ffn_w_down", [DEPTH, HID, D])
    nbias = din("nbias", [2, 16, 128, 14, 64])
    sinkb = din("sinkb", [128, 2, 16])
    oconv = din("oconv", [128, 2, 16, 3])
    fconv = din("fconv", [128, DEPTH, 88, 3])
    ropec = din("ropec", [128, S])
    ropes = din("ropes", [128, S])
    rotm = din("rotm", [128, 128])
    wmask = din("wmask", [128, 256])
    cdsd = din("cdsd", [128, 256], BF16)
    dftc = din("dftc", [S, S], BF16)
    dfts = din("dfts", [S, S], BF16)
    dft256 = din("dft256", [2, L, L], BF16)
    yout = nc.dram_tensor("yout", [D, S], F32, kind="ExternalOutput").ap()

    xB = dscr("xB", [D, NTOK], F32)
    xC = dscr("xC", [D, NTOK], F32)
    PT = dscr("PT", [9216, NTOK], BF16)
    VT = dscr("VT", [NTOK, 2560], BF16)
    OT = dscr("OT", [D, NTOK], BF16)
    YT = dscr("YT", [D, NTOK], F32)
    HPW = NTOK + 4
    HP = dscr("HP", [2 * HID, HPW], F32)

    def hpcol(tok0):
        return tok0 + 1 if tok0 < S else tok0 + 3

    MAIN_TILES = [(512 * j, 512, 0) for j in range(8)]
    CTX_TILE = (S, L, 1)

    def tile_groups(with_ctx, per=2):
        g = [MAIN_TILES[i:i + per] for i in range(0, 8, per)]
        if with_ctx:
            g.append([CTX_TILE])
        return g

    with contextlib.ExitStack() as ges:
        P = Prog(nc, ges)
        ps = [ges.enter_context(nc.psum_tensor("ps%d" % i, [128, 512], F32)) for i in range(8)]
        ones32 = ges.enter_context(SBT("ones32", [128, 128], F32))
        onesbf = ges.enter_context(SBT("onesbf", [128, 128], BF16))
        tab = ges.enter_context(SBT("tab", [128, DEPTH, 6, 2, KC], F32))
        fcw = ges.enter_context(SBT("fcw", [128, DEPTH, 88, 3], F32))
        ocw = ges.enter_context(SBT("ocw", [128, 2, 16, 3], F32))
        esink = ges.enter_context(SBT("esink", [128, 2, 16], F32))
        epsb = ges.enter_context(SBT("epsb", [128, 1], F32))

        with contextlib.ExitStack() as es:
            cv = es.enter_context(SBT("cv", [128, KC, 2], F32))
            cact = es.enter_context(SBT("cact", [128, KC, 2], BF16))
            bm = es.enter_context(SBT("bm", [128, DEPTH, 192], F32))
            gv = es.enter_context(SBT("gv", [128, 4, DEPTH, KC], F32))
            modt = es.enter_context(SBT("modt", [128, DEPTH, 192, 2], F32))
            zt = es.enter_context(SBT("zt", [128, 88, 1], F32))
            wbm = es.enter_context(SBT("wbm", [128, 3, KC, 256], BF16))
            P.op("dve", lambda e: e.memset(ones32[:], 1.0), writes=["ones32"])
            P.op("dve", lambda e: e.memset(onesbf[:], 1.0), writes=["onesbf"])
            P.op("dve", lambda e: e.memset(epsb[:], EPS), writes=["epsb"])
            P.op("dve", lambda e: e.memset(zt[:], 0.0), writes=["zt"])
            HPv = HP.rearrange("(c p) t -> p c t", p=128)
            for pc in (0, S + 1, S + 2, NTOK + 3):
                P.op("sp", lambda e, pc=pc: e.dma_start(out=HPv[:, :, pc:pc + 1], in_=zt[:], allow_slow_non_contiguous=True),
                     reads=["zt"], dma=True)
            P.op("sp", lambda e: e.dma_start(out=cv[:], in_=cvec), writes=["cv"], dma=True)
            P.op("sp", lambda e: e.dma_start(out=bm[:], in_=bmod), writes=["bm"], dma=True)
            P.op("sp", lambda e: e.dma_start(out=gv[:], in_=gvec), writes=["gv"], dma=True)
            P.op("sp", lambda e: e.dma_start(out=fcw[:], in_=fconv), writes=["fcw"], dma=True)
            P.op("sp", lambda e: e.dma_start(out=ocw[:], in_=oconv), writes=["ocw"], dma=True)
            P.op("sp", lambda e: e.dma_start(out=esink[:], in_=sinkb), writes=["esink"], dma=True)
            P.op("act", lambda e: e.activation(out=esink[:], in_=esink[:], func=AF.Exp), writes=["esink"])
            P.op("act", lambda e: e.activation(out=cact[:], in_=cv[:], func=AF.Silu), reads=["cv"],
                 writes=[("cact", c) for c in range(KC)])
            bankrot = Rot(6)
            slotrot = Rot(3)
            for i in range(DEPTH):
                def epi(m, j, A, b, i=i):
                    P.op("dve", lambda e, m=m, b=b, i=i: e.tensor_scalar(
                        out=modt[:, i, m, :], in0=ps[b][:, 0:2], scalar1=bm[:, i, m:m + 1], scalar2=None, op0=ALU.add),
                        reads=[("ps", b), "bm"], writes=[("modt", i, m)])
                groups = [(c0, "f") for c0 in range(0, 6 * D, 256)]
                gemm(P, ps, wbm, 3, slotrot, w_mod[i], KC, groups, [dict(ap=cact, T=2, key=("cact",))], epi, bankrot, tag="wm")
                allm = [("modt", i, m) for m in range(192)]
                for w in range(2):
                    def mk(kind, mlo, gidx, mode, i=i, w=w):
                        if mode == "a":
                            P.op("dve", lambda e: e.scalar_tensor_tensor(
                                out=tab[:, i, kind, w, :], in0=modt[:, i, mlo:mlo + 32, w], scalar=1.0, in1=gv[:, gidx, i, :],
                                op0=ALU.add, op1=ALU.mult), reads=allm + ["gv"], writes=[("tab", i, kind, w)])
                        elif mode == "c":
                            P.op("dve", lambda e: e.tensor_tensor(
                                out=tab[:, i, kind, w, :], in0=modt[:, i, mlo:mlo + 32, w], in1=gv[:, gidx, i, :], op=ALU.mult),
                                reads=allm + ["gv"], writes=[("tab", i, kind, w)])
                        else:
                            P.op("dve", lambda e: e.tensor_copy(out=tab[:, i, kind, w, :], in_=modt[:, i, mlo:mlo + 32, w]),
                                 reads=allm, writes=[("tab", i, kind, w)])
                    mk(0, 32, 0, "a")
                    mk(1, 0, 0, "b")
                    mk(2, 64, 1, "c")
                    mk(3, 128, 2, "a")
                    mk(4, 96, 0, "b")
                    mk(5, 160, 3, "c")
            P.emit()

        _ph = [0]

        def _gate(f):
            def g(*a, **k):
                _ph[0] += 1
                if _STOP[0] is not None and _ph[0] > _STOP[0]:
                    return
                return f(*a, **k)
            return g

        def alloc_norm_bufs(es):
            xs = es.enter_context(SBT("xs", [128, 4, 512], F32))
            sqs = es.enter_context(SBT("sqs", [128, 4, 512], F32))
            rstd = es.enter_context(SBT("rstd", [128, 2, 512], F32))
            tmpn = es.enter_context(SBT("tmpn", [128, 4, 512], F32))
            return xs, sqs, rstd, tmpn, Rot(4)

        def rstd_from_ps(bank, rstd, j, T):
            P.op("act", lambda e: e.activation(out=rstd[:, j, :T], in_=ps[bank][:, :T], func=AF.Sqrt,
                                               bias=epsb[:, 0:1], scale=1.0 / D),
                 reads=[("ps", bank)], writes=[("rstd", j)])
            P.op("dve", lambda e: e.reciprocal(out=rstd[:, j, :T], in_=rstd[:, j, :T]), writes=[("rstd", j)])

        def norm_prologue(bufs, Xsrc, tiles, layer, kA, kB, act):
            xs, sqs, rstd, tmpn, xrot = bufs
            acts = []
            for j, (tok0, T, w) in enumerate(tiles):
                for c in range(KC):
                    sl = xrot.next()
                    P.op("sp", lambda e, sl=sl, c=c, tok0=tok0, T=T: e.dma_start(
                        out=xs[:, sl, :T], in_=Xsrc[c * 128:(c + 1) * 128, tok0:tok0 + T]), writes=[("xs", sl)], dma=True)
                    P.op("act", lambda e, sl=sl, T=T: e.activation(out=sqs[:, sl, :T], in_=xs[:, sl, :T], func=AF.Square),
                         reads=[("xs", sl)], writes=[("sqs", sl)])
                    P.op("pe", lambda e, sl=sl, T=T, c=c: e.matmul(ps[6][:, :T], lhsT=ones32[:], rhs=sqs[:, sl, :T],
                                                                  start=(c == 0), stop=(c == KC - 1)),
                         reads=[("sqs", sl), "ones32"], writes=[("ps", 6)])
                rstd_from_ps(6, rstd, j, T)
                for c in range(KC):
                    sl = xrot.next()
                    P.op("sp", lambda e, sl=sl, c=c, tok0=tok0, T=T: e.dma_start(
                        out=xs[:, sl, :T], in_=Xsrc[c * 128:(c + 1) * 128, tok0:tok0 + T]), writes=[("xs", sl)], dma=True)
                    P.op("dve", lambda e, sl=sl, j=j, T=T: e.tensor_tensor(out=tmpn[:, sl, :T], in0=xs[:, sl, :T], in1=rstd[:, j, :T], op=ALU.mult),
                         reads=[("xs", sl), ("rstd", j)], writes=[("tmpn", sl)])
                    P.op("act", lambda e, sl=sl, j=j, c=c, T=T, w=w: e.activation(
                        out=act[:, j, c, :T], in_=tmpn[:, sl, :T], func=AF.Identity,
                        bias=tab[:, layer, kB, w, c:c + 1], scale=tab[:, layer, kA, w, c:c + 1]),
                        reads=[("tmpn", sl)], writes=[("act", j, c)])
                acts.append(dict(ap=act[:, j], T=T, key=("act", j), tok0=tok0, w=w))
            return acts

        class Resid:
            def __init__(self, es, nslot=2):
                self.n = nslot
                self.ysb = es.enter_context(SBT("r_ysb", [128, nslot, 512], F32))
                self.sq = es.enter_context(SBT("r_sq", [128, nslot, 512], F32))
                self.rstd = es.enter_context(SBT("r_rstd", [128, 2, 512], F32))
                self.ys = es.enter_context(SBT("r_ys", [128, nslot, 512], F32))
                self.xs = es.enter_context(SBT("r_xs", [128, nslot, 512], F32))
                self.xo = es.enter_context(SBT("r_xo", [128, nslot, 512], F32))
                self.tm = es.enter_context(SBT("r_tm", [128, nslot, 512], F32))
                self.yrot = Rot(nslot)
                self.srot = Rot(nslot)
                self.prot = Rot(nslot)
                self.pending = []

            def flush(self, keep=0):
                while len(self.pending) > keep:
                    self.pending.pop(0)()

            def epi(self, m, j, A, b):
                T = A["T"]
                tok0 = A["tok0"]
                self.flush(0)
                o = self.yrot.next()
                s = self.srot.next()
                ysb, sq = self.ysb, self.sq
                P.op("act", lambda e: e.activation(out=ysb[:, o, :T], in_=ps[b][:, :T], func=AF.Copy),
                     reads=[("ps", b)], writes=[("r_ysb", o)])
                P.op("sp", lambda e: e.dma_start(out=YT[m * 128:(m + 1) * 128, tok0:tok0 + T], in_=ysb[:, o, :T]),
                     reads=[("r_ysb", o)], writes=[("YT", m, j)], dma=True)
                P.op("act", lambda e: e.activation(out=sq[:, s, :T], in_=ps[b][:, :T], func=AF.Square),
                     reads=[("ps", b)], writes=[("r_sq", s)])

                def stat():
                    P.op("pe", lambda e: e.matmul(ps[6 + j][:, :T], lhsT=ones32[:], rhs=sq[:, s, :T],
                                                  start=(m == 0), stop=(m == KC - 1)),
                         reads=[("r_sq", s), "ones32"], writes=[("ps", 6 + j)])
                self.pending.append(stat)

            def post(self, acts, Xprev, Xnext, layer, kC):
                self.flush(0)
                for j, A in enumerate(acts):
                    self._post_tile(j, A, Xprev, Xnext, layer, kC)

            def _post_tile(self, j, A, Xprev, Xnext, layer, kC):
                if True:
                    T = A["T"]
                    tok0 = A["tok0"]
                    w = A["w"]
                    rstd_from_ps(6 + j, self.rstd, j, T)
                    for c in range(KC):
                        p = self.prot.next()
                        ys, xs, xo, tm, rstd = self.ys, self.xs, self.xo, self.tm, self.rstd
                        P.op("sp", lambda e, p=p, c=c: e.dma_start(out=ys[:, p, :T], in_=YT[c * 128:(c + 1) * 128, tok0:tok0 + T]),
                             reads=[("YT", c, j)], writes=[("r_ys", p)], dma=True)
                        P.op("sp", lambda e, p=p, c=c: e.dma_start(out=xs[:, p, :T], in_=Xprev[c * 128:(c + 1) * 128, tok0:tok0 + T]),
                             writes=[("r_xs", p)], dma=True)
                        P.op("dve", lambda e, p=p: e.tensor_tensor(out=tm[:, p, :T], in0=ys[:, p, :T], in1=rstd[:, j, :T], op=ALU.mult),
                             reads=[("r_ys", p), ("rstd", j)], writes=[("r_tm", p)])
                        P.op("dve", lambda e, p=p, c=c: e.scalar_tensor_tensor(
                            out=xo[:, p, :T], in0=tm[:, p, :T], scalar=tab[:, layer, kC, w, c:c + 1], in1=xs[:, p, :T],
                            op0=ALU.mult, op1=ALU.add), reads=[("r_tm", p), ("r_xs", p)], writes=[("r_xo", p)])
                        P.op("sp", lambda e, p=p, c=c: e.dma_start(out=Xnext[c * 128:(c + 1) * 128, tok0:tok0 + T], in_=xo[:, p, :T]),
                             reads=[("r_xo", p)], dma=True)

        @_gate
        def phase_s1(layer, Xsrc, with_ctx):
            even = layer % 2 == 0
            W = (even_w_in if even else odd_w_in)[layer // 2]
            with contextlib.ExitStack() as es:
                bufs = alloc_norm_bufs(es)
                act = es.enter_context(SBT("act", [128, 2, KC, 512], BF16))
                wb = es.enter_context(SBT("wb", [128, 3, KC, 256], BF16))
                obf = es.enter_context(SBT("obf", [128, 4, 512], BF16))
                q32 = es.enter_context(SBT("q32", [128, 2, 512], F32))
                rt1 = es.enter_context(SBT("rt1", [128, 2, 512], F32))
                rt2 = es.enter_context(SBT("rt2", [128, 2, 512], F32))
                rc = es.enter_context(SBT("rc", [128, 2, 512], F32))
                rs_ = es.enter_context(SBT("rs", [128, 2, 512], F32))
                rm = es.enter_context(SBT("rm", [128, 128], F32))
                P.op("sp", lambda e: e.dma_start(out=rm[:], in_=rotm), writes=["rm"], dma=True)
                bankrot = Rot(6)
                slotrot = Rot(3)
                orot = Rot(4)
                qrot = Rot(2)
                if even:
                    groups = []
                    for c0 in range(0, 9216, 256):
                        m = c0 // 128
                        mode = "t" if (32 <= m < 48 or m >= 68) else "f"
                        groups.append((c0, mode))
                else:
                    groups = [(c0, "f") for c0 in range(0, 8192, 256)]
                for tiles in tile_groups(with_ctx):
                    acts = norm_prologue(bufs, Xsrc, tiles, layer, 0, 1, act)
                    if even:
                        for j, A in enumerate(acts):
                            if A["w"] == 0:
                                P.op("sp", lambda e, j=j, A=A: e.dma_start(out=rc[:, j, :], in_=ropec[:, A["tok0"]:A["tok0"] + 512]),
                                     writes=[("rc", j)], dma=True)
                                P.op("sp", lambda e, j=j, A=A: e.dma_start(out=rs_[:, j, :], in_=ropes[:, A["tok0"]:A["tok0"] + 512]),
                                     writes=[("rs", j)], dma=True)

                    def epi(m, j, A, b, s=None):
                        T = A["T"]
                        tok0 = A["tok0"]
                        if s is not None:
                            vcol = (m - 4096) if m < 8192 else (m - 8704 + 2048)
                            o = orot.next()
                            P.op("act", lambda e: e.activation(out=obf[:, o, :256], in_=ps[b][:, :256], func=AF.Copy),
                                 reads=[("ps", b)], writes=[("obf", o)])
                            P.op("sp", lambda e: e.dma_start(
                                out=VT[tok0 + s * 128: tok0 + (s + 1) * 128, vcol:vcol + 256], in_=obf[:, o, :256]),
                                reads=[("obf", o)], dma=True)
                            return
                        rope = even and A["w"] == 0 and (48 <= m < 68)
                        o = orot.next()
                        if not rope:
                            P.op("act", lambda e: e.activation(out=obf[:, o, :T], in_=ps[b][:, :T], func=AF.Copy),
                                 reads=[("ps", b)], writes=[("obf", o)])
                        else:
                            q = qrot.next()
                            P.op("act", lambda e: e.activation(out=q32[:, q, :], in_=ps[b][:, :], func=AF.Copy),
                                 reads=[("ps", b)], writes=[("q32", q)])
                            P.op("pe", lambda e: e.matmul(ps[7][:, :], lhsT=rm[:], rhs=q32[:, q, :], start=True, stop=True),
                                 reads=[("q32", q), "rm"], writes=[("ps", 7)])
                            P.op("dve", lambda e: e.tensor_tensor(out=rt1[:, q, :], in0=q32[:, q, :], in1=rc[:, j, :], op=ALU.mult),
                                 reads=[("q32", q), ("rc", j)], writes=[("rt1", q)])
                            P.op("dve", lambda e: e.tensor_tensor(out=rt2[:, q, :], in0=ps[7][:, :], in1=rs_[:, j, :], op=ALU.mult),
                                 reads=[("ps", 7), ("rs", j)], writes=[("rt2", q)])
                            P.op("dve", lambda e: e.tensor_tensor(out=obf[:, o, :], in0=rt1[:, q, :], in1=rt2[:, q, :], op=ALU.add),
                                 reads=[("rt1", q), ("rt2", q)], writes=[("obf", o)])
                        P.op("sp", lambda e: e.dma_start(
                            out=PT[m * 128:(m + 1) * 128, tok0:tok0 + T], in_=obf[:, o, :T]), reads=[("obf", o)], dma=True)

                    gemm(P, ps, wb, 3, slotrot, W, KC, groups, acts, epi, bankrot, tag="w")
                P.emit()

        @_gate
        def phase_s2(layer, Xprev, Xnext, with_ctx):
            even = layer % 2 == 0
            W = (even_w_out if even else odd_w_out)[layer // 2]
            OTv = OT.rearrange("(c p) t -> p c t", p=128)
            with contextlib.ExitStack() as es:
                act = es.enter_context(SBT("act", [128, 2, KC, 512], BF16))
                wb = es.enter_context(SBT("wb", [128, 3, KC, 256], BF16))
                R = Resid(es, 3)
                bankrot = Rot(6)
                slotrot = Rot(3)
                groups = [(c0, "f") for c0 in range(0, D, 256)]
                for tiles in tile_groups(with_ctx):
                    acts = []
                    for j, (tok0, T, w) in enumerate(tiles):
                        for q in range(4):
                            P.op("sp", lambda e, j=j, q=q, tok0=tok0, T=T: e.dma_start(
                                out=act[:, j, q * 8:(q + 1) * 8, :T], in_=OTv[:, q * 8:(q + 1) * 8, tok0:tok0 + T]),
                                writes=[("act", j, c) for c in range(q * 8, (q + 1) * 8)], dma=True)
                        acts.append(dict(ap=act[:, j], T=T, key=("act", j), tok0=tok0, w=w))
                    gemm(P, ps, wb, 3, slotrot, W, KC, groups, acts, R.epi, bankrot, tag="w")
                    R.post(acts, Xprev, Xnext, layer, 2)
                P.emit()

        @_gate
        def phase_s3(layer, Xsrc, with_ctx):
            W = ffn_w_up[layer]
            with contextlib.ExitStack() as es:
                bufs = alloc_norm_bufs(es)
                act = es.enter_context(SBT("act", [128, 2, KC, 512], BF16))
                wb = es.enter_context(SBT("wb", [128, 3, KC, 256], BF16))
                of = es.enter_context(SBT("of", [128, 4, 512], F32))
                bankrot = Rot(6)
                slotrot = Rot(3)
                orot = Rot(4)
                groups = [(c0, "f") for c0 in range(0, 2 * HID, 256)]
                for tiles in tile_groups(with_ctx):
                    acts = norm_prologue(bufs, Xsrc, tiles, layer, 3, 4, act)

                    def epi(m, j, A, b):
                        T = A["T"]
                        col = hpcol(A["tok0"])
                        o = orot.next()
                        P.op("act", lambda e: e.activation(out=of[:, o, :T], in_=ps[b][:, :T], func=AF.Copy),
                             reads=[("ps", b)], writes=[("of", o)])
                        P.op("sp", lambda e: e.dma_start(out=HP[m * 128:(m + 1) * 128, col:col + T], in_=of[:, o, :T]),
                             reads=[("of", o)], dma=True)
                    gemm(P, ps, wb, 3, slotrot, W, KC, groups, acts, epi, bankrot, tag="w")
                P.emit()

        @_gate
        def phase_s4(layer, Xprev, Xnext, with_ctx):
            W = ffn_w_down[layer]
            KD = HID // 128
            with contextlib.ExitStack() as es:
                act = es.enter_context(SBT("act", [128, 1, KD, 512], BF16))
                wb = es.enter_context(SBT("wb", [128, 2, KD, 256], BF16))
                gs = es.enter_context(SBT("gs", [128, 2, 514], F32))
                vs = es.enter_context(SBT("vs", [128, 2, 514], F32))
                ca = es.enter_context(SBT("ca", [128, 2, 512], F32))
                cb = es.enter_context(SBT("cb", [128, 2, 512], F32))
                cc_ = es.enter_context(SBT("cc", [128, 2, 512], F32))
                cd_ = es.enter_context(SBT("cd", [128, 2, 512], F32))
                sg = es.enter_context(SBT("sg", [128, 2, 512], F32))
                R = Resid(es, 2)
                bankrot = Rot(6)
                slotrot = Rot(2)
                crot = Rot(2)
                groups = [(c0, "f") for c0 in range(0, D, 256)]
                for tiles in tile_groups(with_ctx, per=1):
                    acts = []
                    def conv_tile(j, tok0, T, w):
                        col = hpcol(tok0)
                        for c in range(KD):
                            x = crot.next()
                            P.op("sp", lambda e, x=x, c=c: e.dma_start(out=gs[:, x, :T + 2], in_=HP[c * 128:(c + 1) * 128, col - 1:col + T + 1]),
                                 writes=[("gs", x)], dma=True)
                            P.op("sp", lambda e, x=x, c=c: e.dma_start(out=vs[:, x, :T + 2], in_=HP[(KD + c) * 128:(KD + c + 1) * 128, col - 1:col + T + 1]),
                                 writes=[("vs", x)], dma=True)
                            P.op("pool", lambda e, x=x, c=c: e.tensor_scalar(out=ca[:, x, :T], in0=gs[:, x, 1:T + 1], scalar1=fcw[:, layer, c, 1:2], scalar2=None, op0=ALU.mult),
                                 reads=[("gs", x)], writes=[("ca", x)])
                            P.op("dve", lambda e, x=x, c=c: e.scalar_tensor_tensor(out=cb[:, x, :T], in0=gs[:, x, 0:T], scalar=fcw[:, layer, c, 0:1], in1=ca[:, x, :T], op0=ALU.mult, op1=ALU.add),
                                 reads=[("gs", x), ("ca", x)], writes=[("cb", x)])
                            P.op("dve", lambda e, x=x, c=c: e.scalar_tensor_tensor(out=ca[:, x, :T], in0=gs[:, x, 2:T + 2], scalar=fcw[:, layer, c, 2:3], in1=cb[:, x, :T], op0=ALU.mult, op1=ALU.add),
                                 reads=[("gs", x), ("cb", x)], writes=[("ca", x)])
                            P.op("act", lambda e, x=x: e.activation(out=sg[:, x, :T], in_=ca[:, x, :T], func=AF.Silu),
                                 reads=[("ca", x)], writes=[("sg", x)])
                            P.op("pool", lambda e, x=x, c=c: e.tensor_scalar(out=cc_[:, x, :T], in0=vs[:, x, 1:T + 1], scalar1=fcw[:, layer, KD + c, 1:2], scalar2=None, op0=ALU.mult),
                                 reads=[("vs", x)], writes=[("cc", x)])
                            P.op("dve", lambda e, x=x, c=c: e.scalar_tensor_tensor(out=cd_[:, x, :T], in0=vs[:, x, 0:T], scalar=fcw[:, layer, KD + c, 0:1], in1=cc_[:, x, :T], op0=ALU.mult, op1=ALU.add),
                                 reads=[("vs", x), ("cc", x)], writes=[("cd", x)])
                            P.op("dve", lambda e, x=x, c=c: e.scalar_tensor_tensor(out=cc_[:, x, :T], in0=vs[:, x, 2:T + 2], scalar=fcw[:, layer, KD + c, 2:3], in1=cd_[:, x, :T], op0=ALU.mult, op1=ALU.add),
                                 reads=[("vs", x), ("cd", x)], writes=[("cc", x)])
                            P.op("pool", lambda e, x=x, c=c, j=j: e.tensor_tensor(out=act[:, j, c, :T], in0=sg[:, x, :T], in1=cc_[:, x, :T], op=ALU.mult),
                                 reads=[("sg", x), ("cc", x)], writes=[("act", j, c)])
                        acts.append(dict(ap=act[:, j], T=T, key=("act", j), tok0=tok0, w=w))
                    for j, (tok0, T, w) in enumerate(tiles):
                        conv_tile(j, tok0, T, w)
                    gemm(P, ps, wb, 2, slotrot, W, KD, groups, acts, R.epi, bankrot, tag="w")
                    R.post(acts, Xprev, Xnext, layer, 5)
                P.emit()

        @_gate
        def phase_even(layer, ctx_out):
            ei = layer // 2
            ncols = NTOK if ctx_out else S
            with contextlib.ExitStack() as es:
                qT = es.enter_context(SBT("qT", [128, 2, NTOK], BF16))
                kT = es.enter_context(SBT("kT", [128, 2, NTOK], BF16))
                vE = es.enter_context(SBT("vE", [128, 2, 34, 128], BF16))
                vO = es.enter_context(SBT("vO", [128, 2, 31, 128], BF16))
                tbl = es.enter_context(SBT("tbl", [128, 2, 14, 64], F32))
                oT = es.enter_context(SBT("oT", [128, 2, NTOK], BF16))
                sb = es.enter_context(SBT("sb", [128, 2, 4, 64], F32))
                pT = es.enter_context(SBT("pT", [128, 2, 640], BF16))
                pc = es.enter_context(SBT("pc", [128, 512], BF16))
                rec = es.enter_context(SBT("rec", [128, 2, 256], F32))
                dsb = es.enter_context(SBT("dsb", [128, 2, 256], F32))
                wm = es.enter_context(SBT("wm", [128, 256], F32))
                P.op("sp", lambda e: e.dma_start(out=wm[:], in_=wmask), writes=["wm"], dma=True)
                srot = Rot(4)
                orot = Rot(2)
                drot = Rot(2)
                xrot = Rot(2)
                VTa = VT[0:NTOK, :].rearrange("(n p) d -> p n d", p=128)
                VTo = VT[64:64 + 31 * 128, :].rearrange("(n p) d -> p n d", p=128)

                def load_v(dst_key, dst, slot, vcol, odd):
                    src = VTo if odd else VTa
                    n = 31 if odd else 34
                    half = n // 2
                    for (a, b_) in ((0, half), (half, n)):
                        P.op("sp", lambda e, a=a, b_=b_: e.dma_start(out=dst[:, slot, a:b_, :], in_=src[:, a:b_, vcol:vcol + 128]),
                             writes=[(dst_key, slot, a)], dma=True)
                    return [(dst_key, slot, 0), (dst_key, slot, half)]

                def ctx_attn(hb, kb, vb, vkeys, sink_ap):
                    sbk = srot.next()
                    for cc in range(2):
                        P.op("pe", lambda e, cc=cc: e.matmul(ps[sbk][:, cc * 256:(cc + 1) * 256],
                                                          lhsT=kT[:, kb, S + cc * 128:S + (cc + 1) * 128], rhs=qT[:, hb, S:NTOK],
                                                          start=True, stop=True),
                             reads=[("qT", hb), ("kT", kb)], writes=[("ps", sbk)])
                    P.op("act", lambda e: e.activation(out=pc[:, :], in_=ps[sbk][:, :], func=AF.Exp, scale=SCALE),
                         reads=[("ps", sbk)], writes=["pc"])
                    ob = 4 + orot.next()
                    db = 6 + drot.next()
                    for cc in range(2):
                        P.op("pe", lambda e, cc=cc: e.matmul(ps[ob][:, 0:256], lhsT=vE[:, vb, 32 + cc, :], rhs=pc[:, cc * 256:(cc + 1) * 256],
                                                          start=(cc == 0), stop=(cc == 1)),
                             reads=["pc"] + vkeys, writes=[("ps", ob)])
                    for cc in range(2):
                        P.op("pe", lambda e, cc=cc: e.matmul(ps[db][:, 0:256], lhsT=onesbf[:], rhs=pc[:, cc * 256:(cc + 1) * 256],
                                                          start=(cc == 0), stop=(cc == 1)),
                             reads=["pc", "onesbf"], writes=[("ps", db)])
                    x = xrot.next()
                    if sink_ap is not None:
                        P.op("dve", lambda e: e.tensor_scalar(out=dsb[:, x, :], in0=ps[db][:, 0:256], scalar1=sink_ap, scalar2=None, op0=ALU.add),
                             reads=[("ps", db), "esink"], writes=[("dsb", x)])
                        P.op("dve", lambda e: e.reciprocal(out=rec[:, x, :], in_=dsb[:, x, :]), reads=[("dsb", x)], writes=[("rec", x)])
                    else:
                        P.op("dve", lambda e: e.reciprocal(out=rec[:, x, :], in_=ps[db][:, 0:256]), reads=[("ps", db)], writes=[("rec", x)])
                    P.op("dve", lambda e: e.tensor_tensor(out=oT[:, hb, S:NTOK], in0=ps[ob][:, 0:256], in1=rec[:, x, :], op=ALU.mult),
                         reads=[("ps", ob), ("rec", x)], writes=[("oT", hb, "ctx")])

                for h in range(16):
                    hb = h % 2
                    P.op("sp", lambda e, h=h, hb=hb: e.dma_start(out=qT[:, hb, :], in_=PT[h * 128:(h + 1) * 128, :]), writes=[("qT", hb)], dma=True)
                    P.op("sp", lambda e, h=h, hb=hb: e.dma_start(out=kT[:, hb, :], in_=PT[2048 + h * 128:2048 + (h + 1) * 128, :]), writes=[("kT", hb)], dma=True)
                    vek = load_v("vE", vE, hb, h * 128, False)
                    vok = load_v("vO", vO, hb, h * 128, True)
                    P.op("sp", lambda e, h=h, hb=hb: e.dma_start(out=tbl[:, hb], in_=nbias[ei, h]), writes=[("tbl", hb)], dma=True)
                    okeys = []
                    for r in range(64):
                        rs = min(max(r - 4, 0), 56)
                        delta = rs - r + 7
                        sbk = srot.next()
                        for jc in range(4):
                            k0 = (rs + 2 * jc) * 64
                            P.op("pe", lambda e, jc=jc, k0=k0, r=r, sbk=sbk, hb=hb: e.matmul(
                                ps[sbk][:, jc * 64:(jc + 1) * 64], lhsT=kT[:, hb, k0:k0 + 128], rhs=qT[:, hb, r * 64:(r + 1) * 64],
                                start=True, stop=True), reads=[("qT", hb), ("kT", hb)], writes=[("ps", sbk)])
                        for cc in range(2):
                            P.op("pe", lambda e, cc=cc, r=r, sbk=sbk, hb=hb: e.matmul(
                                ps[sbk][:, 256 + cc * 64:256 + (cc + 1) * 64], lhsT=kT[:, hb, S + cc * 128:S + (cc + 1) * 128],
                                rhs=qT[:, hb, r * 64:(r + 1) * 64], start=True, stop=True),
                                reads=[("qT", hb), ("kT", hb)], writes=[("ps", sbk)])
                        x = xrot.next()
                        P.op("dve", lambda e, x=x, sbk=sbk, hb=hb, delta=delta: e.scalar_tensor_tensor(
                            out=sb[:, x], in0=ps[sbk][:, 0:256].rearrange("p (a b) -> p a b", b=64), scalar=SCALE,
                            in1=tbl[:, hb, delta:delta + 7:2, :], op0=ALU.mult, op1=ALU.add),
                            reads=[("ps", sbk), ("tbl", hb)], writes=[("sb", x)])
                        P.op("act", lambda e, x=x: e.activation(out=pT[:, x, 0:256], in_=sb[:, x].rearrange("p a b -> p (a b)"), func=AF.Exp),
                             reads=[("sb", x)], writes=[("pT", x, 0)])
                        P.op("act", lambda e, x=x, sbk=sbk: e.activation(out=pT[:, x, 256:384], in_=ps[sbk][:, 256:384], func=AF.Exp, scale=SCALE),
                             reads=[("ps", sbk)], writes=[("pT", x, 1)])
                        ob = 4 + orot.next()
                        db = 6 + drot.next()
                        for c in range(6):
                            if c < 4:
                                if rs % 2 == 0:
                                    lh = vE[:, hb, rs // 2 + c, :]
                                    vk = vek
                                else:
                                    lh = vO[:, hb, (rs - 1) // 2 + c, :]
                                    vk = vok
                            else:
                                lh = vE[:, hb, 32 + (c - 4), :]
                                vk = vek
                            P.op("pe", lambda e, c=c, lh=lh, x=x, ob=ob: e.matmul(ps[ob][:, 0:64], lhsT=lh, rhs=pT[:, x, c * 64:(c + 1) * 64],
                                                                               start=(c == 0), stop=(c == 5)),
                                 reads=[("pT", x, 0), ("pT", x, 1)] + vk, writes=[("ps", ob)])
                        for c in range(6):
                            P.op("pe", lambda e, c=c, x=x, db=db: e.matmul(ps[db][:, 0:64], lhsT=onesbf[:], rhs=pT[:, x, c * 64:(c + 1) * 64],
                                                                         start=(c == 0), stop=(c == 5)),
                                 reads=[("pT", x, 0), ("pT", x, 1), "onesbf"], writes=[("ps", db)])
                        P.op("dve", lambda e, x=x, db=db: e.reciprocal(out=rec[:, x, 0:64], in_=ps[db][:, 0:64]), reads=[("ps", db)], writes=[("rec", x)])
                        P.op("dve", lambda e, x=x, ob=ob, hb=hb, r=r: e.tensor_tensor(out=oT[:, hb, r * 64:(r + 1) * 64], in0=ps[ob][:, 0:64], in1=rec[:, x, 0:64], op=ALU.mult),
                             reads=[("ps", ob), ("rec", x)], writes=[("oT", hb, r)])
                        okeys.append(("oT", hb, r))
                    if ctx_out:
                        ctx_attn(hb, hb, hb, vek, None)
                        okeys.append(("oT", hb, "ctx"))
                    P.op("sp", lambda e, h=h, hb=hb: e.dma_start(out=OT[h * 128:(h + 1) * 128, 0:ncols], in_=oT[:, hb, 0:ncols]),
                         reads=okeys, dma=True)

                for g in range(4):
                    gb = g % 2
                    P.op("sp", lambda e, g=g, gb=gb: e.dma_start(out=kT[:, gb, :], in_=PT[8192 + g * 128:8192 + (g + 1) * 128, :]), writes=[("kT", gb)], dma=True)
                    vek = load_v("vE", vE, gb, 2048 + g * 128, False)
                    for hh in range(4):
                        h = 4 * g + hh
                        hb = h % 2
                        P.op("sp", lambda e, h=h, hb=hb: e.dma_start(out=qT[:, hb, :], in_=PT[6144 + h * 128:6144 + (h + 1) * 128, :]), writes=[("qT", hb)], dma=True)
                        okeys = []
                        for n in range(32):
                            pb = srot.next() % 2
                            b0, b1 = 2 * pb, 2 * pb + 1
                            chunks = []
                            if n > 0:
                                chunks.append((b0, 0, n - 1))
                            if n < 31:
                                chunks.append((b0, 128, n + 1))
                            chunks += [(b0, 256, n), (b0, 384, 32), (b1, 0, 33)]
                            for (bk, col, kbk) in chunks:
                                P.op("pe", lambda e, bk=bk, col=col, kbk=kbk, n=n, hb=hb, gb=gb: e.matmul(
                                    ps[bk][:, col:col + 128], lhsT=kT[:, gb, kbk * 128:(kbk + 1) * 128], rhs=qT[:, hb, n * 128:(n + 1) * 128],
                                    start=True, stop=True), reads=[("qT", hb), ("kT", gb)], writes=[("ps", bk)])
                            x = xrot.next()
                            lo = 0 if n > 0 else 128
                            hi = 256 if n < 31 else 128
                            sbf = sb[:, x].rearrange("p a b -> p (a b)")
                            P.op("dve", lambda e, x=x, b0=b0, lo=lo, hi=hi, sbf=sbf: e.scalar_tensor_tensor(
                                out=sbf[:, lo:hi], in0=ps[b0][:, lo:hi], scalar=SCALE, in1=wm[:, lo:hi], op0=ALU.mult, op1=ALU.add),
                                reads=[("ps", b0), "wm"], writes=[("sb", x)])
                            P.op("act", lambda e, x=x, lo=lo, hi=hi, sbf=sbf: e.activation(out=pT[:, x, lo:hi], in_=sbf[:, lo:hi], func=AF.Exp),
                                 reads=[("sb", x)], writes=[("pT", x, 0)])
                            P.op("act", lambda e, x=x, b0=b0: e.activation(out=pT[:, x, 256:512], in_=ps[b0][:, 256:512], func=AF.Exp, scale=SCALE),
                                 reads=[("ps", b0)], writes=[("pT", x, 1)])
                            P.op("act", lambda e, x=x, b1=b1: e.activation(out=pT[:, x, 512:640], in_=ps[b1][:, 0:128], func=AF.Exp, scale=SCALE),
                                 reads=[("ps", b1)], writes=[("pT", x, 2)])
                            ob = 4 + orot.next()
                            db = 6 + drot.next()
                            pcols = [(col if bk == b0 else 512, kbk) for (bk, col, kbk) in chunks]
                            nch = len(pcols)
                            for ci, (pcol, kbk) in enumerate(pcols):
                                P.op("pe", lambda e, ci=ci, pcol=pcol, kbk=kbk, x=x, ob=ob, gb=gb, nch=nch: e.matmul(
                                    ps[ob][:, 0:128], lhsT=vE[:, gb, kbk, :], rhs=pT[:, x, pcol:pcol + 128], start=(ci == 0), stop=(ci == nch - 1)),
                                    reads=[("pT", x, 0), ("pT", x, 1), ("pT", x, 2)] + vek, writes=[("ps", ob)])
                            for ci, (pcol, kbk) in enumerate(pcols):
                                P.op("pe", lambda e, ci=ci, pcol=pcol, x=x, db=db, nch=nch: e.matmul(
                                    ps[db][:, 0:128], lhsT=onesbf[:], rhs=pT[:, x, pcol:pcol + 128], start=(ci == 0), stop=(ci == nch - 1)),
                                    reads=[("pT", x, 0), ("pT", x, 1), ("pT", x, 2), "onesbf"], writes=[("ps", db)])
                            P.op("dve", lambda e, x=x, db=db, h=h: e.tensor_scalar(out=dsb[:, x, 0:128], in0=ps[db][:, 0:128], scalar1=esink[:, ei, h:h + 1], scalar2=None, op0=ALU.add),
                                 reads=[("ps", db), "esink"], writes=[("dsb", x)])
                            P.op("dve", lambda e, x=x: e.reciprocal(out=rec[:, x, 0:128], in_=dsb[:, x, 0:128]), reads=[("dsb", x)], writes=[("rec", x)])
                            P.op("dve", lambda e, x=x, ob=ob, hb=hb, n=n: e.tensor_tensor(out=oT[:, hb, n * 128:(n + 1) * 128], in0=ps[ob][:, 0:128], in1=rec[:, x, 0:128], op=ALU.mult),
                                 reads=[("ps", ob), ("rec", x)], writes=[("oT", hb, n)])
                            okeys.append(("oT", hb, n))
                        if ctx_out:
                            ctx_attn(hb, gb, gb, vek, esink[:, ei, h:h + 1])
                            okeys.append(("oT", hb, "ctx"))
                        P.op("sp", lambda e, h=h, hb=hb: e.dma_start(out=OT[2048 + h * 128:2048 + (h + 1) * 128, 0:ncols], in_=oT[:, hb, 0:ncols]),
                             reads=okeys, dma=True)
                P.emit()

        @_gate
        def phase_fnet(layer, ctx_out):
            ntile = 34 if ctx_out else 32
            ncols = NTOK if ctx_out else S
            sc_main = float((S * 128) ** -0.5)
            sc_ctx = float((L * 128) ** -0.5)
            dcv = dftc.rearrange("(c p) k -> p c k", p=128)
            dsv = dfts.rearrange("(c p) k -> p c k", p=128)
            with contextlib.ExitStack() as es:
                uT = es.enter_context(SBT("uT", [128, 2, NTOK], BF16))
                cd = es.enter_context(SBT("cdm", [128, 256], BF16))
                ucs = es.enter_context(SBT("ucs", [128, 4, 34, 256], BF16))
                dft = es.enter_context(SBT("dft", [128, 2, 2, 32, 256], BF16))
                d256 = es.enter_context(SBT("d256", [128, 2, 2, 256], BF16))
                fo = es.enter_context(SBT("fo", [128, 4, 256], BF16))
                P.op("sp", lambda e: e.dma_start(out=cd[:], in_=cdsd), writes=["cd"], dma=True)
                P.op("sp", lambda e: e.dma_start(out=d256[:], in_=dft256.rearrange("w (c p) k -> p w c k", p=128)), writes=["d256"], dma=True)
                bankrot = Rot(8)
                forot = Rot(4)
                dslot = Rot(2)
                for gb in range(4):
                    for gi in range(4):
                        g = 4 * gb + gi
                        ub = g % 2
                        P.op("sp", lambda e, g=g, ub=ub: e.dma_start(out=uT[:, ub, 0:ncols], in_=PT[g * 128:(g + 1) * 128, 0:ncols]), writes=[("uT", ub)], dma=True)
                        for t in range(ntile):
                            b = bankrot.next()
                            P.op("pe", lambda e, b=b, t=t, ub=ub: e.matmul(ps[b][:, 0:256], lhsT=uT[:, ub, t * 128:(t + 1) * 128], rhs=cd[:, :], start=True, stop=True),
                                 reads=[("uT", ub), "cd"], writes=[("ps", b)])
                            eng = "act" if t % 2 == 0 else "dve"
                            if eng == "act":
                                P.op("act", lambda e, b=b, t=t, gi=gi: e.activation(out=ucs[:, gi, t, :], in_=ps[b][:, 0:256], func=AF.Copy),
                                     reads=[("ps", b)], writes=[("ucs", gi, t)])
                            else:
                                P.op("dve", lambda e, b=b, t=t, gi=gi: e.tensor_copy(out=ucs[:, gi, t, :], in_=ps[b][:, 0:256]),
                                     reads=[("ps", b)], writes=[("ucs", gi, t)])
                    for kt in range(16):
                        sl = dslot.next()
                        for q in range(4):
                            P.op("sp", lambda e, sl=sl, q=q, kt=kt: e.dma_start(out=dft[:, sl, 0, q * 8:(q + 1) * 8, :], in_=dcv[:, q * 8:(q + 1) * 8, kt * 256:(kt + 1) * 256]),
                                 writes=[("dft", sl, 0, q)], dma=True)
                            P.op("sp", lambda e, sl=sl, q=q, kt=kt: e.dma_start(out=dft[:, sl, 1, q * 8:(q + 1) * 8, :], in_=dsv[:, q * 8:(q + 1) * 8, kt * 256:(kt + 1) * 256]),
                                 writes=[("dft", sl, 1, q)], dma=True)
                        for gi in range(4):
                            g = 4 * gb + gi
                            b = bankrot.next()
                            for tc in range(32):
                                P.op("pe", lambda e, b=b, tc=tc, gi=gi, sl=sl: e.matmul(ps[b][:, 0:256], lhsT=ucs[:, gi, tc, 0:128], rhs=dft[:, sl, 0, tc, :], start=(tc == 0), stop=False),
                                     reads=[("ucs", gi, tc), ("dft", sl, 0, tc // 8)], writes=[("ps", b)])
                                P.op("pe", lambda e, b=b, tc=tc, gi=gi, sl=sl: e.matmul(ps[b][:, 0:256], lhsT=ucs[:, gi, tc, 128:256], rhs=dft[:, sl, 1, tc, :], start=False, stop=(tc == 31)),
                                     reads=[("ucs", gi, tc), ("dft", sl, 1, tc // 8)], writes=[("ps", b)])
                            o = forot.next()
                            P.op("act", lambda e, b=b, o=o: e.activation(out=fo[:, o, :], in_=ps[b][:, 0:256], func=AF.Copy, scale=sc_main),
                                 reads=[("ps", b)], writes=[("fo", o)])
                            P.op("sp", lambda e, o=o, g=g, kt=kt: e.dma_start(out=OT[g * 128:(g + 1) * 128, kt * 256:(kt + 1) * 256], in_=fo[:, o, :]),
                                 reads=[("fo", o)], dma=True)
                    if ctx_out:
                        for gi in range(4):
                            g = 4 * gb + gi
                            b = bankrot.next()
                            for tc in range(2):
                                P.op("pe", lambda e, b=b, tc=tc, gi=gi: e.matmul(ps[b][:, 0:256], lhsT=ucs[:, gi, 32 + tc, 0:128], rhs=d256[:, 0, tc, :], start=(tc == 0), stop=False),
                                     reads=[("ucs", gi, 32 + tc), "d256"], writes=[("ps", b)])
                                P.op("pe", lambda e, b=b, tc=tc, gi=gi: e.matmul(ps[b][:, 0:256], lhsT=ucs[:, gi, 32 + tc, 128:256], rhs=d256[:, 1, tc, :], start=False, stop=(tc == 1)),
                                     reads=[("ucs", gi, 32 + tc), "d256"], writes=[("ps", b)])
                            o = forot.next()
                            P.op("act", lambda e, b=b, o=o: e.activation(out=fo[:, o, :], in_=ps[b][:, 0:256], func=AF.Copy, scale=sc_ctx),
                                 reads=[("ps", b)], writes=[("fo", o)])
                            P.op("sp", lambda e, o=o, g=g: e.dma_start(out=OT[g * 128:(g + 1) * 128, S:NTOK], in_=fo[:, o, :]),
                                 reads=[("fo", o)], dma=True)
                P.emit()

        @_gate
        def phase_sconv(layer, ctx_out):
            oi = layer // 2
            ncols = NTOK if ctx_out else S
            MW = NTOK + 6
            segs = [(1, 0, S)]
            if ctx_out:
                segs.append((S + 3, S, L))
            with contextlib.ExitStack() as es:
                bgT = es.enter_context(SBT("bgT", [128, 2, NTOK], BF16))
                cgT = es.enter_context(SBT("cgT", [128, 2, NTOK], BF16))
                hvT = es.enter_context(SBT("hvT", [128, 2, NTOK], BF16))
                mm = es.enter_context(SBT("mm", [128, 2, MW], F32))
                a0 = es.enter_context(SBT("a0", [128, NTOK], F32))
                a1 = es.enter_context(SBT("a1", [128, NTOK], F32))
                so = es.enter_context(SBT("so", [128, 2, NTOK], BF16))
                for s_ in range(2):
                    P.op("pool", lambda e, s_=s_: e.memset(mm[:, s_, :], 0.0), writes=[("mm", s_)])
                for c in range(16):
                    sl = c % 2
                    for (buf, key, row0) in ((bgT, "bg", 2048), (cgT, "cg", 4096), (hvT, "hv", 6144)):
                        P.op("sp", lambda e, buf=buf, row0=row0, c=c, sl=sl: e.dma_start(out=buf[:, sl, 0:ncols], in_=PT[row0 + c * 128:row0 + (c + 1) * 128, 0:ncols]),
                             writes=[(key, sl)], dma=True)
                    for (mo, to, n) in segs:
                        P.op("pool", lambda e, sl=sl, mo=mo, to=to, n=n: e.tensor_tensor(out=mm[:, sl, mo:mo + n], in0=cgT[:, sl, to:to + n], in1=hvT[:, sl, to:to + n], op=ALU.mult),
                             reads=[("cg", sl), ("hv", sl)], writes=[("mm", sl)])
                    for (mo, to, n) in segs:
                        P.op("act", lambda e, sl=sl, mo=mo, to=to, n=n, c=c: e.activation(out=a0[:, to:to + n], in_=mm[:, sl, mo:mo + n], func=AF.Identity, scale=ocw[:, oi, c, 1:2]),
                             reads=[("mm", sl)], writes=[("a0", to)])
                        P.op("dve", lambda e, sl=sl, mo=mo, to=to, n=n, c=c: e.scalar_tensor_tensor(out=a1[:, to:to + n], in0=mm[:, sl, mo - 1:mo - 1 + n], scalar=ocw[:, oi, c, 0:1], in1=a0[:, to:to + n], op0=ALU.mult, op1=ALU.add),
                             reads=[("mm", sl), ("a0", to)], writes=[("a1", to)])
                        P.op("dve", lambda e, sl=sl, mo=mo, to=to, n=n, c=c: e.scalar_tensor_tensor(out=a0[:, to:to + n], in0=mm[:, sl, mo + 1:mo + 1 + n], scalar=ocw[:, oi, c, 2:3], in1=a1[:, to:to + n], op0=ALU.mult, op1=ALU.add),
                             reads=[("mm", sl), ("a1", to)], writes=[("a0", to)])
                        P.op("pool", lambda e, sl=sl, to=to, n=n: e.tensor_tensor(out=so[:, sl, to:to + n], in0=a0[:, to:to + n], in1=bgT[:, sl, to:to + n], op=ALU.mult),
                             reads=[("a0", to), ("bg", sl)], writes=[("so", sl, to)])
                    P.op("sp", lambda e, sl=sl, c=c: e.dma_start(out=OT[2048 + c * 128:2048 + (c + 1) * 128, 0:ncols], in_=so[:, sl, 0:ncols]),
                         reads=[("so", sl, to) for (mo, to, n) in segs], dma=True)
                P.emit()

        Xa = xin
        for layer in range(N_LAYERS_RUN):
            ctx_live = any(j % 2 == 0 for j in range(layer + 1, DEPTH))
            need_hc = (layer % 2 == 0) or ctx_live
            phase_s1(layer, Xa, need_hc)
            if layer % 2 == 0:
                phase_even(layer, ctx_live)
            else:
                phase_fnet(layer, ctx_live)
                phase_sconv(layer, ctx_live)
            phase_s2(layer, Xa, xB, ctx_live)
            phase_s3(layer, xB, ctx_live)
            Xn = yout if layer == N_LAYERS_RUN - 1 else xC
            phase_s4(layer, xB, Xn, ctx_live and (layer != N_LAYERS_RUN - 1))
            Xa = xC
        P.emit(final=True)
        nops = P.nops
    return nc, nops


N_LAYERS_RUN = DEPTH
_NC = [NCORES]
_RUNKW = {}
_LAST = [None]
_DEBUG = [False]
_STOP = [None]


def _host_consts():
    t = np.arange(S)
    row = (t // 64).astype(np.float32)
    col = (t % 64).astype(np.float32)
    nf = 32
    inv = (10000.0 ** (-np.arange(nf, dtype=np.float32) / nf)).astype(np.float32)
    ropec = np.zeros((128, S), np.float32)
    ropes = np.zeros((128, S), np.float32)
    rotm = np.zeros((128, 128), np.float32)
    for a, pos in enumerate((row, col)):
        ang = pos[None, :] * inv[:, None]
        for j in range(2):
            d0 = a * 64 + j * 32
            ropec[d0:d0 + 32] = np.cos(ang)
            ropes[d0:d0 + 32] = -np.sin(ang) if j == 0 else np.sin(ang)
        for f in range(nf):
            rotm[a * 64 + f, a * 64 + 32 + f] = 1.0
            rotm[a * 64 + 32 + f, a * 64 + f] = 1.0
    kl = np.arange(128)[:, None]
    ql = np.arange(128)[None, :]
    wmask = np.zeros((128, 256), np.float32)
    wmask[:, 0:128] = np.where(kl >= ql, 0.0, NEG)
    wmask[:, 128:256] = np.where(kl <= ql, 0.0, NEG)
    d = np.arange(128)
    angd = 2 * np.pi * np.outer(d, d) / 128.0
    cdsd = np.concatenate([np.cos(angd), np.sin(angd)], axis=1).astype(ml_dtypes.bfloat16)
    tt = np.arange(S, dtype=np.int64)
    prod = (np.outer(tt, tt) % S).astype(np.float64) * (2 * np.pi / S)
    dftc = np.cos(prod).astype(ml_dtypes.bfloat16)
    dfts = (-np.sin(prod)).astype(ml_dtypes.bfloat16)
    t2 = np.arange(L, dtype=np.int64)
    p2 = (np.outer(t2, t2) % L).astype(np.float64) * (2 * np.pi / L)
    dft256 = np.stack([np.cos(p2), -np.sin(p2)]).astype(ml_dtypes.bfloat16)
    return dict(ropec=ropec, ropes=ropes, rotm=rotm, wmask=wmask, cdsd=cdsd, dftc=dftc, dfts=dfts, dft256=dft256)


def _nbias_table(rpb):
    E = rpb.shape[0]
    kc = np.arange(64)[:, None]
    c = np.arange(64)[None, :]
    cstart = np.clip(c - 8, 0, 48)
    ok = (kc >= cstart) & (kc < cstart + 16)
    rel = np.clip(kc - c, -15, 15) + 15
    out = np.full((E, 16, 128, 14, 64), NEG, np.float32)
    for jpar in range(2):
        for m in range(14):
            rr = m + jpar
            if rr > 14:
                continue
            g = rpb[:, :, rr, :][:, :, rel]
            out[:, :, jpar * 64:(jpar + 1) * 64, m, :] = np.where(ok[None, None], g, np.float32(NEG))
    return out


def kernel(x, c, ctx, c_ctx, w_mod, b_mod, g_mix_pre, g_mix_post, g_ffn_pre, g_ffn_post,
           even_w_in, even_rpb, even_sink, even_w_out, odd_w_in, odd_conv, odd_w_out,
           ffn_w_up, ffn_conv, ffn_w_down):
    f32 = np.float32
    x = np.asarray(x, f32); c = np.asarray(c, f32); ctx = np.asarray(ctx, f32); c_ctx = np.asarray(c_ctx, f32)
    nc, _ = build_program()
    consts = _host_consts()

    def pl(v, n):
        v = np.asarray(v, f32)
        lead = v.shape[:-1]
        return np.ascontiguousarray(np.moveaxis(v.reshape(lead + (n, 128)), -1, 0))

    shared = dict(
        w_mod=np.asarray(w_mod, f32),
        bmod=pl(b_mod, 192),
        gvec=np.ascontiguousarray(np.stack([pl(g_mix_pre, KC), pl(g_mix_post, KC), pl(g_ffn_pre, KC), pl(g_ffn_post, KC)], axis=1)),
        even_w_in=np.asarray(even_w_in, f32), even_w_out=np.asarray(even_w_out, f32),
        odd_w_in=np.asarray(odd_w_in, f32), odd_w_out=np.asarray(odd_w_out, f32),
        ffn_w_up=np.asarray(ffn_w_up, f32), ffn_w_down=np.asarray(ffn_w_down, f32),
        nbias=_nbias_table(np.asarray(even_rpb, f32)),
        sinkb=np.ascontiguousarray(np.broadcast_to(np.asarray(even_sink, f32)[None], (128, 2, 16))),
        oconv=np.ascontiguousarray(np.transpose(np.asarray(odd_conv, f32).reshape(2, 3, 16, 128), (3, 0, 2, 1))),
        fconv=np.ascontiguousarray(np.transpose(np.asarray(ffn_conv, f32).reshape(DEPTH, 3, 88, 128), (3, 0, 2, 1))),
        **consts,
    )
    in_maps = []
    for b in range(_NC[0]):
        xin = np.ascontiguousarray(np.concatenate([x[b], ctx[b]], axis=0).T)
        cv = np.ascontiguousarray(np.stack([c[b].reshape(KC, 128).T, c_ctx.reshape(KC, 128).T], axis=-1))
        m = dict(shared)
        m["xin"] = xin
        m["cvec"] = cv
        in_maps.append(m)
    res = run_bass_kernel_spmd(nc, in_maps, core_ids=list(range(_NC[0])), **_RUNKW)
    _LAST[0] = res
    out = np.stack([np.ascontiguousarray(res.results[b]["yout"].T) for b in range(_NC[0])], axis=0)
    return out.astype(f32)
```

```python
import contextlib
import numpy as np
import ml_dtypes
import concourse.bass as bass
import concourse.mybir as mybir
from concourse.bass_utils import run_bass_kernel_spmd

F32 = mybir.dt.float32
BF16 = mybir.dt.bfloat16
AF = mybir.ActivationFunctionType
ALU = mybir.AluOpType

D = 4096
S = 4096
L = 256
NTOK = S + L
DEPTH = 4
KC = 32
HID = 5632
NCORES = 4
NEG = -30000.0
SCALE = 128 ** -0.5
EPS = 1e-6

ENGS = ("pe", "act", "dve", "pool", "sp")


class Prog:
    ND = 40

    def __init__(self, nc, es):
        self.nc = nc
        self.esem = {e: es.enter_context(nc.semaphore("c_" + e)) for e in ENGS}
        self.ecnt = {e: 0 for e in ENGS}
        self.dsem = [es.enter_context(nc.semaphore("d%d" % i)) for i in range(self.ND)]
        self.dcnt = [0] * self.ND
        self.dnext = 0
        self.waited = {e: {} for e in ENGS}
        self.nops = 0
        self._reset()

    def _reset(self):
        self.streams = {e: [] for e in ENGS}
        self.lastw = {}
        self.readers = {}

    def op(self, eng, fn, reads=(), writes=(), dma=False):
        deps = []
        for k in reads:
            t = self.lastw.get(k)
            if t is not None:
                deps.append(t)
        for k in writes:
            t = self.lastw.get(k)
            if t is not None:
                deps.append(t)
            r = self.readers.get(k)
            if r:
                deps.extend(r.values())
        if dma:
            s = self.dnext
            self.dnext = (self.dnext + 1) % self.ND
            if self.dcnt[s] > 0:
                deps.append(("d%d" % s, self.dsem[s], self.dcnt[s]))
            self.dcnt[s] += 16
            tok = ("d%d" % s, self.dsem[s], self.dcnt[s])
            inc = (self.dsem[s], 16)
        else:
            self.ecnt[eng] += 1
            tok = (eng, self.esem[eng], self.ecnt[eng])
            inc = (self.esem[eng], 1)
        need = {}
        w = self.waited[eng]
        for (sid, sem, val) in deps:
            if sid == "pe" and eng == "pe" and not dma:
                continue
            if w.get(sid, 0) >= val:
                continue
            if sid not in need or need[sid][1] < val:
                need[sid] = (sem, val)
        for sid, (sem, val) in need.items():
            w[sid] = val
        self.streams[eng].append((list(need.values()), fn, inc))
        for k in writes:
            self.lastw[k] = tok
            self.readers[k] = {}
        for k in reads:
            if k in writes:
                continue
            r = self.readers.setdefault(k, {})
            old = r.get(tok[0])
            if old is None or old[2] < tok[2]:
                r[tok[0]] = tok
        self.nops += 1
        return tok

    def emit(self, final=False):
        nc = self.nc
        pre = self.pre if hasattr(self, "pre") else []
        fin = []
        for s in range(self.ND):
            if self.dcnt[s] > 0:
                fin.append(("d%d" % s, self.dsem[s], self.dcnt[s]))
        for e in ENGS:
            if self.ecnt[e] > 0:
                fin.append((e, self.esem[e], self.ecnt[e]))
        streams = self.streams

        def run(e, name):
            for sem, val in pre:
                e.wait_ge(sem, val)
            for waits, fn, inc in streams[name]:
                for sem, val in waits:
                    e.wait_ge(sem, val)
                ins = fn(e)
                ins.then_inc(inc[0], inc[1])
            if final and name == "sp":
                for sid, sem, val in fin:
                    e.wait_ge(sem, val)

        with nc.Block() as block:
            @block.tensor
            def _(e):
                run(e, "pe")

            @block.scalar
            def _(e):
                run(e, "act")

            @block.vector
            def _(e):
                run(e, "dve")

            @block.gpsimd
            def _(e):
                run(e, "pool")

            @block.sync
            def _(e):
                run(e, "sp")
        self.pre = [(sem, val) for (sid, sem, val) in fin]
        for e in ENGS:
            for sid, sem, val in fin:
                self.waited[e][sid] = val
        self._reset()


class Rot:
    def __init__(self, n):
        self.n = n
        self.i = 0

    def next(self):
        v = self.i
        self.i = (self.i + 1) % self.n
        return v


def gemm(P, ps, wb, nslot, slotrot, W, kcn, groups, acts, epi, bankrot, MG=256, tag="w", wq=("pool",)):
    Wv = W.rearrange("(c p) m -> p c m", p=128)
    nq = 4
    qs = [(q * kcn // nq, (q + 1) * kcn // nq) for q in range(nq)]

    def qof(c):
        for qi, (a, b) in enumerate(qs):
            if a <= c < b:
                return qi

    for gi, (col0, mode) in enumerate(groups):
        slot = slotrot.next()
        for qi, (a, b) in enumerate(qs):
            P.op(wq[qi % len(wq)], lambda e, slot=slot, a=a, b=b, col0=col0: e.dma_start(
                out=wb[:, slot, a:b, :], in_=Wv[:, a:b, col0:col0 + MG]),
                writes=[(tag, slot, qi)], dma=True)
        if mode == "f":
            for m in range(MG // 128):
                for j, A in enumerate(acts):
                    b = bankrot.next()
                    T = A["T"]
                    for c in range(kcn):
                        P.op("pe", lambda e, b=b, slot=slot, c=c, m=m, A=A, T=T: e.matmul(
                            ps[b][:, :T], lhsT=wb[:, slot, c, m * 128:(m + 1) * 128], rhs=A["ap"][:, c, :T],
                            start=(c == 0), stop=(c == kcn - 1)),
                            reads=[(tag, slot, qof(c)), A["key"] + (c,)], writes=[("ps", b)])
                    epi(col0 // 128 + m, j, A, b)
        else:
            for j, A in enumerate(acts):
                T = A["T"]
                for s in range(T // 128):
                    b = bankrot.next()
                    for c in range(kcn):
                        P.op("pe", lambda e, b=b, slot=slot, c=c, s=s, A=A: e.matmul(
                            ps[b][:, :MG], lhsT=A["ap"][:, c, s * 128:(s + 1) * 128], rhs=wb[:, slot, c, :],
                            start=(c == 0), stop=(c == kcn - 1)),
                            reads=[(tag, slot, qof(c)), A["key"] + (c,)], writes=[("ps", b)])
                    epi(col0, j, A, b, s)


def build_program():
    nc = bass.Bass("TRN2", target_bir_lowering=False)

    _uid = [0]

    def SBT(name, shape, dt):
        _uid[0] += 1
        return nc.sbuf_tensor("%s_%d" % (name, _uid[0]), shape, dt)

    def din(name, shape, dt=F32):
        return nc.dram_tensor(name, list(shape), dt, kind="ExternalInput").ap()

    def dscr(name, shape, dt):
        if _DEBUG[0]:
            return nc.dram_tensor(name, list(shape), dt, kind="ExternalOutput").ap()
        return nc.dram_tensor(name, list(shape), dt).ap()

    xin = din("xin", [D, NTOK])
    cvec = din("cvec", [128, KC, 2])
    w_mod = din("w_mod", [DEPTH, D, 6 * D])
    bmod = din("bmod", [128, DEPTH, 192])
    gvec = din("gvec", [128, 4, DEPTH, KC])
    even_w_in = din("even_w_in", [2, D, 9216])
    even_w_out = din("even_w_out", [2, D, D])
    odd_w_in = din("odd_w_in", [2, D, 8192])
    odd_w_out = din("odd_w_out", [2, D, D])
    ffn_w_up = din("ffn_w_up", [DEPTH, D, 2 * HID])
    ffn_w_down = din("ffn_w_down", [DEPTH, HID, D])
    nbias = din("nbias", [2, 16, 128, 14, 64])
    sinkb = din("sinkb", [128, 2, 16])
    oconv = din("oconv", [128, 2, 16, 3])
    fconv = din("fconv", [128, DEPTH, 88, 3])
    ropec = din("ropec", [128, S])
    ropes = din("ropes", [128, S])
    rotm = din("rotm", [128, 128])
    wmask = din("wmask", [128, 256])
    cdsd = din("cdsd", [128, 256], BF16)
    dftc = din("dftc", [S, S], BF16)
    dfts = din("dfts", [S, S], BF16)
    dft256 = din("dft256", [2, L, L], BF16)
    yout = nc.dram_tensor("yout", [D, S], F32, kind="ExternalOutput").ap()

    xB = dscr("xB", [D, NTOK], F32)
    xC = dscr("xC", [D, NTOK], F32)
    PT = dscr("PT", [9216, NTOK], BF16)
    VT = dscr("VT", [NTOK, 2560], BF16)
    OT = dscr("OT", [D, NTOK], BF16)
    YT = dscr("YT", [D, NTOK], F32)
    HPW = NTOK + 4
    HP = dscr("HP", [2 * HID, HPW], F32)

    def hpcol(tok0):
        return tok0 + 1 if tok0 < S else tok0 + 3

    MAIN_TILES = [(512 * j, 512, 0) for j in range(8)]
    CTX_TILE = (S, L, 1)

    def tile_groups(with_ctx, per=2):
        g = [list(MAIN_TILES[i:i + per]) for i in range(0, 8, per)]
        if with_ctx:
            if per == 2:
                g[-1].append(CTX_TILE)
            else:
                g.append([CTX_TILE])
        return g

    with contextlib.ExitStack() as ges:
        P = Prog(nc, ges)
        ps = [ges.enter_context(nc.psum_tensor("ps%d" % i, [128, 512], F32)) for i in range(8)]
        ones32 = ges.enter_context(SBT("ones32", [128, 128], F32))
        onesbf = ges.enter_context(SBT("onesbf", [128, 128], BF16))
        tab = ges.enter_context(SBT("tab", [128, DEPTH, 6, 2, KC], F32))
        fcw = ges.enter_context(SBT("fcw", [128, DEPTH, 88, 3], F32))
        ocw = ges.enter_context(SBT("ocw", [128, 2, 16, 3], F32))
        esink = ges.enter_context(SBT("esink", [128, 2, 16], F32))
        epsb = ges.enter_context(SBT("epsb", [128, 1], F32))

        with contextlib.ExitStack() as es:
            cv = es.enter_context(SBT("cv", [128, KC, 2], F32))
            cact = es.enter_context(SBT("cact", [128, KC, 2], F32))
            bm = es.enter_context(SBT("bm", [128, DEPTH, 192], F32))
            gv = es.enter_context(SBT("gv", [128, 4, DEPTH, KC], F32))
            modt = es.enter_context(SBT("modt", [128, DEPTH, 192, 2], F32))
            zt = es.enter_context(SBT("zt", [128, 88, 1], F32))
            wbm = es.enter_context(SBT("wbm", [128, 3, KC, 256], F32))
            P.op("dve", lambda e: e.memset(ones32[:], 1.0), writes=["ones32"])
            P.op("dve", lambda e: e.memset(onesbf[:], 1.0), writes=["onesbf"])
            P.op("dve", lambda e: e.memset(epsb[:], EPS), writes=["epsb"])
            P.op("dve", lambda e: e.memset(zt[:], 0.0), writes=["zt"])
            HPv = HP.rearrange("(c p) t -> p c t", p=128)
            for pc in (0, S + 1, S + 2, NTOK + 3):
                P.op("sp", lambda e, pc=pc: e.dma_start(out=HPv[:, :, pc:pc + 1], in_=zt[:], allow_slow_non_contiguous=True),
                     reads=["zt"], dma=True)
            P.op("sp", lambda e: e.dma_start(out=cv[:], in_=cvec), writes=["cv"], dma=True)
            P.op("sp", lambda e: e.dma_start(out=bm[:], in_=bmod), writes=["bm"], dma=True)
            P.op("sp", lambda e: e.dma_start(out=gv[:], in_=gvec), writes=["gv"], dma=True)
            P.op("sp", lambda e: e.dma_start(out=fcw[:], in_=fconv), writes=["fcw"], dma=True)
            P.op("sp", lambda e: e.dma_start(out=ocw[:], in_=oconv), writes=["ocw"], dma=True)
            P.op("sp", lambda e: e.dma_start(out=esink[:], in_=sinkb), writes=["esink"], dma=True)
            P.op("act", lambda e: e.activation(out=esink[:], in_=esink[:], func=AF.Exp), writes=["esink"])
            P.op("act", lambda e: e.activation(out=cact[:], in_=cv[:], func=AF.Silu), reads=["cv"],
                 writes=[("cact", c) for c in range(KC)])
            bankrot = Rot(6)
            slotrot = Rot(3)
            for i in range(DEPTH):
                def epi(m, j, A, b, i=i):
                    P.op("dve", lambda e, m=m, b=b, i=i: e.tensor_scalar(
                        out=modt[:, i, m, :], in0=ps[b][:, 0:2], scalar1=bm[:, i, m:m + 1], scalar2=None, op0=ALU.add),
                        reads=[("ps", b), "bm"], writes=[("modt", i, m)])
                groups = [(c0, "f") for c0 in range(0, 6 * D, 256)]
                gemm(P, ps, wbm, 3, slotrot, w_mod[i], KC, groups, [dict(ap=cact, T=2, key=("cact",))], epi, bankrot, tag="wm", wq=("sp", "act"))
                allm = [("modt", i, m) for m in range(192)]
                for w in range(2):
                    def mk(kind, mlo, gidx, mode, i=i, w=w):
                        if mode == "a":
                            P.op("dve", lambda e: e.scalar_tensor_tensor(
                                out=tab[:, i, kind, w, :], in0=modt[:, i, mlo:mlo + 32, w], scalar=1.0, in1=gv[:, gidx, i, :],
                                op0=ALU.add, op1=ALU.mult), reads=allm + ["gv"], writes=[("tab", i, kind, w)])
                        elif mode == "c":
                            P.op("dve", lambda e: e.tensor_tensor(
                                out=tab[:, i, kind, w, :], in0=modt[:, i, mlo:mlo + 32, w], in1=gv[:, gidx, i, :], op=ALU.mult),
                                reads=allm + ["gv"], writes=[("tab", i, kind, w)])
                        else:
                            P.op("dve", lambda e: e.tensor_copy(out=tab[:, i, kind, w, :], in_=modt[:, i, mlo:mlo + 32, w]),
                                 reads=allm, writes=[("tab", i, kind, w)])
                    mk(0, 32, 0, "a")
                    mk(1, 0, 0, "b")
                    mk(2, 64, 1, "c")
                    mk(3, 128, 2, "a")
                    mk(4, 96, 0, "b")
                    mk(5, 160, 3, "c")
            P.emit()

        _ph = [0]

        def _gate(f):
            def g(*a, **k):
                _ph[0] += 1
                if _STOP[0] is not None and _ph[0] > _STOP[0]:
                    return
                return f(*a, **k)
            return g

        def alloc_norm_bufs(es):
            xs = es.enter_context(SBT("xs", [128, 4, 512], F32))
            sqs = es.enter_context(SBT("sqs", [128, 4, 512], F32))
            rstd = es.enter_context(SBT("rstd", [128, 3, 512], F32))
            tmpn = es.enter_context(SBT("tmpn", [128, 4, 512], F32))
            return xs, sqs, rstd, tmpn, Rot(4)

        def rstd_from_ps(bank, rstd, j, T):
            P.op("act", lambda e: e.activation(out=rstd[:, j, :T], in_=ps[bank][:, :T], func=AF.Sqrt,
                                               bias=epsb[:, 0:1], scale=1.0 / D),
                 reads=[("ps", bank)], writes=[("rstd", j)])
            P.op("dve", lambda e: e.reciprocal(out=rstd[:, j, :T], in_=rstd[:, j, :T]), writes=[("rstd", j)])

        def norm_prologue(bufs, Xsrc, tiles, layer, kA, kB, act):
            xs, sqs, rstd, tmpn, xrot = bufs
            acts = []
            for j, (tok0, T, w) in enumerate(tiles):
                for c in range(KC):
                    sl = xrot.next()
                    P.op("sp", lambda e, sl=sl, c=c, tok0=tok0, T=T: e.dma_start(
                        out=xs[:, sl, :T], in_=Xsrc[c * 128:(c + 1) * 128, tok0:tok0 + T]), writes=[("xs", sl)], dma=True)
                    P.op("act", lambda e, sl=sl, T=T: e.activation(out=sqs[:, sl, :T], in_=xs[:, sl, :T], func=AF.Square),
                         reads=[("xs", sl)], writes=[("sqs", sl)])
                    P.op("pe", lambda e, sl=sl, T=T, c=c: e.matmul(ps[6][:, :T], lhsT=ones32[:], rhs=sqs[:, sl, :T],
                                                                  start=(c == 0), stop=(c == KC - 1)),
                         reads=[("sqs", sl), "ones32"], writes=[("ps", 6)])
                rstd_from_ps(6, rstd, j, T)
                for c in range(KC):
                    sl = xrot.next()
                    P.op("sp", lambda e, sl=sl, c=c, tok0=tok0, T=T: e.dma_start(
                        out=xs[:, sl, :T], in_=Xsrc[c * 128:(c + 1) * 128, tok0:tok0 + T]), writes=[("xs", sl)], dma=True)
                    P.op("dve", lambda e, sl=sl, j=j, T=T: e.tensor_tensor(out=tmpn[:, sl, :T], in0=xs[:, sl, :T], in1=rstd[:, j, :T], op=ALU.mult),
                         reads=[("xs", sl), ("rstd", j)], writes=[("tmpn", sl)])
                    P.op("act", lambda e, sl=sl, j=j, c=c, T=T, w=w: e.activation(
                        out=act[j][:, c, :T], in_=tmpn[:, sl, :T], func=AF.Identity,
                        bias=tab[:, layer, kB, w, c:c + 1], scale=tab[:, layer, kA, w, c:c + 1]),
                        reads=[("tmpn", sl)], writes=[("act", j, c)])
                acts.append(dict(ap=act[j], T=T, key=("act", j), tok0=tok0, w=w))
            return acts

        class Resid:
            def __init__(self, es, nslot=2, sbase=6):
                self.n = nslot
                self.sbase = sbase
                self.ysb = es.enter_context(SBT("r_ysb", [128, nslot, 512], F32))
                self.sq = es.enter_context(SBT("r_sq", [128, nslot, 512], F32))
                self.rstd = es.enter_context(SBT("r_rstd", [128, 3, 512], F32))
                self.ys = es.enter_context(SBT("r_ys", [128, nslot, 512], F32))
                self.xs = es.enter_context(SBT("r_xs", [128, nslot, 512], F32))
                self.xo = es.enter_context(SBT("r_xo", [128, nslot, 512], F32))
                self.tm = es.enter_context(SBT("r_tm", [128, nslot, 512], F32))
                self.yrot = Rot(nslot)
                self.srot = Rot(nslot)
                self.prot = Rot(nslot)
                self.pending = []

            def flush(self, keep=0):
                while len(self.pending) > keep:
                    self.pending.pop(0)()

            def epi(self, m, j, A, b):
                T = A["T"]
                tok0 = A["tok0"]
                self.flush(0)
                o = self.yrot.next()
                s = self.srot.next()
                ysb, sq = self.ysb, self.sq
                P.op("act", lambda e: e.activation(out=ysb[:, o, :T], in_=ps[b][:, :T], func=AF.Copy),
                     reads=[("ps", b)], writes=[("r_ysb", o)])
                P.op("sp", lambda e: e.dma_start(out=YT[m * 128:(m + 1) * 128, tok0:tok0 + T], in_=ysb[:, o, :T]),
                     reads=[("r_ysb", o)], writes=[("YT", m, j)], dma=True)
                P.op("act", lambda e: e.activation(out=sq[:, s, :T], in_=ps[b][:, :T], func=AF.Square),
                     reads=[("ps", b)], writes=[("r_sq", s)])

                sb_ = self.sbase + j

                def stat():
                    P.op("pe", lambda e: e.matmul(ps[sb_][:, :T], lhsT=ones32[:], rhs=sq[:, s, :T],
                                                  start=(m == 0), stop=(m == KC - 1)),
                         reads=[("r_sq", s), "ones32"], writes=[("ps", sb_)])
                self.pending.append(stat)

            def post(self, acts, Xprev, Xnext, layer, kC):
                self.flush(0)
                for j, A in enumerate(acts):
                    self._post_tile(j, A, Xprev, Xnext, layer, kC)

            def _post_tile(self, j, A, Xprev, Xnext, layer, kC):
                if True:
                    T = A["T"]
                    tok0 = A["tok0"]
                    w = A["w"]
                    rstd_from_ps(self.sbase + j, self.rstd, j, T)
                    for c in range(KC):
                        p = self.prot.next()
                        ys, xs, xo, tm, rstd = self.ys, self.xs, self.xo, self.tm, self.rstd
                        P.op("sp", lambda e, p=p, c=c: e.dma_start(out=ys[:, p, :T], in_=YT[c * 128:(c + 1) * 128, tok0:tok0 + T]),
                             reads=[("YT", c, j)], writes=[("r_ys", p)], dma=True)
                        P.op("sp", lambda e, p=p, c=c: e.dma_start(out=xs[:, p, :T], in_=Xprev[c * 128:(c + 1) * 128, tok0:tok0 + T]),
                             writes=[("r_xs", p)], dma=True)
                        P.op("dve", lambda e, p=p: e.tensor_tensor(out=tm[:, p, :T], in0=ys[:, p, :T], in1=rstd[:, j, :T], op=ALU.mult),
                             reads=[("r_ys", p), ("rstd", j)], writes=[("r_tm", p)])
                        P.op("dve", lambda e, p=p, c=c: e.scalar_tensor_tensor(
                            out=xo[:, p, :T], in0=tm[:, p, :T], scalar=tab[:, layer, kC, w, c:c + 1], in1=xs[:, p, :T],
                            op0=ALU.mult, op1=ALU.add), reads=[("r_tm", p), ("r_xs", p)], writes=[("r_xo", p)])
                        P.op("sp", lambda e, p=p, c=c: e.dma_start(out=Xnext[c * 128:(c + 1) * 128, tok0:tok0 + T], in_=xo[:, p, :T]),
                             reads=[("r_xo", p)], dma=True)

        @_gate
        def phase_s1(layer, Xsrc, with_ctx):
            even = layer % 2 == 0
            W = (even_w_in if even else odd_w_in)[layer // 2]
            with contextlib.ExitStack() as es:
                bufs = alloc_norm_bufs(es)
                act_ = es.enter_context(SBT("act", [128, 2, KC, 512], BF16))
                actc = es.enter_context(SBT("actc", [128, KC, 256], BF16))
                act = [act_[:, 0], act_[:, 1], actc]
                wb = es.enter_context(SBT("wb", [128, 3, KC, 256], BF16))
                obf = es.enter_context(SBT("obf", [128, 4, 512], BF16))
                q32 = es.enter_context(SBT("q32", [128, 2, 512], F32))
                rt1 = es.enter_context(SBT("rt1", [128, 2, 512], F32))
                rt2 = es.enter_context(SBT("rt2", [128, 2, 512], F32))
                rc = es.enter_context(SBT("rc", [128, 2, 512], F32))
                rs_ = es.enter_context(SBT("rs", [128, 2, 512], F32))
                rm = es.enter_context(SBT("rm", [128, 128], F32))
                P.op("sp", lambda e: e.dma_start(out=rm[:], in_=rotm), writes=["rm"], dma=True)
                bankrot = Rot(6)
                slotrot = Rot(3)
                orot = Rot(4)
                qrot = Rot(2)
                if even:
                    groups = []
                    for c0 in range(0, 9216, 256):
                        m = c0 // 128
                        mode = "t" if (32 <= m < 48 or m >= 68) else "f"
                        groups.append((c0, mode))
                else:
                    groups = [(c0, "f") for c0 in range(0, 8192, 256)]
                for tiles in tile_groups(with_ctx):
                    acts = norm_prologue(bufs, Xsrc, tiles, layer, 0, 1, act)
                    if even:
                        for j, A in enumerate(acts):
                            if A["w"] == 0:
                                P.op("sp", lambda e, j=j, A=A: e.dma_start(out=rc[:, j, :], in_=ropec[:, A["tok0"]:A["tok0"] + 512]),
                                     writes=[("rc", j)], dma=True)
                                P.op("sp", lambda e, j=j, A=A: e.dma_start(out=rs_[:, j, :], in_=ropes[:, A["tok0"]:A["tok0"] + 512]),
                                     writes=[("rs", j)], dma=True)

                    def epi(m, j, A, b, s=None):
                        T = A["T"]
                        tok0 = A["tok0"]
                        if s is not None:
                            vcol = (m - 4096) if m < 8192 else (m - 8704 + 2048)
                            o = orot.next()
                            P.op("act", lambda e: e.activation(out=obf[:, o, :256], in_=ps[b][:, :256], func=AF.Copy),
                                 reads=[("ps", b)], writes=[("obf", o)])
                            P.op("sp", lambda e: e.dma_start(
                                out=VT[tok0 + s * 128: tok0 + (s + 1) * 128, vcol:vcol + 256], in_=obf[:, o, :256]),
                                reads=[("obf", o)], dma=True)
                            return
                        rope = even and A["w"] == 0 and (48 <= m < 68)
                        o = orot.next()
                        if not rope:
                            P.op("act", lambda e: e.activation(out=obf[:, o, :T], in_=ps[b][:, :T], func=AF.Copy),
                                 reads=[("ps", b)], writes=[("obf", o)])
                        else:
                            q = qrot.next()
                            P.op("act", lambda e: e.activation(out=q32[:, q, :], in_=ps[b][:, :], func=AF.Copy),
                                 reads=[("ps", b)], writes=[("q32", q)])
                            P.op("pe", lambda e: e.matmul(ps[7][:, :], lhsT=rm[:], rhs=q32[:, q, :], start=True, stop=True),
                                 reads=[("q32", q), "rm"], writes=[("ps", 7)])
                            P.op("dve", lambda e: e.tensor_tensor(out=rt1[:, q, :], in0=q32[:, q, :], in1=rc[:, j, :], op=ALU.mult),
                                 reads=[("q32", q), ("rc", j)], writes=[("rt1", q)])
                            P.op("dve", lambda e: e.tensor_tensor(out=rt2[:, q, :], in0=ps[7][:, :], in1=rs_[:, j, :], op=ALU.mult),
                                 reads=[("ps", 7), ("rs", j)], writes=[("rt2", q)])
                            P.op("dve", lambda e: e.tensor_tensor(out=obf[:, o, :], in0=rt1[:, q, :], in1=rt2[:, q, :], op=ALU.add),
                                 reads=[("rt1", q), ("rt2", q)], writes=[("obf", o)])
                        P.op("sp", lambda e: e.dma_start(
                            out=PT[m * 128:(m + 1) * 128, tok0:tok0 + T], in_=obf[:, o, :T]), reads=[("obf", o)], dma=True)

                    gemm(P, ps, wb, 3, slotrot, W, KC, groups, acts, epi, bankrot, tag="w")
                P.emit()

        @_gate
        def phase_s2(layer, Xprev, Xnext, with_ctx):
            even = layer % 2 == 0
            W = (even_w_out if even else odd_w_out)[layer // 2]
            OTv = OT.rearrange("(c p) t -> p c t", p=128)
            with contextlib.ExitStack() as es:
                act_ = es.enter_context(SBT("act", [128, 2, KC, 512], BF16))
                actc = es.enter_context(SBT("actc", [128, KC, 256], BF16))
                act = [act_[:, 0], act_[:, 1], actc]
                wb = es.enter_context(SBT("wb", [128, 3, KC, 256], BF16))
                R = Resid(es, 3, sbase=5)
                bankrot = Rot(5)
                slotrot = Rot(3)
                groups = [(c0, "f") for c0 in range(0, D, 256)]
                for tiles in tile_groups(with_ctx):
                    acts = []
                    for j, (tok0, T, w) in enumerate(tiles):
                        for q in range(4):
                            P.op("sp", lambda e, j=j, q=q, tok0=tok0, T=T: e.dma_start(
                                out=act[j][:, q * 8:(q + 1) * 8, :T], in_=OTv[:, q * 8:(q + 1) * 8, tok0:tok0 + T]),
                                writes=[("act", j, c) for c in range(q * 8, (q + 1) * 8)], dma=True)
                        acts.append(dict(ap=act[j], T=T, key=("act", j), tok0=tok0, w=w))
                    gemm(P, ps, wb, 3, slotrot, W, KC, groups, acts, R.epi, bankrot, tag="w")
                    R.post(acts, Xprev, Xnext, layer, 2)
                P.emit()

        @_gate
        def phase_s3(layer, Xsrc, with_ctx):
            W = ffn_w_up[layer]
            with contextlib.ExitStack() as es:
                bufs = alloc_norm_bufs(es)
                act_ = es.enter_context(SBT("act", [128, 2, KC, 512], BF16))
                actc = es.enter_context(SBT("actc", [128, KC, 256], BF16))
                act = [act_[:, 0], act_[:, 1], actc]
                wb = es.enter_context(SBT("wb", [128, 3, KC, 256], BF16))
                of = es.enter_context(SBT("of", [128, 4, 512], F32))
                bankrot = Rot(6)
                slotrot = Rot(3)
                orot = Rot(4)
                groups = [(c0, "f") for c0 in range(0, 2 * HID, 256)]
                for tiles in tile_groups(with_ctx):
                    acts = norm_prologue(bufs, Xsrc, tiles, layer, 3, 4, act)

                    def epi(m, j, A, b):
                        T = A["T"]
                        col = hpcol(A["tok0"])
                        o = orot.next()
                        P.op("act", lambda e: e.activation(out=of[:, o, :T], in_=ps[b][:, :T], func=AF.Copy),
                             reads=[("ps", b)], writes=[("of", o)])
                        P.op("sp", lambda e: e.dma_start(out=HP[m * 128:(m + 1) * 128, col:col + T], in_=of[:, o, :T]),
                             reads=[("of", o)], dma=True)
                    gemm(P, ps, wb, 3, slotrot, W, KC, groups, acts, epi, bankrot, tag="w")
                P.emit()

        @_gate
        def phase_s4(layer, Xprev, Xnext, with_ctx):
            W = ffn_w_down[layer]
            KD = HID // 128
            with contextlib.ExitStack() as es:
                act = es.enter_context(SBT("act", [128, 2, KD, 512], BF16))
                wb = es.enter_context(SBT("wb", [128, 2, KD, 256], BF16))
                gs = es.enter_context(SBT("gs", [128, 2, 514], F32))
                vs = es.enter_context(SBT("vs", [128, 2, 514], F32))
                ca = es.enter_context(SBT("ca", [128, 2, 512], F32))
                cb = es.enter_context(SBT("cb", [128, 2, 512], F32))
                cc_ = es.enter_context(SBT("cc", [128, 2, 512], F32))
                cd_ = es.enter_context(SBT("cd", [128, 2, 512], F32))
                sg = es.enter_context(SBT("sg", [128, 2, 512], F32))
                R = Resid(es, 2)
                bankrot = Rot(6)
                slotrot = Rot(2)
                crot = Rot(2)
                groups = [(c0, "f") for c0 in range(0, D, 256)]
                if True:
                    def conv_tile(j, tok0, T, w):
                        col = hpcol(tok0)
                        for c in range(KD):
                            x = crot.next()
                            P.op("sp", lambda e, x=x, c=c: e.dma_start(out=gs[:, x, :T + 2], in_=HP[c * 128:(c + 1) * 128, col - 1:col + T + 1]),
                                 writes=[("gs", x)], dma=True)
                            P.op("sp", lambda e, x=x, c=c: e.dma_start(out=vs[:, x, :T + 2], in_=HP[(KD + c) * 128:(KD + c + 1) * 128, col - 1:col + T + 1]),
                                 writes=[("vs", x)], dma=True)
                            P.op("act", lambda e, x=x, c=c: e.activation(out=ca[:, x, :T], in_=gs[:, x, 1:T + 1], func=AF.Identity, scale=fcw[:, layer, c, 1:2]),
                                 reads=[("gs", x)], writes=[("ca", x)])
                            P.op("dve", lambda e, x=x, c=c: e.scalar_tensor_tensor(out=cb[:, x, :T], in0=gs[:, x, 0:T], scalar=fcw[:, layer, c, 0:1], in1=ca[:, x, :T], op0=ALU.mult, op1=ALU.add),
                                 reads=[("gs", x), ("ca", x)], writes=[("cb", x)])
                            P.op("dve", lambda e, x=x, c=c: e.scalar_tensor_tensor(out=ca[:, x, :T], in0=gs[:, x, 2:T + 2], scalar=fcw[:, layer, c, 2:3], in1=cb[:, x, :T], op0=ALU.mult, op1=ALU.add),
                                 reads=[("gs", x), ("cb", x)], writes=[("ca", x)])
                            P.op("act", lambda e, x=x: e.activation(out=sg[:, x, :T], in_=ca[:, x, :T], func=AF.Silu),
                                 reads=[("ca", x)], writes=[("sg", x)])
                            P.op("act", lambda e, x=x, c=c: e.activation(out=cc_[:, x, :T], in_=vs[:, x, 1:T + 1], func=AF.Identity, scale=fcw[:, layer, KD + c, 1:2]),
                                 reads=[("vs", x)], writes=[("cc", x)])
                            P.op("dve", lambda e, x=x, c=c: e.scalar_tensor_tensor(out=cd_[:, x, :T], in0=vs[:, x, 0:T], scalar=fcw[:, layer, KD + c, 0:1], in1=cc_[:, x, :T], op0=ALU.mult, op1=ALU.add),
                                 reads=[("vs", x), ("cc", x)], writes=[("cd", x)])
                            P.op("dve", lambda e, x=x, c=c: e.scalar_tensor_tensor(out=cc_[:, x, :T], in0=vs[:, x, 2:T + 2], scalar=fcw[:, layer, KD + c, 2:3], in1=cd_[:, x, :T], op0=ALU.mult, op1=ALU.add),
                                 reads=[("vs", x), ("cd", x)], writes=[("cc", x)])
                            P.op("dve", lambda e, x=x, c=c, j=j: e.tensor_tensor(out=act[:, j, c, :T], in0=sg[:, x, :T], in1=cc_[:, x, :T], op=ALU.mult),
                                 reads=[("sg", x), ("cc", x)], writes=[("act", j, c)])
                        return dict(ap=act[:, j], T=T, key=("act", j), tok0=tok0, w=w)
                    tl = [t[0] for t in tile_groups(with_ctx, per=1)]
                    nxt = conv_tile(0, *tl[0])
                    for gi in range(len(tl)):
                        cur = nxt
                        if gi + 1 < len(tl):
                            nxt = conv_tile((gi + 1) % 2, *tl[gi + 1])
                        gemm(P, ps, wb, 2, slotrot, W, KD, groups, [cur], R.epi, bankrot, tag="w")
                        R.post([cur], Xprev, Xnext, layer, 5)
                P.emit()

        @_gate
        def phase_even(layer, ctx_out):
            ei = layer // 2
            ncols = NTOK if ctx_out else S
            with contextlib.ExitStack() as es:
                qT = es.enter_context(SBT("qT", [128, 2, NTOK], BF16))
                kT = es.enter_context(SBT("kT", [128, 2, NTOK], BF16))
                vE = es.enter_context(SBT("vE", [128, 2, 34, 128], BF16))
                vO = es.enter_context(SBT("vO", [128, 2, 31, 128], BF16))
                tbl = es.enter_context(SBT("tbl", [128, 2, 14, 64], F32))
                oT = es.enter_context(SBT("oT", [128, 2, NTOK], BF16))
                sb = es.enter_context(SBT("sb", [128, 2, 4, 64], F32))
                pT = es.enter_context(SBT("pT", [128, 2, 640], BF16))
                pc = es.enter_context(SBT("pc", [128, 512], BF16))
                rec = es.enter_context(SBT("rec", [128, 2, 256], F32))
                dsb = es.enter_context(SBT("dsb", [128, 2, 256], F32))
                wm = es.enter_context(SBT("wm", [128, 256], F32))
                P.op("sp", lambda e: e.dma_start(out=wm[:], in_=wmask), writes=["wm"], dma=True)
                srot = Rot(4)
                orot = Rot(2)
                drot = Rot(2)
                xrot = Rot(2)
                VTa = VT[0:NTOK, :].rearrange("(n p) d -> p n d", p=128)
                VTo = VT[64:64 + 31 * 128, :].rearrange("(n p) d -> p n d", p=128)

                def load_v(dst_key, dst, slot, vcol, odd):
                    src = VTo if odd else VTa
                    n = 31 if odd else 34
                    half = n // 2
                    for (a, b_) in ((0, half), (half, n)):
                        P.op("sp", lambda e, a=a, b_=b_: e.dma_start(out=dst[:, slot, a:b_, :], in_=src[:, a:b_, vcol:vcol + 128]),
                             writes=[(dst_key, slot, a)], dma=True)
                    return [(dst_key, slot, 0), (dst_key, slot, half)]

                def ctx_attn(hb, kb, vb, vkeys, sink_ap):
                    sbk = srot.next()
                    for cc in range(2):
                        P.op("pe", lambda e, cc=cc: e.matmul(ps[sbk][:, cc * 256:(cc + 1) * 256],
                                                          lhsT=kT[:, kb, S + cc * 128:S + (cc + 1) * 128], rhs=qT[:, hb, S:NTOK],
                                                          start=True, stop=True),
                             reads=[("qT", hb), ("kT", kb)], writes=[("ps", sbk)])
                    P.op("act", lambda e: e.activation(out=pc[:, :], in_=ps[sbk][:, :], func=AF.Exp, scale=SCALE),
                         reads=[("ps", sbk)], writes=["pc"])
                    ob = 4 + orot.next()
                    db = 6 + drot.next()
                    for cc in range(2):
                        P.op("pe", lambda e, cc=cc: e.matmul(ps[ob][:, 0:256], lhsT=vE[:, vb, 32 + cc, :], rhs=pc[:, cc * 256:(cc + 1) * 256],
                                                          start=(cc == 0), stop=(cc == 1)),
                             reads=["pc"] + vkeys, writes=[("ps", ob)])
                    for cc in range(2):
                        P.op("pe", lambda e, cc=cc: e.matmul(ps[db][:, 0:256], lhsT=onesbf[:], rhs=pc[:, cc * 256:(cc + 1) * 256],
                                                          start=(cc == 0), stop=(cc == 1)),
                             reads=["pc", "onesbf"], writes=[("ps", db)])
                    x = xrot.next()
                    if sink_ap is not None:
                        P.op("dve", lambda e: e.tensor_scalar(out=dsb[:, x, :], in0=ps[db][:, 0:256], scalar1=sink_ap, scalar2=None, op0=ALU.add),
                             reads=[("ps", db), "esink"], writes=[("dsb", x)])
                        P.op("dve", lambda e: e.reciprocal(out=rec[:, x, :], in_=dsb[:, x, :]), reads=[("dsb", x)], writes=[("rec", x)])
                    else:
                        P.op("dve", lambda e: e.reciprocal(out=rec[:, x, :], in_=ps[db][:, 0:256]), reads=[("ps", db)], writes=[("rec", x)])
                    P.op("dve", lambda e: e.tensor_tensor(out=oT[:, hb, S:NTOK], in0=ps[ob][:, 0:256], in1=rec[:, x, :], op=ALU.mult),
                         reads=[("ps", ob), ("rec", x)], writes=[("oT", hb, "ctx")])

                for h in range(16):
                    hb = h % 2
                    P.op("sp", lambda e, h=h, hb=hb: e.dma_start(out=qT[:, hb, :], in_=PT[h * 128:(h + 1) * 128, :]), writes=[("qT", hb)], dma=True)
                    P.op("sp", lambda e, h=h, hb=hb: e.dma_start(out=kT[:, hb, :], in_=PT[2048 + h * 128:2048 + (h + 1) * 128, :]), writes=[("kT", hb)], dma=True)
                    vek = load_v("vE", vE, hb, h * 128, False)
                    vok = load_v("vO", vO, hb, h * 128, True)
                    P.op("sp", lambda e, h=h, hb=hb: e.dma_start(out=tbl[:, hb], in_=nbias[ei, h]), writes=[("tbl", hb)], dma=True)
                    okeys = []
                    for r in range(64):
                        rs = min(max(r - 4, 0), 56)
                        delta = rs - r + 7
                        sbk = srot.next()
                        for jc in range(4):
                            k0 = (rs + 2 * jc) * 64
                            P.op("pe", lambda e, jc=jc, k0=k0, r=r, sbk=sbk, hb=hb: e.matmul(
                                ps[sbk][:, jc * 64:(jc + 1) * 64], lhsT=kT[:, hb, k0:k0 + 128], rhs=qT[:, hb, r * 64:(r + 1) * 64],
                                start=True, stop=True), reads=[("qT", hb), ("kT", hb)], writes=[("ps", sbk)])
                        for cc in range(2):
                            P.op("pe", lambda e, cc=cc, r=r, sbk=sbk, hb=hb: e.matmul(
                                ps[sbk][:, 256 + cc * 64:256 + (cc + 1) * 64], lhsT=kT[:, hb, S + cc * 128:S + (cc + 1) * 128],
                                rhs=qT[:, hb, r * 64:(r + 1) * 64], start=True, stop=True),
                                reads=[("qT", hb), ("kT", hb)], writes=[("ps", sbk)])
                        x = xrot.next()
                        P.op("dve", lambda e, x=x, sbk=sbk, hb=hb, delta=delta: e.scalar_tensor_tensor(
                            out=sb[:, x], in0=ps[sbk][:, 0:256].rearrange("p (a b) -> p a b", b=64), scalar=SCALE,
                            in1=tbl[:, hb, delta:delta + 7:2, :], op0=ALU.mult, op1=ALU.add),
                            reads=[("ps", sbk), ("tbl", hb)], writes=[("sb", x)])
                        P.op("act", lambda e, x=x: e.activation(out=pT[:, x, 0:256], in_=sb[:, x].rearrange("p a b -> p (a b)"), func=AF.Exp),
                             reads=[("sb", x)], writes=[("pT", x, 0)])
                        P.op("act", lambda e, x=x, sbk=sbk: e.activation(out=pT[:, x, 256:384], in_=ps[sbk][:, 256:384], func=AF.Exp, scale=SCALE),
                             reads=[("ps", sbk)], writes=[("pT", x, 1)])
                        ob = 4 + orot.next()
                        db = 6 + drot.next()
                        for c in range(6):
                            if c < 4:
                                if rs % 2 == 0:
                                    lh = vE[:, hb, rs // 2 + c, :]
                                    vk = vek
                                else:
                                    lh = vO[:, hb, (rs - 1) // 2 + c, :]
                                    vk = vok
                            else:
                                lh = vE[:, hb, 32 + (c - 4), :]
                                vk = vek
                            P.op("pe", lambda e, c=c, lh=lh, x=x, ob=ob: e.matmul(ps[ob][:, 0:64], lhsT=lh, rhs=pT[:, x, c * 64:(c + 1) * 64],
                                                                               start=(c == 0), stop=(c == 5)),
                                 reads=[("pT", x, 0), ("pT", x, 1)] + vk, writes=[("ps", ob)])
                        for c in range(6):
                            P.op("pe", lambda e, c=c, x=x, db=db: e.matmul(ps[db][:, 0:64], lhsT=onesbf[:], rhs=pT[:, x, c * 64:(c + 1) * 64],
                                                                         start=(c == 0), stop=(c == 5)),
                                 reads=[("pT", x, 0), ("pT", x, 1), "onesbf"], writes=[("ps", db)])
                        P.op("dve", lambda e, x=x, db=db: e.reciprocal(out=rec[:, x, 0:64], in_=ps[db][:, 0:64]), reads=[("ps", db)], writes=[("rec", x)])
                        P.op("dve", lambda e, x=x, ob=ob, hb=hb, r=r: e.tensor_tensor(out=oT[:, hb, r * 64:(r + 1) * 64], in0=ps[ob][:, 0:64], in1=rec[:, x, 0:64], op=ALU.mult),
                             reads=[("ps", ob), ("rec", x)], writes=[("oT", hb, r)])
                        okeys.append(("oT", hb, r))
                    if ctx_out:
                        ctx_attn(hb, hb, hb, vek, None)
                        okeys.append(("oT", hb, "ctx"))
                    P.op("sp", lambda e, h=h, hb=hb: e.dma_start(out=OT[h * 128:(h + 1) * 128, 0:ncols], in_=oT[:, hb, 0:ncols]),
                         reads=okeys, dma=True)

                for g in range(4):
                    gb = g % 2
                    P.op("sp", lambda e, g=g, gb=gb: e.dma_start(out=kT[:, gb, :], in_=PT[8192 + g * 128:8192 + (g + 1) * 128, :]), writes=[("kT", gb)], dma=True)
                    vek = load_v("vE", vE, gb, 2048 + g * 128, False)
                    for hh in range(4):
                        h = 4 * g + hh
                        hb = h % 2
                        P.op("sp", lambda e, h=h, hb=hb: e.dma_start(out=qT[:, hb, :], in_=PT[6144 + h * 128:6144 + (h + 1) * 128, :]), writes=[("qT", hb)], dma=True)
                        okeys = []
                        for n in range(32):
                            pb = srot.next() % 2
                            b0, b1 = 2 * pb, 2 * pb + 1
                            chunks = []
                            if n > 0:
                                chunks.append((b0, 0, n - 1))
                            if n < 31:
                                chunks.append((b0, 128, n + 1))
                            chunks += [(b0, 256, n), (b0, 384, 32), (b1, 0, 33)]
                            for (bk, col, kbk) in chunks:
                                P.op("pe", lambda e, bk=bk, col=col, kbk=kbk, n=n, hb=hb, gb=gb: e.matmul(
                                    ps[bk][:, col:col + 128], lhsT=kT[:, gb, kbk * 128:(kbk + 1) * 128], rhs=qT[:, hb, n * 128:(n + 1) * 128],
                                    start=True, stop=True), reads=[("qT", hb), ("kT", gb)], writes=[("ps", bk)])
                            x = xrot.next()
                            lo = 0 if n > 0 else 128
                            hi = 256 if n < 31 else 128
                            sbf = sb[:, x].rearrange("p a b -> p (a b)")
                            P.op("dve", lambda e, x=x, b0=b0, lo=lo, hi=hi, sbf=sbf: e.scalar_tensor_tensor(
                                out=sbf[:, lo:hi], in0=ps[b0][:, lo:hi], scalar=SCALE, in1=wm[:, lo:hi], op0=ALU.mult, op1=ALU.add),
                                reads=[("ps", b0), "wm"], writes=[("sb", x)])
                            P.op("act", lambda e, x=x, lo=lo, hi=hi, sbf=sbf: e.activation(out=pT[:, x, lo:hi], in_=sbf[:, lo:hi], func=AF.Exp),
                                 reads=[("sb", x)], writes=[("pT", x, 0)])
                            P.op("act", lambda e, x=x, b0=b0: e.activation(out=pT[:, x, 256:512], in_=ps[b0][:, 256:512], func=AF.Exp, scale=SCALE),
                                 reads=[("ps", b0)], writes=[("pT", x, 1)])
                            P.op("act", lambda e, x=x, b1=b1: e.activation(out=pT[:, x, 512:640], in_=ps[b1][:, 0:128], func=AF.Exp, scale=SCALE),
                                 reads=[("ps", b1)], writes=[("pT", x, 2)])
                            ob = 4 + orot.next()
                            db = 6 + drot.next()
                            pcols = [(col if bk == b0 else 512, kbk) for (bk, col, kbk) in chunks]
                            nch = len(pcols)
                            for ci, (pcol, kbk) in enumerate(pcols):
                                P.op("pe", lambda e, ci=ci, pcol=pcol, kbk=kbk, x=x, ob=ob, gb=gb, nch=nch: e.matmul(
                                    ps[ob][:, 0:128], lhsT=vE[:, gb, kbk, :], rhs=pT[:, x, pcol:pcol + 128], start=(ci == 0), stop=(ci == nch - 1)),
                                    reads=[("pT", x, 0), ("pT", x, 1), ("pT", x, 2)] + vek, writes=[("ps", ob)])
                            for ci, (pcol, kbk) in enumerate(pcols):
                                P.op("pe", lambda e, ci=ci, pcol=pcol, x=x, db=db, nch=nch: e.matmul(
                                    ps[db][:, 0:128], lhsT=onesbf[:], rhs=pT[:, x, pcol:pcol + 128], start=(ci == 0), stop=(ci == nch - 1)),
                                    reads=[("pT", x, 0), ("pT", x, 1), ("pT", x, 2), "onesbf"], writes=[("ps", db)])
                            P.op("dve", lambda e, x=x, db=db, h=h: e.tensor_scalar(out=dsb[:, x, 0:128], in0=ps[db][:, 0:128], scalar1=esink[:, ei, h:h + 1], scalar2=None, op0=ALU.add),
                                 reads=[("ps", db), "esink"], writes=[("dsb", x)])
                            P.op("dve", lambda e, x=x: e.reciprocal(out=rec[:, x, 0:128], in_=dsb[:, x, 0:128]), reads=[("dsb", x)], writes=[("rec", x)])
                            P.op("dve", lambda e, x=x, ob=ob, hb=hb, n=n: e.tensor_tensor(out=oT[:, hb, n * 128:(n + 1) * 128], in0=ps[ob][:, 0:128], in1=rec[:, x, 0:128], op=ALU.mult),
                                 reads=[("ps", ob), ("rec", x)], writes=[("oT", hb, n)])
                            okeys.append(("oT", hb, n))
                        if ctx_out:
                            ctx_attn(hb, gb, gb, vek, esink[:, ei, h:h + 1])
                            okeys.append(("oT", hb, "ctx"))
                        P.op("sp", lambda e, h=h, hb=hb: e.dma_start(out=OT[2048 + h * 128:2048 + (h + 1) * 128, 0:ncols], in_=oT[:, hb, 0:ncols]),
                             reads=okeys, dma=True)
                P.emit()

        @_gate
        def phase_fnet(layer, ctx_out):
            ntile = 34 if ctx_out else 32
            ncols = NTOK if ctx_out else S
            sc_main = float((S * 128) ** -0.5)
            sc_ctx = float((L * 128) ** -0.5)
            dcv = dftc.rearrange("(c p) k -> p c k", p=128)
            dsv = dfts.rearrange("(c p) k -> p c k", p=128)
            with contextlib.ExitStack() as es:
                uT = es.enter_context(SBT("uT", [128, 2, NTOK], BF16))
                cd = es.enter_context(SBT("cdm", [128, 256], BF16))
                ucs = es.enter_context(SBT("ucs", [128, 4, 34, 256], BF16))
                dft = es.enter_context(SBT("dft", [128, 2, 2, 32, 256], BF16))
                d256 = es.enter_context(SBT("d256", [128, 2, 2, 256], BF16))
                fo = es.enter_context(SBT("fo", [128, 4, 256], BF16))
                P.op("sp", lambda e: e.dma_start(out=cd[:], in_=cdsd), writes=["cd"], dma=True)
                P.op("sp", lambda e: e.dma_start(out=d256[:], in_=dft256.rearrange("w (c p) k -> p w c k", p=128)), writes=["d256"], dma=True)
                bankrot = Rot(8)
                forot = Rot(4)
                dslot = Rot(2)
                for gb in range(4):
                    for gi in range(4):
                        g = 4 * gb + gi
                        ub = g % 2
                        P.op("sp", lambda e, g=g, ub=ub: e.dma_start(out=uT[:, ub, 0:ncols], in_=PT[g * 128:(g + 1) * 128, 0:ncols]), writes=[("uT", ub)], dma=True)
                        for t in range(ntile):
                            b = bankrot.next()
                            P.op("pe", lambda e, b=b, t=t, ub=ub: e.matmul(ps[b][:, 0:256], lhsT=uT[:, ub, t * 128:(t + 1) * 128], rhs=cd[:, :], start=True, stop=True),
                                 reads=[("uT", ub), "cd"], writes=[("ps", b)])
                            eng = "act" if t % 2 == 0 else "dve"
                            if eng == "act":
                                P.op("act", lambda e, b=b, t=t, gi=gi: e.activation(out=ucs[:, gi, t, :], in_=ps[b][:, 0:256], func=AF.Copy),
                                     reads=[("ps", b)], writes=[("ucs", gi, t)])
                            else:
                                P.op("dve", lambda e, b=b, t=t, gi=gi: e.tensor_copy(out=ucs[:, gi, t, :], in_=ps[b][:, 0:256]),
                                     reads=[("ps", b)], writes=[("ucs", gi, t)])
                    for kt in range(16):
                        sl = dslot.next()
                        for q in range(4):
                            P.op("sp", lambda e, sl=sl, q=q, kt=kt: e.dma_start(out=dft[:, sl, 0, q * 8:(q + 1) * 8, :], in_=dcv[:, q * 8:(q + 1) * 8, kt * 256:(kt + 1) * 256]),
                                 writes=[("dft", sl, 0, q)], dma=True)
                            P.op("sp", lambda e, sl=sl, q=q, kt=kt: e.dma_start(out=dft[:, sl, 1, q * 8:(q + 1) * 8, :], in_=dsv[:, q * 8:(q + 1) * 8, kt * 256:(kt + 1) * 256]),
                                 writes=[("dft", sl, 1, q)], dma=True)
                        for gi in range(4):
                            g = 4 * gb + gi
                            b = bankrot.next()
                            for tc in range(32):
                                P.op("pe", lambda e, b=b, tc=tc, gi=gi, sl=sl: e.matmul(ps[b][:, 0:256], lhsT=ucs[:, gi, tc, 0:128], rhs=dft[:, sl, 0, tc, :], start=(tc == 0), stop=False),
                                     reads=[("ucs", gi, tc), ("dft", sl, 0, tc // 8)], writes=[("ps", b)])
                                P.op("pe", lambda e, b=b, tc=tc, gi=gi, sl=sl: e.matmul(ps[b][:, 0:256], lhsT=ucs[:, gi, tc, 128:256], rhs=dft[:, sl, 1, tc, :], start=False, stop=(tc == 31)),
                                     reads=[("ucs", gi, tc), ("dft", sl, 1, tc // 8)], writes=[("ps", b)])
                            o = forot.next()
                            P.op("act", lambda e, b=b, o=o: e.activation(out=fo[:, o, :], in_=ps[b][:, 0:256], func=AF.Copy, scale=sc_main),
                                 reads=[("ps", b)], writes=[("fo", o)])
                            P.op("sp", lambda e, o=o, g=g, kt=kt: e.dma_start(out=OT[g * 128:(g + 1) * 128, kt * 256:(kt + 1) * 256], in_=fo[:, o, :]),
                                 reads=[("fo", o)], dma=True)
                    if ctx_out:
                        for gi in range(4):
                            g = 4 * gb + gi
                            b = bankrot.next()
                            for tc in range(2):
                                P.op("pe", lambda e, b=b, tc=tc, gi=gi: e.matmul(ps[b][:, 0:256], lhsT=ucs[:, gi, 32 + tc, 0:128], rhs=d256[:, 0, tc, :], start=(tc == 0), stop=False),
                                     reads=[("ucs", gi, 32 + tc), "d256"], writes=[("ps", b)])
                                P.op("pe", lambda e, b=b, tc=tc, gi=gi: e.matmul(ps[b][:, 0:256], lhsT=ucs[:, gi, 32 + tc, 128:256], rhs=d256[:, 1, tc, :], start=False, stop=(tc == 1)),
                                     reads=[("ucs", gi, 32 + tc), "d256"], writes=[("ps", b)])
                            o = forot.next()
                            P.op("act", lambda e, b=b, o=o: e.activation(out=fo[:, o, :], in_=ps[b][:, 0:256], func=AF.Copy, scale=sc_ctx),
                                 reads=[("ps", b)], writes=[("fo", o)])
                            P.op("sp", lambda e, o=o, g=g: e.dma_start(out=OT[g * 128:(g + 1) * 128, S:NTOK], in_=fo[:, o, :]),
                                 reads=[("fo", o)], dma=True)
                P.emit()

        @_gate
        def phase_sconv(layer, ctx_out):
            oi = layer // 2
            ncols = NTOK if ctx_out else S
            MW = NTOK + 6
            segs = [(1, 0, S)]
            if ctx_out:
                segs.append((S + 3, S, L))
            with contextlib.ExitStack() as es:
                bgT = es.enter_context(SBT("bgT", [128, 2, NTOK], BF16))
                cgT = es.enter_context(SBT("cgT", [128, 2, NTOK], BF16))
                hvT = es.enter_context(SBT("hvT", [128, 2, NTOK], BF16))
                mm = es.enter_context(SBT("mm", [128, 2, MW], F32))
                a0 = es.enter_context(SBT("a0", [128, NTOK], F32))
                a1 = es.enter_context(SBT("a1", [128, NTOK], F32))
                so = es.enter_context(SBT("so", [128, 2, NTOK], BF16))
                for s_ in range(2):
                    P.op("pool", lambda e, s_=s_: e.memset(mm[:, s_, :], 0.0), writes=[("mm", s_)])
                for c in range(16):
                    sl = c % 2
                    for (buf, key, row0) in ((bgT, "bg", 2048), (cgT, "cg", 4096), (hvT, "hv", 6144)):
                        P.op("sp", lambda e, buf=buf, row0=row0, c=c, sl=sl: e.dma_start(out=buf[:, sl, 0:ncols], in_=PT[row0 + c * 128:row0 + (c + 1) * 128, 0:ncols]),
                             writes=[(key, sl)], dma=True)
                    for (mo, to, n) in segs:
                        P.op("dve", lambda e, sl=sl, mo=mo, to=to, n=n: e.tensor_tensor(out=mm[:, sl, mo:mo + n], in0=cgT[:, sl, to:to + n], in1=hvT[:, sl, to:to + n], op=ALU.mult),
                             reads=[("cg", sl), ("hv", sl)], writes=[("mm", sl)])
                    for (mo, to, n) in segs:
                        P.op("act", lambda e, sl=sl, mo=mo, to=to, n=n, c=c: e.activation(out=a0[:, to:to + n], in_=mm[:, sl, mo:mo + n], func=AF.Identity, scale=ocw[:, oi, c, 1:2]),
                             reads=[("mm", sl)], writes=[("a0", to)])
                        P.op("dve", lambda e, sl=sl, mo=mo, to=to, n=n, c=c: e.scalar_tensor_tensor(out=a1[:, to:to + n], in0=mm[:, sl, mo - 1:mo - 1 + n], scalar=ocw[:, oi, c, 0:1], in1=a0[:, to:to + n], op0=ALU.mult, op1=ALU.add),
                             reads=[("mm", sl), ("a0", to)], writes=[("a1", to)])
                        P.op("dve", lambda e, sl=sl, mo=mo, to=to, n=n, c=c: e.scalar_tensor_tensor(out=a0[:, to:to + n], in0=mm[:, sl, mo + 1:mo + 1 + n], scalar=ocw[:, oi, c, 2:3], in1=a1[:, to:to + n], op0=ALU.mult, op1=ALU.add),
                             reads=[("mm", sl), ("a1", to)], writes=[("a0", to)])
                        P.op("dve", lambda e, sl=sl, to=to, n=n: e.tensor_tensor(out=so[:, sl, to:to + n], in0=a0[:, to:to + n], in1=bgT[:, sl, to:to + n], op=ALU.mult),
                             reads=[("a0", to), ("bg", sl)], writes=[("so", sl, to)])
                    P.op("sp", lambda e, sl=sl, c=c: e.dma_start(out=OT[2048 + c * 128:2048 + (c + 1) * 128, 0:ncols], in_=so[:, sl, 0:ncols]),
                         reads=[("so", sl, to) for (mo, to, n) in segs], dma=True)
                P.emit()

        Xa = xin
        for layer in range(N_LAYERS_RUN):
            ctx_live = any(j % 2 == 0 for j in range(layer + 1, DEPTH))
            need_hc = (layer % 2 == 0) or ctx_live
            phase_s1(layer, Xa, need_hc)
            if layer % 2 == 0:
                phase_even(layer, ctx_live)
            else:
                phase_fnet(layer, ctx_live)
                phase_sconv(layer, ctx_live)
            phase_s2(layer, Xa, xB, ctx_live)
            phase_s3(layer, xB, ctx_live)
            Xn = yout if layer == N_LAYERS_RUN - 1 else xC
            phase_s4(layer, xB, Xn, ctx_live and (layer != N_LAYERS_RUN - 1))
            Xa = xC
        P.emit(final=True)
        nops = P.nops
    return nc, nops


N_LAYERS_RUN = DEPTH
_NC = [NCORES]
_RUNKW = {}
_LAST = [None]
_DEBUG = [False]
_STOP = [None]


def _host_consts():
    t = np.arange(S)
    row = (t // 64).astype(np.float32)
    col = (t % 64).astype(np.float32)
    nf = 32
    inv = (10000.0 ** (-np.arange(nf, dtype=np.float32) / nf)).astype(np.float32)
    ropec = np.zeros((128, S), np.float32)
    ropes = np.zeros((128, S), np.float32)
    rotm = np.zeros((128, 128), np.float32)
    for a, pos in enumerate((row, col)):
        ang = pos[None, :] * inv[:, None]
        for j in range(2):
            d0 = a * 64 + j * 32
            ropec[d0:d0 + 32] = np.cos(ang)
            ropes[d0:d0 + 32] = -np.sin(ang) if j == 0 else np.sin(ang)
        for f in range(nf):
            rotm[a * 64 + f, a * 64 + 32 + f] = 1.0
            rotm[a * 64 + 32 + f, a * 64 + f] = 1.0
    kl = np.arange(128)[:, None]
    ql = np.arange(128)[None, :]
    wmask = np.zeros((128, 256), np.float32)
    wmask[:, 0:128] = np.where(kl >= ql, 0.0, NEG)
    wmask[:, 128:256] = np.where(kl <= ql, 0.0, NEG)
    d = np.arange(128)
    angd = 2 * np.pi * np.outer(d, d) / 128.0
    cdsd = np.concatenate([np.cos(angd), np.sin(angd)], axis=1).astype(ml_dtypes.bfloat16)
    tt = np.arange(S, dtype=np.int64)
    prod = (np.outer(tt, tt) % S).astype(np.float64) * (2 * np.pi / S)
    dftc = np.cos(prod).astype(ml_dtypes.bfloat16)
    dfts = (-np.sin(prod)).astype(ml_dtypes.bfloat16)
    t2 = np.arange(L, dtype=np.int64)
    p2 = (np.outer(t2, t2) % L).astype(np.float64) * (2 * np.pi / L)
    dft256 = np.stack([np.cos(p2), -np.sin(p2)]).astype(ml_dtypes.bfloat16)
    return dict(ropec=ropec, ropes=ropes, rotm=rotm, wmask=wmask, cdsd=cdsd, dftc=dftc, dfts=dfts, dft256=dft256)


def _nbias_table(rpb):
    E = rpb.shape[0]
    kc = np.arange(64)[:, None]
    c = np.arange(64)[None, :]
    cstart = np.clip(c - 8, 0, 48)
    ok = (kc >= cstart) & (kc < cstart + 16)
    rel = np.clip(kc - c, -15, 15) + 15
    out = np.full((E, 16, 128, 14, 64), NEG, np.float32)
    for jpar in range(2):
        for m in range(14):
            rr = m + jpar
            if rr > 14:
                continue
            g = rpb[:, :, rr, :][:, :, rel]
            out[:, :, jpar * 64:(jpar + 1) * 64, m, :] = np.where(ok[None, None], g, np.float32(NEG))
    return out


def kernel(x, c, ctx, c_ctx, w_mod, b_mod, g_mix_pre, g_mix_post, g_ffn_pre, g_ffn_post,
           even_w_in, even_rpb, even_sink, even_w_out, odd_w_in, odd_conv, odd_w_out,
           ffn_w_up, ffn_conv, ffn_w_down):
    f32 = np.float32
    x = np.asarray(x, f32); c = np.asarray(c, f32); ctx = np.asarray(ctx, f32); c_ctx = np.asarray(c_ctx, f32)
    nc, _ = build_program()
    consts = _host_consts()

    def pl(v, n):
        v = np.asarray(v, f32)
        lead = v.shape[:-1]
        return np.ascontiguousarray(np.moveaxis(v.reshape(lead + (n, 128)), -1, 0))

    shared = dict(
        w_mod=np.asarray(w_mod, f32),
        bmod=pl(b_mod, 192),
        gvec=np.ascontiguousarray(np.stack([pl(g_mix_pre, KC), pl(g_mix_post, KC), pl(g_ffn_pre, KC), pl(g_ffn_post, KC)], axis=1)),
        even_w_in=np.asarray(even_w_in, f32), even_w_out=np.asarray(even_w_out, f32),
        odd_w_in=np.asarray(odd_w_in, f32), odd_w_out=np.asarray(odd_w_out, f32),
        ffn_w_up=np.asarray(ffn_w_up, f32), ffn_w_down=np.asarray(ffn_w_down, f32),
        nbias=_nbias_table(np.asarray(even_rpb, f32)),
        sinkb=np.ascontiguousarray(np.broadcast_to(np.asarray(even_sink, f32)[None], (128, 2, 16))),
        oconv=np.ascontiguousarray(np.transpose(np.asarray(odd_conv, f32).reshape(2, 3, 16, 128), (3, 0, 2, 1))),
        fconv=np.ascontiguousarray(np.transpose(np.asarray(ffn_conv, f32).reshape(DEPTH, 3, 88, 128), (3, 0, 2, 1))),
        **consts,
    )
    in_maps = []
    for b in range(_NC[0]):
        xin = np.ascontiguousarray(np.concatenate([x[b], ctx[b]], axis=0).T)
        cv = np.ascontiguousarray(np.stack([c[b].reshape(KC, 128).T, c_ctx.reshape(KC, 128).T], axis=-1))
        m = dict(shared)
        m["xin"] = xin
        m["cvec"] = cv
        in_maps.append(m)
    res = run_bass_kernel_spmd(nc, in_maps, core_ids=list(range(_NC[0])), **_RUNKW)
    _LAST[0] = res
    out = np.stack([np.ascontiguousarray(res.results[b]["yout"].T) for b in range(_NC[0])], axis=0)
    return out.astype(f32)
```

```python
import contextlib
import numpy as np
import ml_dtypes
import concourse.bass as bass
import concourse.mybir as mybir
from concourse.bass_utils import run_bass_kernel_spmd

F32 = mybir.dt.float32
BF16 = mybir.dt.bfloat16
AF = mybir.ActivationFunctionType
ALU = mybir.AluOpType

D = 4096
S = 4096
L = 256
NTOK = S + L
DEPTH = 4
KC = 32
HID = 5632
NCORES = 4
NEG = -30000.0
SCALE = 128 ** -0.5
EPS = 1e-6

ENGS = ("pe", "act", "dve", "pool", "sp")


class Prog:
    ND = 40

    def __init__(self, nc, es):
        self.nc = nc
        self.esem = {e: es.enter_context(nc.semaphore("c_" + e)) for e in ENGS}
        self.ecnt = {e: 0 for e in ENGS}
        self.dsem = [es.enter_context(nc.semaphore("d%d" % i)) for i in range(self.ND)]
        self.dcnt = [0] * self.ND
        self.dnext = 0
        self.waited = {e: {} for e in ENGS}
        self.nops = 0
        self._reset()

    def _reset(self):
        self.streams = {e: [] for e in ENGS}
        self.lastw = {}
        self.readers = {}

    def op(self, eng, fn, reads=(), writes=(), dma=False):
        deps = []
        for k in reads:
            t = self.lastw.get(k)
            if t is not None:
                deps.append(t)
        for k in writes:
            t = self.lastw.get(k)
            if t is not None:
                deps.append(t)
            r = self.readers.get(k)
            if r:
                deps.extend(r.values())
        if dma:
            s = self.dnext
            self.dnext = (self.dnext + 1) % self.ND
            if self.dcnt[s] > 0:
                deps.append(("d%d" % s, self.dsem[s], self.dcnt[s]))
            self.dcnt[s] += 16
            tok = ("d%d" % s, self.dsem[s], self.dcnt[s])
            inc = (self.dsem[s], 16)
        else:
            self.ecnt[eng] += 1
            tok = (eng, self.esem[eng], self.ecnt[eng])
            inc = (self.esem[eng], 1)
        need = {}
        w = self.waited[eng]
        for (sid, sem, val) in deps:
            if sid == "pe" and eng == "pe" and not dma:
                continue
            if w.get(sid, 0) >= val:
                continue
            if sid not in need or need[sid][1] < val:
                need[sid] = (sem, val)
        for sid, (sem, val) in need.items():
            w[sid] = val
        self.streams[eng].append((list(need.values()), fn, inc))
        for k in writes:
            self.lastw[k] = tok
            self.readers[k] = {}
        for k in reads:
            if k in writes:
                continue
            r = self.readers.setdefault(k, {})
            old = r.get(tok[0])
            if old is None or old[2] < tok[2]:
                r[tok[0]] = tok
        self.nops += 1
        return tok

    def emit(self, final=False):
        nc = self.nc
        pre = self.pre if hasattr(self, "pre") else []
        fin = []
        for s in range(self.ND):
            if self.dcnt[s] > 0:
                fin.append(("d%d" % s, self.dsem[s], self.dcnt[s]))
        for e in ENGS:
            if self.ecnt[e] > 0:
                fin.append((e, self.esem[e], self.ecnt[e]))
        streams = self.streams

        def run(e, name):
            for sem, val in pre:
                e.wait_ge(sem, val)
            for waits, fn, inc in streams[name]:
                for sem, val in waits:
                    e.wait_ge(sem, val)
                ins = fn(e)
                ins.then_inc(inc[0], inc[1])
            if final and name == "sp":
                for sid, sem, val in fin:
                    e.wait_ge(sem, val)

        with nc.Block() as block:
            @block.tensor
            def _(e):
                run(e, "pe")

            @block.scalar
            def _(e):
                run(e, "act")

            @block.vector
            def _(e):
                run(e, "dve")

            @block.gpsimd
            def _(e):
                run(e, "pool")

            @block.sync
            def _(e):
                run(e, "sp")
        self.pre = [(sem, val) for (sid, sem, val) in fin]
        for e in ENGS:
            for sid, sem, val in fin:
                self.waited[e][sid] = val
        self._reset()


class Rot:
    def __init__(self, n):
        self.n = n
        self.i = 0

    def next(self):
        v = self.i
        self.i = (self.i + 1) % self.n
        return v


def gemm(P, ps, wb, nslot, slotrot, W, kcn, groups, acts, epi, bankrot, MG=256, tag="w", wq=("pool",)):
    Wv = W.rearrange("(c p) m -> p c m", p=128)
    nq = 4
    qs = [(q * kcn // nq, (q + 1) * kcn // nq) for q in range(nq)]

    def qof(c):
        for qi, (a, b) in enumerate(qs):
            if a <= c < b:
                return qi

    for gi, (col0, mode) in enumerate(groups):
        slot = slotrot.next()
        for qi, (a, b) in enumerate(qs):
            P.op(wq[qi % len(wq)], lambda e, slot=slot, a=a, b=b, col0=col0: e.dma_start(
                out=wb[:, slot, a:b, :], in_=Wv[:, a:b, col0:col0 + MG]),
                writes=[(tag, slot, qi)], dma=True)
        if mode == "f":
            for m in range(MG // 128):
                for j, A in enumerate(acts):
                    b = bankrot.next()
                    T = A["T"]
                    for c in range(kcn):
                        P.op("pe", lambda e, b=b, slot=slot, c=c, m=m, A=A, T=T: e.matmul(
                            ps[b][:, :T], lhsT=wb[:, slot, c, m * 128:(m + 1) * 128], rhs=A["ap"][:, c, :T],
                            start=(c == 0), stop=(c == kcn - 1)),
                            reads=[(tag, slot, qof(c)), A["key"] + (c,)], writes=[("ps", b)])
                    epi(col0 // 128 + m, j, A, b)
        else:
            for j, A in enumerate(acts):
                T = A["T"]
                for s in range(T // 128):
                    b = bankrot.next()
                    for c in range(kcn):
                        P.op("pe", lambda e, b=b, slot=slot, c=c, s=s, A=A: e.matmul(
                            ps[b][:, :MG], lhsT=A["ap"][:, c, s * 128:(s + 1) * 128], rhs=wb[:, slot, c, :],
                            start=(c == 0), stop=(c == kcn - 1)),
                            reads=[(tag, slot, qof(c)), A["key"] + (c,)], writes=[("ps", b)])
                    epi(col0, j, A, b, s)


def build_program():
    nc = bass.Bass("TRN2", target_bir_lowering=False)

    _uid = [0]

    def SBT(name, shape, dt):
        _uid[0] += 1
        return nc.sbuf_tensor("%s_%d" % (name, _uid[0]), shape, dt)

    def din(name, shape, dt=F32):
        return nc.dram_tensor(name, list(shape), dt, kind="ExternalInput").ap()

    def dscr(name, shape, dt):
        if _DEBUG[0]:
            return nc.dram_tensor(name, list(shape), dt, kind="ExternalOutput").ap()
        return nc.dram_tensor(name, list(shape), dt).ap()

    xin = din("xin", [D, NTOK])
    cvec = din("cvec", [128, KC, 2])
    w_mod = din("w_mod", [DEPTH, D, 6 * D])
    bmod = din("bmod", [128, DEPTH, 192])
    gvec = din("gvec", [128, 4, DEPTH, KC])
    even_w_in = din("even_w_in", [2, D, 9216])
    even_w_out = din("even_w_out", [2, D, D])
    odd_w_in = din("odd_w_in", [2, D, 8192])
    odd_w_out = din("odd_w_out", [2, D, D])
    ffn_w_up = din("ffn_w_up", [DEPTH, D, 2 * HID])
    ffn_w_down = din("ffn_w_down", [DEPTH, HID, D])
    nbias = din("nbias", [2, 16, 128, 14, 64])
    sinkb = din("sinkb", [128, 2, 16])
    oconv = din("oconv", [128, 2, 16, 3])
    fconv = din("fconv", [128, DEPTH, 88, 3])
    ropec = din("ropec", [128, S])
    ropes = din("ropes", [128, S])
    rotm = din("rotm", [128, 128])
    wmask = din("wmask", [128, 256])
    cdsd = din("cdsd", [128, 256], BF16)
    dftc = din("dftc", [S, S], BF16)
    dfts = din("dfts", [S, S], BF16)
    dft256 = din("dft256", [2, L, L], BF16)
    yout = nc.dram_tensor("yout", [D, S], F32, kind="ExternalOutput").ap()

    xB = dscr("xB", [D, NTOK], F32)
    xC = dscr("xC", [D, NTOK], F32)
    PT = dscr("PT", [9216, NTOK], BF16)
    VT = dscr("VT", [NTOK, 2560], BF16)
    OT = dscr("OT", [D, NTOK], BF16)
    YT = dscr("YT", [D, NTOK], F32)
    HPW = NTOK + 4
    HP = dscr("HP", [2 * HID, HPW], F32)

    def hpcol(tok0):
        return tok0 + 1 if tok0 < S else tok0 + 3

    MAIN_TILES = [(512 * j, 512, 0) for j in range(8)]
    CTX_TILE = (S, L, 1)

    def tile_groups(with_ctx, per=2):
        g = [list(MAIN_TILES[i:i + per]) for i in range(0, 8, per)]
        if with_ctx:
            if per == 2:
                g[-1].append(CTX_TILE)
            else:
                g.append([CTX_TILE])
        return g

    with contextlib.ExitStack() as ges:
        P = Prog(nc, ges)
        ps = [ges.enter_context(nc.psum_tensor("ps%d" % i, [128, 512], F32)) for i in range(8)]
        ones32 = ges.enter_context(SBT("ones32", [128, 128], F32))
        onesbf = ges.enter_context(SBT("onesbf", [128, 128], BF16))
        tab = ges.enter_context(SBT("tab", [128, DEPTH, 6, 2, KC], F32))
        fcw = ges.enter_context(SBT("fcw", [128, DEPTH, 88, 3], F32))
        ocw = ges.enter_context(SBT("ocw", [128, 2, 16, 3], F32))
        esink = ges.enter_context(SBT("esink", [128, 2, 16], F32))
        epsb = ges.enter_context(SBT("epsb", [128, 1], F32))

        with contextlib.ExitStack() as es:
            cv = es.enter_context(SBT("cv", [128, KC, 2], F32))
            cact = es.enter_context(SBT("cact", [128, KC, 2], BF16))
            bm = es.enter_context(SBT("bm", [128, DEPTH, 192], F32))
            gv = es.enter_context(SBT("gv", [128, 4, DEPTH, KC], F32))
            modt = es.enter_context(SBT("modt", [128, DEPTH, 192, 2], F32))
            zt = es.enter_context(SBT("zt", [128, 88, 1], F32))
            wbm = es.enter_context(SBT("wbm", [128, 3, KC, 256], BF16))
            P.op("dve", lambda e: e.memset(ones32[:], 1.0), writes=["ones32"])
            P.op("dve", lambda e: e.memset(onesbf[:], 1.0), writes=["onesbf"])
            P.op("dve", lambda e: e.memset(epsb[:], EPS), writes=["epsb"])
            P.op("dve", lambda e: e.memset(zt[:], 0.0), writes=["zt"])
            HPv = HP.rearrange("(c p) t -> p c t", p=128)
            for pc in (0, S + 1, S + 2, NTOK + 3):
                P.op("sp", lambda e, pc=pc: e.dma_start(out=HPv[:, :, pc:pc + 1], in_=zt[:], allow_slow_non_contiguous=True),
                     reads=["zt"], dma=True)
            P.op("sp", lambda e: e.dma_start(out=cv[:], in_=cvec), writes=["cv"], dma=True)
            P.op("sp", lambda e: e.dma_start(out=bm[:], in_=bmod), writes=["bm"], dma=True)
            P.op("sp", lambda e: e.dma_start(out=gv[:], in_=gvec), writes=["gv"], dma=True)
            P.op("sp", lambda e: e.dma_start(out=fcw[:], in_=fconv), writes=["fcw"], dma=True)
            P.op("sp", lambda e: e.dma_start(out=ocw[:], in_=oconv), writes=["ocw"], dma=True)
            P.op("sp", lambda e: e.dma_start(out=esink[:], in_=sinkb), writes=["esink"], dma=True)
            P.op("act", lambda e: e.activation(out=esink[:], in_=esink[:], func=AF.Exp), writes=["esink"])
            P.op("act", lambda e: e.activation(out=cact[:], in_=cv[:], func=AF.Silu), reads=["cv"],
                 writes=[("cact", c) for c in range(KC)])
            bankrot = Rot(6)
            slotrot = Rot(3)
            for i in range(DEPTH):
                def epi(m, j, A, b, i=i):
                    P.op("dve", lambda e, m=m, b=b, i=i: e.tensor_scalar(
                        out=modt[:, i, m, :], in0=ps[b][:, 0:2], scalar1=bm[:, i, m:m + 1], scalar2=None, op0=ALU.add),
                        reads=[("ps", b), "bm"], writes=[("modt", i, m)])
                groups = [(c0, "f") for c0 in range(0, 6 * D, 256)]
                gemm(P, ps, wbm, 3, slotrot, w_mod[i], KC, groups, [dict(ap=cact, T=2, key=("cact",))], epi, bankrot, tag="wm")
                allm = [("modt", i, m) for m in range(192)]
                for w in range(2):
                    def mk(kind, mlo, gidx, mode, i=i, w=w):
                        if mode == "a":
                            P.op("dve", lambda e: e.scalar_tensor_tensor(
                                out=tab[:, i, kind, w, :], in0=modt[:, i, mlo:mlo + 32, w], scalar=1.0, in1=gv[:, gidx, i, :],
                                op0=ALU.add, op1=ALU.mult), reads=allm + ["gv"], writes=[("tab", i, kind, w)])
                        elif mode == "c":
                            P.op("dve", lambda e: e.tensor_tensor(
                                out=tab[:, i, kind, w, :], in0=modt[:, i, mlo:mlo + 32, w], in1=gv[:, gidx, i, :], op=ALU.mult),
                                reads=allm + ["gv"], writes=[("tab", i, kind, w)])
                        else:
                            P.op("dve", lambda e: e.tensor_copy(out=tab[:, i, kind, w, :], in_=modt[:, i, mlo:mlo + 32, w]),
                                 reads=allm, writes=[("tab", i, kind, w)])
                    mk(0, 32, 0, "a")
                    mk(1, 0, 0, "b")
                    mk(2, 64, 1, "c")
                    mk(3, 128, 2, "a")
                    mk(4, 96, 0, "b")
                    mk(5, 160, 3, "c")
            P.emit()

        _ph = [0]

        def _gate(f):
            def g(*a, **k):
                _ph[0] += 1
                if _STOP[0] is not None and _ph[0] > _STOP[0]:
                    return
                return f(*a, **k)
            return g

        def alloc_norm_bufs(es):
            xs = es.enter_context(SBT("xs", [128, 4, 512], F32))
            sqs = es.enter_context(SBT("sqs", [128, 4, 512], F32))
            rstd = es.enter_context(SBT("rstd", [128, 3, 512], F32))
            tmpn = es.enter_context(SBT("tmpn", [128, 4, 512], F32))
            return xs, sqs, rstd, tmpn, Rot(4)

        def rstd_from_ps(bank, rstd, j, T):
            P.op("act", lambda e: e.activation(out=rstd[:, j, :T], in_=ps[bank][:, :T], func=AF.Sqrt,
                                               bias=epsb[:, 0:1], scale=1.0 / D),
                 reads=[("ps", bank)], writes=[("rstd", j)])
            P.op("dve", lambda e: e.reciprocal(out=rstd[:, j, :T], in_=rstd[:, j, :T]), writes=[("rstd", j)])

        def norm_prologue(bufs, Xsrc, tiles, layer, kA, kB, act):
            xs, sqs, rstd, tmpn, xrot = bufs
            acts = []
            for j, (tok0, T, w) in enumerate(tiles):
                for c in range(KC):
                    sl = xrot.next()
                    P.op("sp", lambda e, sl=sl, c=c, tok0=tok0, T=T: e.dma_start(
                        out=xs[:, sl, :T], in_=Xsrc[c * 128:(c + 1) * 128, tok0:tok0 + T]), writes=[("xs", sl)], dma=True)
                    P.op("act", lambda e, sl=sl, T=T: e.activation(out=sqs[:, sl, :T], in_=xs[:, sl, :T], func=AF.Square),
                         reads=[("xs", sl)], writes=[("sqs", sl)])
                    P.op("pe", lambda e, sl=sl, T=T, c=c: e.matmul(ps[6][:, :T], lhsT=ones32[:], rhs=sqs[:, sl, :T],
                                                                  start=(c == 0), stop=(c == KC - 1)),
                         reads=[("sqs", sl), "ones32"], writes=[("ps", 6)])
                rstd_from_ps(6, rstd, j, T)
                for c in range(KC):
                    sl = xrot.next()
                    P.op("sp", lambda e, sl=sl, c=c, tok0=tok0, T=T: e.dma_start(
                        out=xs[:, sl, :T], in_=Xsrc[c * 128:(c + 1) * 128, tok0:tok0 + T]), writes=[("xs", sl)], dma=True)
                    P.op("dve", lambda e, sl=sl, j=j, T=T: e.tensor_tensor(out=tmpn[:, sl, :T], in0=xs[:, sl, :T], in1=rstd[:, j, :T], op=ALU.mult),
                         reads=[("xs", sl), ("rstd", j)], writes=[("tmpn", sl)])
                    P.op("act", lambda e, sl=sl, j=j, c=c, T=T, w=w: e.activation(
                        out=act[j][:, c, :T], in_=tmpn[:, sl, :T], func=AF.Identity,
                        bias=tab[:, layer, kB, w, c:c + 1], scale=tab[:, layer, kA, w, c:c + 1]),
                        reads=[("tmpn", sl)], writes=[("act", j, c)])
                acts.append(dict(ap=act[j], T=T, key=("act", j), tok0=tok0, w=w))
            return acts

        class Resid:
            def __init__(self, es, nslot=2, sbase=6):
                self.n = nslot
                self.sbase = sbase
                self.ysb = es.enter_context(SBT("r_ysb", [128, nslot, 512], F32))
                self.sq = es.enter_context(SBT("r_sq", [128, nslot, 512], F32))
                self.rstd = es.enter_context(SBT("r_rstd", [128, 3, 512], F32))
                self.ys = es.enter_context(SBT("r_ys", [128, nslot, 512], F32))
                self.xs = es.enter_context(SBT("r_xs", [128, nslot, 512], F32))
                self.xo = es.enter_context(SBT("r_xo", [128, nslot, 512], F32))
                self.tm = es.enter_context(SBT("r_tm", [128, nslot, 512], F32))
                self.yrot = Rot(nslot)
                self.srot = Rot(nslot)
                self.prot = Rot(nslot)
                self.pending = []

            def flush(self, keep=0):
                while len(self.pending) > keep:
                    self.pending.pop(0)()

            def epi(self, m, j, A, b):
                T = A["T"]
                tok0 = A["tok0"]
                self.flush(0)
                o = self.yrot.next()
                s = self.srot.next()
                ysb, sq = self.ysb, self.sq
                P.op("act", lambda e: e.activation(out=ysb[:, o, :T], in_=ps[b][:, :T], func=AF.Copy),
                     reads=[("ps", b)], writes=[("r_ysb", o)])
                P.op("act", lambda e: e.activation(out=sq[:, s, :T], in_=ps[b][:, :T], func=AF.Square),
                     reads=[("ps", b)], writes=[("r_sq", s)])
                P.op("act", lambda e: e.dma_start(out=YT[m * 128:(m + 1) * 128, tok0:tok0 + T], in_=ysb[:, o, :T]),
                     reads=[("r_ysb", o)], writes=[("YT", m, j)], dma=True)

                sb_ = self.sbase + j

                def stat():
                    P.op("pe", lambda e: e.matmul(ps[sb_][:, :T], lhsT=ones32[:], rhs=sq[:, s, :T],
                                                  start=(m == 0), stop=(m == KC - 1)),
                         reads=[("r_sq", s), "ones32"], writes=[("ps", sb_)])
                self.pending.append(stat)

            def post(self, acts, Xprev, Xnext, layer, kC):
                self.flush(0)
                for j, A in enumerate(acts):
                    self._post_tile(j, A, Xprev, Xnext, layer, kC)

            def _post_tile(self, j, A, Xprev, Xnext, layer, kC):
                if True:
                    T = A["T"]
                    tok0 = A["tok0"]
                    w = A["w"]
                    rstd_from_ps(self.sbase + j, self.rstd, j, T)
                    for c in range(KC):
                        p = self.prot.next()
                        ys, xs, xo, tm, rstd = self.ys, self.xs, self.xo, self.tm, self.rstd
                        P.op("sp", lambda e, p=p, c=c: e.dma_start(out=ys[:, p, :T], in_=YT[c * 128:(c + 1) * 128, tok0:tok0 + T]),
                             reads=[("YT", c, j)], writes=[("r_ys", p)], dma=True)
                        P.op("sp", lambda e, p=p, c=c: e.dma_start(out=xs[:, p, :T], in_=Xprev[c * 128:(c + 1) * 128, tok0:tok0 + T]),
                             writes=[("r_xs", p)], dma=True)
                        P.op("dve", lambda e, p=p: e.tensor_tensor(out=tm[:, p, :T], in0=ys[:, p, :T], in1=rstd[:, j, :T], op=ALU.mult),
                             reads=[("r_ys", p), ("rstd", j)], writes=[("r_tm", p)])
                        P.op("dve", lambda e, p=p, c=c: e.scalar_tensor_tensor(
                            out=xo[:, p, :T], in0=tm[:, p, :T], scalar=tab[:, layer, kC, w, c:c + 1], in1=xs[:, p, :T],
                            op0=ALU.mult, op1=ALU.add), reads=[("r_tm", p), ("r_xs", p)], writes=[("r_xo", p)])
                        P.op("sp", lambda e, p=p, c=c: e.dma_start(out=Xnext[c * 128:(c + 1) * 128, tok0:tok0 + T], in_=xo[:, p, :T]),
                             reads=[("r_xo", p)], dma=True)

        @_gate
        def phase_s1(layer, Xsrc, with_ctx):
            even = layer % 2 == 0
            W = (even_w_in if even else odd_w_in)[layer // 2]
            with contextlib.ExitStack() as es:
                bufs = alloc_norm_bufs(es)
                act_ = es.enter_context(SBT("act", [128, 2, KC, 512], BF16))
                actc = es.enter_context(SBT("actc", [128, KC, 256], BF16))
                act = [act_[:, 0], act_[:, 1], actc]
                wb = es.enter_context(SBT("wb", [128, 3, KC, 256], BF16))
                obf = es.enter_context(SBT("obf", [128, 4, 512], BF16))
                q32 = es.enter_context(SBT("q32", [128, 2, 512], F32))
                rt1 = es.enter_context(SBT("rt1", [128, 2, 512], F32))
                rt2 = es.enter_context(SBT("rt2", [128, 2, 512], F32))
                rc = es.enter_context(SBT("rc", [128, 2, 512], F32))
                rs_ = es.enter_context(SBT("rs", [128, 2, 512], F32))
                rm = es.enter_context(SBT("rm", [128, 128], F32))
                P.op("sp", lambda e: e.dma_start(out=rm[:], in_=rotm), writes=["rm"], dma=True)
                bankrot = Rot(6)
                slotrot = Rot(3)
                orot = Rot(4)
                qrot = Rot(2)
                if even:
                    groups = []
                    for c0 in range(0, 9216, 256):
                        m = c0 // 128
                        mode = "t" if (32 <= m < 48 or m >= 68) else "f"
                        groups.append((c0, mode))
                else:
                    groups = [(c0, "f") for c0 in range(0, 8192, 256)]
                for tiles in tile_groups(with_ctx):
                    acts = norm_prologue(bufs, Xsrc, tiles, layer, 0, 1, act)
                    if even:
                        for j, A in enumerate(acts):
                            if A["w"] == 0:
                                P.op("sp", lambda e, j=j, A=A: e.dma_start(out=rc[:, j, :], in_=ropec[:, A["tok0"]:A["tok0"] + 512]),
                                     writes=[("rc", j)], dma=True)
                                P.op("sp", lambda e, j=j, A=A: e.dma_start(out=rs_[:, j, :], in_=ropes[:, A["tok0"]:A["tok0"] + 512]),
                                     writes=[("rs", j)], dma=True)

                    def epi(m, j, A, b, s=None):
                        T = A["T"]
                        tok0 = A["tok0"]
                        if s is not None:
                            vcol = (m - 4096) if m < 8192 else (m - 8704 + 2048)
                            o = orot.next()
                            P.op("act", lambda e: e.activation(out=obf[:, o, :256], in_=ps[b][:, :256], func=AF.Copy),
                                 reads=[("ps", b)], writes=[("obf", o)])
                            P.op("act", lambda e: e.dma_start(
                                out=VT[tok0 + s * 128: tok0 + (s + 1) * 128, vcol:vcol + 256], in_=obf[:, o, :256]),
                                reads=[("obf", o)], dma=True)
                            return
                        rope = even and A["w"] == 0 and (48 <= m < 68)
                        o = orot.next()
                        if not rope:
                            P.op("act", lambda e: e.activation(out=obf[:, o, :T], in_=ps[b][:, :T], func=AF.Copy),
                                 reads=[("ps", b)], writes=[("obf", o)])
                        else:
                            q = qrot.next()
                            P.op("act", lambda e: e.activation(out=q32[:, q, :], in_=ps[b][:, :], func=AF.Copy),
                                 reads=[("ps", b)], writes=[("q32", q)])
                            P.op("pe", lambda e: e.matmul(ps[7][:, :], lhsT=rm[:], rhs=q32[:, q, :], start=True, stop=True),
                                 reads=[("q32", q), "rm"], writes=[("ps", 7)])
                            P.op("dve", lambda e: e.tensor_tensor(out=rt1[:, q, :], in0=q32[:, q, :], in1=rc[:, j, :], op=ALU.mult),
                                 reads=[("q32", q), ("rc", j)], writes=[("rt1", q)])
                            P.op("dve", lambda e: e.tensor_tensor(out=rt2[:, q, :], in0=ps[7][:, :], in1=rs_[:, j, :], op=ALU.mult),
                                 reads=[("ps", 7), ("rs", j)], writes=[("rt2", q)])
                            P.op("dve", lambda e: e.tensor_tensor(out=obf[:, o, :], in0=rt1[:, q, :], in1=rt2[:, q, :], op=ALU.add),
                                 reads=[("rt1", q), ("rt2", q)], writes=[("obf", o)])
                        P.op("act", lambda e: e.dma_start(
                            out=PT[m * 128:(m + 1) * 128, tok0:tok0 + T], in_=obf[:, o, :T]), reads=[("obf", o)], dma=True)

                    gemm(P, ps, wb, 3, slotrot, W, KC, groups, acts, epi, bankrot, tag="w")
                P.emit()

        @_gate
        def phase_s2(layer, Xprev, Xnext, with_ctx):
            even = layer % 2 == 0
            W = (even_w_out if even else odd_w_out)[layer // 2]
            OTv = OT.rearrange("(c p) t -> p c t", p=128)
            with contextlib.ExitStack() as es:
                act_ = es.enter_context(SBT("act", [128, 2, KC, 512], BF16))
                actc = es.enter_context(SBT("actc", [128, KC, 256], BF16))
                act = [act_[:, 0], act_[:, 1], actc]
                wb = es.enter_context(SBT("wb", [128, 3, KC, 256], BF16))
                R = Resid(es, 3, sbase=5)
                bankrot = Rot(5)
                slotrot = Rot(3)
                groups = [(c0, "f") for c0 in range(0, D, 256)]
                for tiles in tile_groups(with_ctx):
                    acts = []
                    for j, (tok0, T, w) in enumerate(tiles):
                        for q in range(4):
                            P.op("sp", lambda e, j=j, q=q, tok0=tok0, T=T: e.dma_start(
                                out=act[j][:, q * 8:(q + 1) * 8, :T], in_=OTv[:, q * 8:(q + 1) * 8, tok0:tok0 + T]),
                                writes=[("act", j, c) for c in range(q * 8, (q + 1) * 8)], dma=True)
                        acts.append(dict(ap=act[j], T=T, key=("act", j), tok0=tok0, w=w))
                    gemm(P, ps, wb, 3, slotrot, W, KC, groups, acts, R.epi, bankrot, tag="w")
                    R.post(acts, Xprev, Xnext, layer, 2)
                P.emit()

        @_gate
        def phase_s3(layer, Xsrc, with_ctx):
            W = ffn_w_up[layer]
            with contextlib.ExitStack() as es:
                bufs = alloc_norm_bufs(es)
                act_ = es.enter_context(SBT("act", [128, 2, KC, 512], BF16))
                actc = es.enter_context(SBT("actc", [128, KC, 256], BF16))
                act = [act_[:, 0], act_[:, 1], actc]
                wb = es.enter_context(SBT("wb", [128, 3, KC, 256], BF16))
                of = es.enter_context(SBT("of", [128, 4, 512], F32))
                bankrot = Rot(6)
                slotrot = Rot(3)
                orot = Rot(4)
                groups = [(c0, "f") for c0 in range(0, 2 * HID, 256)]
                for tiles in tile_groups(with_ctx):
                    acts = norm_prologue(bufs, Xsrc, tiles, layer, 3, 4, act)

                    def epi(m, j, A, b):
                        T = A["T"]
                        col = hpcol(A["tok0"])
                        o = orot.next()
                        P.op("act", lambda e: e.activation(out=of[:, o, :T], in_=ps[b][:, :T], func=AF.Copy),
                             reads=[("ps", b)], writes=[("of", o)])
                        P.op("act", lambda e: e.dma_start(out=HP[m * 128:(m + 1) * 128, col:col + T], in_=of[:, o, :T]),
                             reads=[("of", o)], dma=True)
                    gemm(P, ps, wb, 3, slotrot, W, KC, groups, acts, epi, bankrot, tag="w")
                P.emit()

        @_gate
        def phase_s4(layer, Xprev, Xnext, with_ctx):
            W = ffn_w_down[layer]
            KD = HID // 128
            with contextlib.ExitStack() as es:
                act = es.enter_context(SBT("act", [128, 2, KD, 512], BF16))
                wb = es.enter_context(SBT("wb", [128, 2, KD, 256], BF16))
                gs = es.enter_context(SBT("gs", [128, 2, 514], F32))
                vs = es.enter_context(SBT("vs", [128, 2, 514], F32))
                ca = es.enter_context(SBT("ca", [128, 2, 512], F32))
                cb = es.enter_context(SBT("cb", [128, 2, 512], F32))
                cc_ = es.enter_context(SBT("cc", [128, 2, 512], F32))
                cd_ = es.enter_context(SBT("cd", [128, 2, 512], F32))
                sg = es.enter_context(SBT("sg", [128, 2, 512], F32))
                R = Resid(es, 2)
                bankrot = Rot(6)
                slotrot = Rot(2)
                crot = Rot(2)
                groups = [(c0, "f") for c0 in range(0, D, 256)]
                if True:
                    def conv_tile(j, tok0, T, w):
                        col = hpcol(tok0)
                        for c in range(KD):
                            x = crot.next()
                            P.op("sp", lambda e, x=x, c=c: e.dma_start(out=gs[:, x, :T + 2], in_=HP[c * 128:(c + 1) * 128, col - 1:col + T + 1]),
                                 writes=[("gs", x)], dma=True)
                            P.op("sp", lambda e, x=x, c=c: e.dma_start(out=vs[:, x, :T + 2], in_=HP[(KD + c) * 128:(KD + c + 1) * 128, col - 1:col + T + 1]),
                                 writes=[("vs", x)], dma=True)
                            P.op("act", lambda e, x=x, c=c: e.activation(out=ca[:, x, :T], in_=gs[:, x, 1:T + 1], func=AF.Identity, scale=fcw[:, layer, c, 1:2]),
                                 reads=[("gs", x)], writes=[("ca", x)])
                            P.op("dve", lambda e, x=x, c=c: e.scalar_tensor_tensor(out=cb[:, x, :T], in0=gs[:, x, 0:T], scalar=fcw[:, layer, c, 0:1], in1=ca[:, x, :T], op0=ALU.mult, op1=ALU.add),
                                 reads=[("gs", x), ("ca", x)], writes=[("cb", x)])
                            P.op("dve", lambda e, x=x, c=c: e.scalar_tensor_tensor(out=ca[:, x, :T], in0=gs[:, x, 2:T + 2], scalar=fcw[:, layer, c, 2:3], in1=cb[:, x, :T], op0=ALU.mult, op1=ALU.add),
                                 reads=[("gs", x), ("cb", x)], writes=[("ca", x)])
                            P.op("act", lambda e, x=x: e.activation(out=sg[:, x, :T], in_=ca[:, x, :T], func=AF.Silu),
                                 reads=[("ca", x)], writes=[("sg", x)])
                            P.op("act", lambda e, x=x, c=c: e.activation(out=cc_[:, x, :T], in_=vs[:, x, 1:T + 1], func=AF.Identity, scale=fcw[:, layer, KD + c, 1:2]),
                                 reads=[("vs", x)], writes=[("cc", x)])
                            P.op("dve", lambda e, x=x, c=c: e.scalar_tensor_tensor(out=cd_[:, x, :T], in0=vs[:, x, 0:T], scalar=fcw[:, layer, KD + c, 0:1], in1=cc_[:, x, :T], op0=ALU.mult, op1=ALU.add),
                                 reads=[("vs", x), ("cc", x)], writes=[("cd", x)])
                            P.op("dve", lambda e, x=x, c=c: e.scalar_tensor_tensor(out=cc_[:, x, :T], in0=vs[:, x, 2:T + 2], scalar=fcw[:, layer, KD + c, 2:3], in1=cd_[:, x, :T], op0=ALU.mult, op1=ALU.add),
                                 reads=[("vs", x), ("cd", x)], writes=[("cc", x)])
                            P.op("dve", lambda e, x=x, c=c, j=j: e.tensor_tensor(out=act[:, j, c, :T], in0=sg[:, x, :T], in1=cc_[:, x, :T], op=ALU.mult),
                                 reads=[("sg", x), ("cc", x)], writes=[("act", j, c)])
                        return dict(ap=act[:, j], T=T, key=("act", j), tok0=tok0, w=w)
                    tl = [t[0] for t in tile_groups(with_ctx, per=1)]
                    nxt = conv_tile(0, *tl[0])
                    for gi in range(len(tl)):
                        cur = nxt
                        if gi + 1 < len(tl):
                            nxt = conv_tile((gi + 1) % 2, *tl[gi + 1])
                        gemm(P, ps, wb, 2, slotrot, W, KD, groups, [cur], R.epi, bankrot, tag="w")
                        R.post([cur], Xprev, Xnext, layer, 5)
                P.emit()

        @_gate
        def phase_even(layer, ctx_out):
            ei = layer // 2
            ncols = NTOK if ctx_out else S
            with contextlib.ExitStack() as es:
                qT = es.enter_context(SBT("qT", [128, 2, NTOK], BF16))
                kT = es.enter_context(SBT("kT", [128, 2, NTOK], BF16))
                vE = es.enter_context(SBT("vE", [128, 2, 34, 128], BF16))
                vO = es.enter_context(SBT("vO", [128, 2, 31, 128], BF16))
                tbl = es.enter_context(SBT("tbl", [128, 2, 14, 64], F32))
                oT = es.enter_context(SBT("oT", [128, 2, NTOK], BF16))
                sb = es.enter_context(SBT("sb", [128, 2, 4, 64], F32))
                pT = es.enter_context(SBT("pT", [128, 2, 640], BF16))
                pc = es.enter_context(SBT("pc", [128, 512], BF16))
                rec = es.enter_context(SBT("rec", [128, 2, 256], F32))
                dsb = es.enter_context(SBT("dsb", [128, 2, 256], F32))
                wm = es.enter_context(SBT("wm", [128, 256], F32))
                P.op("sp", lambda e: e.dma_start(out=wm[:], in_=wmask), writes=["wm"], dma=True)
                srot = Rot(4)
                orot = Rot(2)
                drot = Rot(2)
                xrot = Rot(2)
                VTa = VT[0:NTOK, :].rearrange("(n p) d -> p n d", p=128)
                VTo = VT[64:64 + 31 * 128, :].rearrange("(n p) d -> p n d", p=128)

                def load_v(dst_key, dst, slot, vcol, odd):
                    src = VTo if odd else VTa
                    n = 31 if odd else 34
                    half = n // 2
                    for (a, b_) in ((0, half), (half, n)):
                        P.op("sp", lambda e, a=a, b_=b_: e.dma_start(out=dst[:, slot, a:b_, :], in_=src[:, a:b_, vcol:vcol + 128]),
                             writes=[(dst_key, slot, a)], dma=True)
                    return [(dst_key, slot, 0), (dst_key, slot, half)]

                def ctx_attn(hb, kb, vb, vkeys, sink_ap):
                    sbk = srot.next()
                    for cc in range(2):
                        P.op("pe", lambda e, cc=cc: e.matmul(ps[sbk][:, cc * 256:(cc + 1) * 256],
                                                          lhsT=kT[:, kb, S + cc * 128:S + (cc + 1) * 128], rhs=qT[:, hb, S:NTOK],
                                                          start=True, stop=True),
                             reads=[("qT", hb), ("kT", kb)], writes=[("ps", sbk)])
                    P.op("act", lambda e: e.activation(out=pc[:, :], in_=ps[sbk][:, :], func=AF.Exp, scale=SCALE),
                         reads=[("ps", sbk)], writes=["pc"])
                    ob = 4 + orot.next()
                    db = 6 + drot.next()
                    for cc in range(2):
                        P.op("pe", lambda e, cc=cc: e.matmul(ps[ob][:, 0:256], lhsT=vE[:, vb, 32 + cc, :], rhs=pc[:, cc * 256:(cc + 1) * 256],
                                                          start=(cc == 0), stop=(cc == 1)),
                             reads=["pc"] + vkeys, writes=[("ps", ob)])
                    for cc in range(2):
                        P.op("pe", lambda e, cc=cc: e.matmul(ps[db][:, 0:256], lhsT=onesbf[:], rhs=pc[:, cc * 256:(cc + 1) * 256],
                                                          start=(cc == 0), stop=(cc == 1)),
                             reads=["pc", "onesbf"], writes=[("ps", db)])
                    x = xrot.next()
                    if sink_ap is not None:
                        P.op("dve", lambda e: e.tensor_scalar(out=dsb[:, x, :], in0=ps[db][:, 0:256], scalar1=sink_ap, scalar2=None, op0=ALU.add),
                             reads=[("ps", db), "esink"], writes=[("dsb", x)])
                        P.op("dve", lambda e: e.reciprocal(out=rec[:, x, :], in_=dsb[:, x, :]), reads=[("dsb", x)], writes=[("rec", x)])
                    else:
                        P.op("dve", lambda e: e.reciprocal(out=rec[:, x, :], in_=ps[db][:, 0:256]), reads=[("ps", db)], writes=[("rec", x)])
                    P.op("dve", lambda e: e.tensor_tensor(out=oT[:, hb, S:NTOK], in0=ps[ob][:, 0:256], in1=rec[:, x, :], op=ALU.mult),
                         reads=[("ps", ob), ("rec", x)], writes=[("oT", hb, "ctx")])

                for h in range(16):
                    hb = h % 2
                    P.op("sp", lambda e, h=h, hb=hb: e.dma_start(out=qT[:, hb, :], in_=PT[h * 128:(h + 1) * 128, :]), writes=[("qT", hb)], dma=True)
                    P.op("sp", lambda e, h=h, hb=hb: e.dma_start(out=kT[:, hb, :], in_=PT[2048 + h * 128:2048 + (h + 1) * 128, :]), writes=[("kT", hb)], dma=True)
                    vek = load_v("vE", vE, hb, h * 128, False)
                    vok = load_v("vO", vO, hb, h * 128, True)
                    P.op("sp", lambda e, h=h, hb=hb: e.dma_start(out=tbl[:, hb], in_=nbias[ei, h]), writes=[("tbl", hb)], dma=True)
                    okeys = []
                    for r in range(64):
                        rs = min(max(r - 4, 0), 56)
                        delta = rs - r + 7
                        sbk = srot.next()
                        for jc in range(4):
                            k0 = (rs + 2 * jc) * 64
                            P.op("pe", lambda e, jc=jc, k0=k0, r=r, sbk=sbk, hb=hb: e.matmul(
                                ps[sbk][:, jc * 64:(jc + 1) * 64], lhsT=kT[:, hb, k0:k0 + 128], rhs=qT[:, hb, r * 64:(r + 1) * 64],
                                start=True, stop=True), reads=[("qT", hb), ("kT", hb)], writes=[("ps", sbk)])
                        for cc in range(2):
                            P.op("pe", lambda e, cc=cc, r=r, sbk=sbk, hb=hb: e.matmul(
                                ps[sbk][:, 256 + cc * 64:256 + (cc + 1) * 64], lhsT=kT[:, hb, S + cc * 128:S + (cc + 1) * 128],
                                rhs=qT[:, hb, r * 64:(r + 1) * 64], start=True, stop=True),
                                reads=[("qT", hb), ("kT", hb)], writes=[("ps", sbk)])
                        x = xrot.next()
                        P.op("dve", lambda e, x=x, sbk=sbk, hb=hb, delta=delta: e.scalar_tensor_tensor(
                            out=sb[:, x], in0=ps[sbk][:, 0:256].rearrange("p (a b) -> p a b", b=64), scalar=SCALE,
                            in1=tbl[:, hb, delta:delta + 7:2, :], op0=ALU.mult, op1=ALU.add),
                            reads=[("ps", sbk), ("tbl", hb)], writes=[("sb", x)])
                        P.op("act", lambda e, x=x: e.activation(out=pT[:, x, 0:256], in_=sb[:, x].rearrange("p a b -> p (a b)"), func=AF.Exp),
                             reads=[("sb", x)], writes=[("pT", x, 0)])
                        P.op("act", lambda e, x=x, sbk=sbk: e.activation(out=pT[:, x, 256:384], in_=ps[sbk][:, 256:384], func=AF.Exp, scale=SCALE),
                             reads=[("ps", sbk)], writes=[("pT", x, 1)])
                        ob = 4 + orot.next()
                        db = 6 + drot.next()
                        for c in range(6):
                            if c < 4:
                                if rs % 2 == 0:
                                    lh = vE[:, hb, rs // 2 + c, :]
                                    vk = vek
                                else:
                                    lh = vO[:, hb, (rs - 1) // 2 + c, :]
                                    vk = vok
                            else:
                                lh = vE[:, hb, 32 + (c - 4), :]
                                vk = vek
                            P.op("pe", lambda e, c=c, lh=lh, x=x, ob=ob: e.matmul(ps[ob][:, 0:64], lhsT=lh, rhs=pT[:, x, c * 64:(c + 1) * 64],
                                                                               start=(c == 0), stop=(c == 5)),
                                 reads=[("pT", x, 0), ("pT", x, 1)] + vk, writes=[("ps", ob)])
                        for c in range(6):
                            P.op("pe", lambda e, c=c, x=x, db=db: e.matmul(ps[db][:, 0:64], lhsT=onesbf[:], rhs=pT[:, x, c * 64:(c + 1) * 64],
                                                                         start=(c == 0), stop=(c == 5)),
                                 reads=[("pT", x, 0), ("pT", x, 1), "onesbf"], writes=[("ps", db)])
                        P.op("dve", lambda e, x=x, db=db: e.reciprocal(out=rec[:, x, 0:64], in_=ps[db][:, 0:64]), reads=[("ps", db)], writes=[("rec", x)])
                        P.op("dve", lambda e, x=x, ob=ob, hb=hb, r=r: e.tensor_tensor(out=oT[:, hb, r * 64:(r + 1) * 64], in0=ps[ob][:, 0:64], in1=rec[:, x, 0:64], op=ALU.mult),
                             reads=[("ps", ob), ("rec", x)], writes=[("oT", hb, r)])
                        okeys.append(("oT", hb, r))
                    if ctx_out:
                        ctx_attn(hb, hb, hb, vek, None)
                        okeys.append(("oT", hb, "ctx"))
                    P.op("sp", lambda e, h=h, hb=hb: e.dma_start(out=OT[h * 128:(h + 1) * 128, 0:ncols], in_=oT[:, hb, 0:ncols]),
                         reads=okeys, dma=True)

                for g in range(4):
                    gb = g % 2
                    P.op("sp", lambda e, g=g, gb=gb: e.dma_start(out=kT[:, gb, :], in_=PT[8192 + g * 128:8192 + (g + 1) * 128, :]), writes=[("kT", gb)], dma=True)
                    vek = load_v("vE", vE, gb, 2048 + g * 128, False)
                    for hh in range(4):
                        h = 4 * g + hh
                        hb = h % 2
                        P.op("sp", lambda e, h=h, hb=hb: e.dma_start(out=qT[:, hb, :], in_=PT[6144 + h * 128:6144 + (h + 1) * 128, :]), writes=[("qT", hb)], dma=True)
                        okeys = []
                        for n in range(32):
                            pb = srot.next() % 2
                            b0, b1 = 2 * pb, 2 * pb + 1
                            chunks = []
                            if n > 0:
                                chunks.append((b0, 0, n - 1))
                            if n < 31:
                                chunks.append((b0, 128, n + 1))
                            chunks += [(b0, 256, n), (b0, 384, 32), (b1, 0, 33)]
                            for (bk, col, kbk) in chunks:
                                P.op("pe", lambda e, bk=bk, col=col, kbk=kbk, n=n, hb=hb, gb=gb: e.matmul(
                                    ps[bk][:, col:col + 128], lhsT=kT[:, gb, kbk * 128:(kbk + 1) * 128], rhs=qT[:, hb, n * 128:(n + 1) * 128],
                                    start=True, stop=True), reads=[("qT", hb), ("kT", gb)], writes=[("ps", bk)])
                            x = xrot.next()
                            lo = 0 if n > 0 else 128
                            hi = 256 if n < 31 else 128
                            sbf = sb[:, x].rearrange("p a b -> p (a b)")
                            P.op("dve", lambda e, x=x, b0=b0, lo=lo, hi=hi, sbf=sbf: e.scalar_tensor_tensor(
                                out=sbf[:, lo:hi], in0=ps[b0][:, lo:hi], scalar=SCALE, in1=wm[:, lo:hi], op0=ALU.mult, op1=ALU.add),
                                reads=[("ps", b0), "wm"], writes=[("sb", x)])
                            P.op("act", lambda e, x=x, lo=lo, hi=hi, sbf=sbf: e.activation(out=pT[:, x, lo:hi], in_=sbf[:, lo:hi], func=AF.Exp),
                                 reads=[("sb", x)], writes=[("pT", x, 0)])
                            P.op("act", lambda e, x=x, b0=b0: e.activation(out=pT[:, x, 256:512], in_=ps[b0][:, 256:512], func=AF.Exp, scale=SCALE),
                                 reads=[("ps", b0)], writes=[("pT", x, 1)])
                            P.op("act", lambda e, x=x, b1=b1: e.activation(out=pT[:, x, 512:640], in_=ps[b1][:, 0:128], func=AF.Exp, scale=SCALE),
                                 reads=[("ps", b1)], writes=[("pT", x, 2)])
                            ob = 4 + orot.next()
                            db = 6 + drot.next()
                            pcols = [(col if bk == b0 else 512, kbk) for (bk, col, kbk) in chunks]
                            nch = len(pcols)
                            for ci, (pcol, kbk) in enumerate(pcols):
                                P.op("pe", lambda e, ci=ci, pcol=pcol, kbk=kbk, x=x, ob=ob, gb=gb, nch=nch: e.matmul(
                                    ps[ob][:, 0:128], lhsT=vE[:, gb, kbk, :], rhs=pT[:, x, pcol:pcol + 128], start=(ci == 0), stop=(ci == nch - 1)),
                                    reads=[("pT", x, 0), ("pT", x, 1), ("pT", x, 2)] + vek, writes=[("ps", ob)])
                            for ci, (pcol, kbk) in enumerate(pcols):
                                P.op("pe", lambda e, ci=ci, pcol=pcol, x=x, db=db, nch=nch: e.matmul(
                                    ps[db][:, 0:128], lhsT=onesbf[:], rhs=pT[:, x, pcol:pcol + 128], start=(ci == 0), stop=(ci == nch - 1)),
                                    reads=[("pT", x, 0), ("pT", x, 1), ("pT", x, 2), "onesbf"], writes=[("ps", db)])
                            P.op("dve", lambda e, x=x, db=db, h=h: e.tensor_scalar(out=dsb[:, x, 0:128], in0=ps[db][:, 0:128], scalar1=esink[:, ei, h:h + 1], scalar2=None, op0=ALU.add),
                                 reads=[("ps", db), "esink"], writes=[("dsb", x)])
                            P.op("dve", lambda e, x=x: e.reciprocal(out=rec[:, x, 0:128], in_=dsb[:, x, 0:128]), reads=[("dsb", x)], writes=[("rec", x)])
                            P.op("dve", lambda e, x=x, ob=ob, hb=hb, n=n: e.tensor_tensor(out=oT[:, hb, n * 128:(n + 1) * 128], in0=ps[ob][:, 0:128], in1=rec[:, x, 0:128], op=ALU.mult),
                                 reads=[("ps", ob), ("rec", x)], writes=[("oT", hb, n)])
                            okeys.append(("oT", hb, n))
                        if ctx_out:
                            ctx_attn(hb, gb, gb, vek, esink[:, ei, h:h + 1])
                            okeys.append(("oT", hb, "ctx"))
                        P.op("sp", lambda e, h=h, hb=hb: e.dma_start(out=OT[2048 + h * 128:2048 + (h + 1) * 128, 0:ncols], in_=oT[:, hb, 0:ncols]),
                             reads=okeys, dma=True)
                P.emit()

        @_gate
        def phase_fnet(layer, ctx_out):
            ntile = 34 if ctx_out else 32
            ncols = NTOK if ctx_out else S
            sc_main = float((S * 128) ** -0.5)
            sc_ctx = float((L * 128) ** -0.5)
            dcv = dftc.rearrange("(c p) k -> p c k", p=128)
            dsv = dfts.rearrange("(c p) k -> p c k", p=128)
            with contextlib.ExitStack() as es:
                uT = es.enter_context(SBT("uT", [128, 2, NTOK], BF16))
                cd = es.enter_context(SBT("cdm", [128, 256], BF16))
                ucs = es.enter_context(SBT("ucs", [128, 4, 34, 256], BF16))
                dft = es.enter_context(SBT("dft", [128, 2, 2, 32, 256], BF16))
                d256 = es.enter_context(SBT("d256", [128, 2, 2, 256], BF16))
                fo = es.enter_context(SBT("fo", [128, 4, 256], BF16))
                P.op("sp", lambda e: e.dma_start(out=cd[:], in_=cdsd), writes=["cd"], dma=True)
                P.op("sp", lambda e: e.dma_start(out=d256[:], in_=dft256.rearrange("w (c p) k -> p w c k", p=128)), writes=["d256"], dma=True)
                bankrot = Rot(8)
                forot = Rot(4)
                dslot = Rot(2)
                for gb in range(4):
                    for gi in range(4):
                        g = 4 * gb + gi
                        ub = g % 2
                        P.op("sp", lambda e, g=g, ub=ub: e.dma_start(out=uT[:, ub, 0:ncols], in_=PT[g * 128:(g + 1) * 128, 0:ncols]), writes=[("uT", ub)], dma=True)
                        for t in range(ntile):
                            b = bankrot.next()
                            P.op("pe", lambda e, b=b, t=t, ub=ub: e.matmul(ps[b][:, 0:256], lhsT=uT[:, ub, t * 128:(t + 1) * 128], rhs=cd[:, :], start=True, stop=True),
                                 reads=[("uT", ub), "cd"], writes=[("ps", b)])
                            eng = "act" if t % 2 == 0 else "dve"
                            if eng == "act":
                                P.op("act", lambda e, b=b, t=t, gi=gi: e.activation(out=ucs[:, gi, t, :], in_=ps[b][:, 0:256], func=AF.Copy),
                                     reads=[("ps", b)], writes=[("ucs", gi, t)])
                            else:
                                P.op("dve", lambda e, b=b, t=t, gi=gi: e.tensor_copy(out=ucs[:, gi, t, :], in_=ps[b][:, 0:256]),
                                     reads=[("ps", b)], writes=[("ucs", gi, t)])
                    for kt in range(16):
                        sl = dslot.next()
                        for q in range(4):
                            P.op("sp", lambda e, sl=sl, q=q, kt=kt: e.dma_start(out=dft[:, sl, 0, q * 8:(q + 1) * 8, :], in_=dcv[:, q * 8:(q + 1) * 8, kt * 256:(kt + 1) * 256]),
                                 writes=[("dft", sl, 0, q)], dma=True)
                            P.op("sp", lambda e, sl=sl, q=q, kt=kt: e.dma_start(out=dft[:, sl, 1, q * 8:(q + 1) * 8, :], in_=dsv[:, q * 8:(q + 1) * 8, kt * 256:(kt + 1) * 256]),
                                 writes=[("dft", sl, 1, q)], dma=True)
                        for gi in range(4):
                            g = 4 * gb + gi
                            b = bankrot.next()
                            for tc in range(32):
                                P.op("pe", lambda e, b=b, tc=tc, gi=gi, sl=sl: e.matmul(ps[b][:, 0:256], lhsT=ucs[:, gi, tc, 0:128], rhs=dft[:, sl, 0, tc, :], start=(tc == 0), stop=False),
                                     reads=[("ucs", gi, tc), ("dft", sl, 0, tc // 8)], writes=[("ps", b)])
                                P.op("pe", lambda e, b=b, tc=tc, gi=gi, sl=sl: e.matmul(ps[b][:, 0:256], lhsT=ucs[:, gi, tc, 128:256], rhs=dft[:, sl, 1, tc, :], start=False, stop=(tc == 31)),
                                     reads=[("ucs", gi, tc), ("dft", sl, 1, tc // 8)], writes=[("ps", b)])
                            o = forot.next()
                            P.op("act", lambda e, b=b, o=o: e.activation(out=fo[:, o, :], in_=ps[b][:, 0:256], func=AF.Copy, scale=sc_main),
                                 reads=[("ps", b)], writes=[("fo", o)])
                            P.op("act", lambda e, o=o, g=g, kt=kt: e.dma_start(out=OT[g * 128:(g + 1) * 128, kt * 256:(kt + 1) * 256], in_=fo[:, o, :]),
                                 reads=[("fo", o)], dma=True)
                    if ctx_out:
                        for gi in range(4):
                            g = 4 * gb + gi
                            b = bankrot.next()
                            for tc in range(2):
                                P.op("pe", lambda e, b=b, tc=tc, gi=gi: e.matmul(ps[b][:, 0:256], lhsT=ucs[:, gi, 32 + tc, 0:128], rhs=d256[:, 0, tc, :], start=(tc == 0), stop=False),
                                     reads=[("ucs", gi, 32 + tc), "d256"], writes=[("ps", b)])
                                P.op("pe", lambda e, b=b, tc=tc, gi=gi: e.matmul(ps[b][:, 0:256], lhsT=ucs[:, gi, 32 + tc, 128:256], rhs=d256[:, 1, tc, :], start=False, stop=(tc == 1)),
                                     reads=[("ucs", gi, 32 + tc), "d256"], writes=[("ps", b)])
                            o = forot.next()
                            P.op("act", lambda e, b=b, o=o: e.activation(out=fo[:, o, :], in_=ps[b][:, 0:256], func=AF.Copy, scale=sc_ctx),
                                 reads=[("ps", b)], writes=[("fo", o)])
                            P.op("sp", lambda e, o=o, g=g: e.dma_start(out=OT[g * 128:(g + 1) * 128, S:NTOK], in_=fo[:, o, :]),
                                 reads=[("fo", o)], dma=True)
                P.emit()

        @_gate
        def phase_sconv(layer, ctx_out):
            oi = layer // 2
            ncols = NTOK if ctx_out else S
            MW = NTOK + 6
            segs = [(1, 0, S)]
            if ctx_out:
                segs.append((S + 3, S, L))
            with contextlib.ExitStack() as es:
                bgT = es.enter_context(SBT("bgT", [128, 2, NTOK], BF16))
                cgT = es.enter_context(SBT("cgT", [128, 2, NTOK], BF16))
                hvT = es.enter_context(SBT("hvT", [128, 2, NTOK], BF16))
                mm = es.enter_context(SBT("mm", [128, 2, MW], F32))
                a0 = es.enter_context(SBT("a0", [128, NTOK], F32))
                a1 = es.enter_context(SBT("a1", [128, NTOK], F32))
                so = es.enter_context(SBT("so", [128, 2, NTOK], BF16))
                for s_ in range(2):
                    P.op("pool", lambda e, s_=s_: e.memset(mm[:, s_, :], 0.0), writes=[("mm", s_)])
                for c in range(16):
                    sl = c % 2
                    for (buf, key, row0) in ((bgT, "bg", 2048), (cgT, "cg", 4096), (hvT, "hv", 6144)):
                        P.op("sp", lambda e, buf=buf, row0=row0, c=c, sl=sl: e.dma_start(out=buf[:, sl, 0:ncols], in_=PT[row0 + c * 128:row0 + (c + 1) * 128, 0:ncols]),
                             writes=[(key, sl)], dma=True)
                    for (mo, to, n) in segs:
                        P.op("dve", lambda e, sl=sl, mo=mo, to=to, n=n: e.tensor_tensor(out=mm[:, sl, mo:mo + n], in0=cgT[:, sl, to:to + n], in1=hvT[:, sl, to:to + n], op=ALU.mult),
                             reads=[("cg", sl), ("hv", sl)], writes=[("mm", sl)])
                    for (mo, to, n) in segs:
                        P.op("act", lambda e, sl=sl, mo=mo, to=to, n=n, c=c: e.activation(out=a0[:, to:to + n], in_=mm[:, sl, mo:mo + n], func=AF.Identity, scale=ocw[:, oi, c, 1:2]),
                             reads=[("mm", sl)], writes=[("a0", to)])
                        P.op("dve", lambda e, sl=sl, mo=mo, to=to, n=n, c=c: e.scalar_tensor_tensor(out=a1[:, to:to + n], in0=mm[:, sl, mo - 1:mo - 1 + n], scalar=ocw[:, oi, c, 0:1], in1=a0[:, to:to + n], op0=ALU.mult, op1=ALU.add),
                             reads=[("mm", sl), ("a0", to)], writes=[("a1", to)])
                        P.op("dve", lambda e, sl=sl, mo=mo, to=to, n=n, c=c: e.scalar_tensor_tensor(out=a0[:, to:to + n], in0=mm[:, sl, mo + 1:mo + 1 + n], scalar=ocw[:, oi, c, 2:3], in1=a1[:, to:to + n], op0=ALU.mult, op1=ALU.add),
                             reads=[("mm", sl), ("a1", to)], writes=[("a0", to)])
                        P.op("dve", lambda e, sl=sl, to=to, n=n: e.tensor_tensor(out=so[:, sl, to:to + n], in0=a0[:, to:to + n], in1=bgT[:, sl, to:to + n], op=ALU.mult),
                             reads=[("a0", to), ("bg", sl)], writes=[("so", sl, to)])
                    P.op("sp", lambda e, sl=sl, c=c: e.dma_start(out=OT[2048 + c * 128:2048 + (c + 1) * 128, 0:ncols], in_=so[:, sl, 0:ncols]),
                         reads=[("so", sl, to) for (mo, to, n) in segs], dma=True)
                P.emit()

        Xa = xin
        for layer in range(N_LAYERS_RUN):
            ctx_live = any(j % 2 == 0 for j in range(layer + 1, DEPTH))
            need_hc = (layer % 2 == 0) or ctx_live
            phase_s1(layer, Xa, need_hc)
            if layer % 2 == 0:
                phase_even(layer, ctx_live)
            else:
                phase_fnet(layer, ctx_live)
                phase_sconv(layer, ctx_live)
            phase_s2(layer, Xa, xB, ctx_live)
            phase_s3(layer, xB, ctx_live)
            Xn = yout if layer == N_LAYERS_RUN - 1 else xC
            phase_s4(layer, xB, Xn, ctx_live and (layer != N_LAYERS_RUN - 1))
            Xa = xC
        P.emit(final=True)
        nops = P.nops
    return nc, nops


N_LAYERS_RUN = DEPTH
_NC = [NCORES]
_RUNKW = {}
_LAST = [None]
_DEBUG = [False]
_STOP = [None]


def _host_consts():
    t = np.arange(S)
    row = (t // 64).astype(np.float32)
    col = (t % 64).astype(np.float32)
    nf = 32
    inv = (10000.0 ** (-np.arange(nf, dtype=np.float32) / nf)).astype(np.float32)
    ropec = np.zeros((128, S), np.float32)
    ropes = np.zeros((128, S), np.float32)
    rotm = np.zeros((128, 128), np.float32)
    for a, pos in enumerate((row, col)):
        ang = pos[None, :] * inv[:, None]
        for j in range(2):
            d0 = a * 64 + j * 32
            ropec[d0:d0 + 32] = np.cos(ang)
            ropes[d0:d0 + 32] = -np.sin(ang) if j == 0 else np.sin(ang)
        for f in range(nf):
            rotm[a * 64 + f, a * 64 + 32 + f] = 1.0
            rotm[a * 64 + 32 + f, a * 64 + f] = 1.0
    kl = np.arange(128)[:, None]
    ql = np.arange(128)[None, :]
    wmask = np.zeros((128, 256), np.float32)
    wmask[:, 0:128] = np.where(kl >= ql, 0.0, NEG)
    wmask[:, 128:256] = np.where(kl <= ql, 0.0, NEG)
    d = np.arange(128)
    angd = 2 * np.pi * np.outer(d, d) / 128.0
    cdsd = np.concatenate([np.cos(angd), np.sin(angd)], axis=1).astype(ml_dtypes.bfloat16)
    tt = np.arange(S, dtype=np.int64)
    prod = (np.outer(tt, tt) % S).astype(np.float64) * (2 * np.pi / S)
    dftc = np.cos(prod).astype(ml_dtypes.bfloat16)
    dfts = (-np.sin(prod)).astype(ml_dtypes.bfloat16)
    t2 = np.arange(L, dtype=np.int64)
    p2 = (np.outer(t2, t2) % L).astype(np.float64) * (2 * np.pi / L)
    dft256 = np.stack([np.cos(p2), -np.sin(p2)]).astype(ml_dtypes.bfloat16)
    return dict(ropec=ropec, ropes=ropes, rotm=rotm, wmask=wmask, cdsd=cdsd, dftc=dftc, dfts=dfts, dft256=dft256)


def _nbias_table(rpb):
    E = rpb.shape[0]
    kc = np.arange(64)[:, None]
    c = np.arange(64)[None, :]
    cstart = np.clip(c - 8, 0, 48)
    ok = (kc >= cstart) & (kc < cstart + 16)
    rel = np.clip(kc - c, -15, 15) + 15
    out = np.full((E, 16, 128, 14, 64), NEG, np.float32)
    for jpar in range(2):
        for m in range(14):
            rr = m + jpar
            if rr > 14:
                continue
            g = rpb[:, :, rr, :][:, :, rel]
            out[:, :, jpar * 64:(jpar + 1) * 64, m, :] = np.where(ok[None, None], g, np.float32(NEG))
    return out


def kernel(x, c, ctx, c_ctx, w_mod, b_mod, g_mix_pre, g_mix_post, g_ffn_pre, g_ffn_post,
           even_w_in, even_rpb, even_sink, even_w_out, odd_w_in, odd_conv, odd_w_out,
           ffn_w_up, ffn_conv, ffn_w_down):
    f32 = np.float32
    x = np.asarray(x, f32); c = np.asarray(c, f32); ctx = np.asarray(ctx, f32); c_ctx = np.asarray(c_ctx, f32)
    nc, _ = build_program()
    consts = _host_consts()

    def pl(v, n):
        v = np.asarray(v, f32)
        lead = v.shape[:-1]
        return np.ascontiguousarray(np.moveaxis(v.reshape(lead + (n, 128)), -1, 0))

    shared = dict(
        w_mod=np.asarray(w_mod, f32),
        bmod=pl(b_mod, 192),
        gvec=np.ascontiguousarray(np.stack([pl(g_mix_pre, KC), pl(g_mix_post, KC), pl(g_ffn_pre, KC), pl(g_ffn_post, KC)], axis=1)),
        even_w_in=np.asarray(even_w_in, f32), even_w_out=np.asarray(even_w_out, f32),
        odd_w_in=np.asarray(odd_w_in, f32), odd_w_out=np.asarray(odd_w_out, f32),
        ffn_w_up=np.asarray(ffn_w_up, f32), ffn_w_down=np.asarray(ffn_w_down, f32),
        nbias=_nbias_table(np.asarray(even_rpb, f32)),
        sinkb=np.ascontiguousarray(np.broadcast_to(np.asarray(even_sink, f32)[None], (128, 2, 16))),
        oconv=np.ascontiguousarray(np.transpose(np.asarray(odd_conv, f32).reshape(2, 3, 16, 128), (3, 0, 2, 1))),
        fconv=np.ascontiguousarray(np.transpose(np.asarray(ffn_conv, f32).reshape(DEPTH, 3, 88, 128), (3, 0, 2, 1))),
        **consts,
    )
    in_maps = []
    for b in range(_NC[0]):
        xin = np.ascontiguousarray(np.concatenate([x[b], ctx[b]], axis=0).T)
        cv = np.ascontiguousarray(np.stack([c[b].reshape(KC, 128).T, c_ctx.reshape(KC, 128).T], axis=-1))
        m = dict(shared)
        m["xin"] = xin
        m["cvec"] = cv
        in_maps.append(m)
    res = run_bass_kernel_spmd(nc, in_maps, core_ids=list(range(_NC[0])), **_RUNKW)
    _LAST[0] = res
    out = np.stack([np.ascontiguousarray(res.results[b]["yout"].T) for b in range(_NC[0])], axis=0)
    return out.astype(f32)
```

```python
import contextlib
import numpy as np
import ml_dtypes
import concourse.bass as bass
import concourse.mybir as mybir
from concourse.bass_utils import run_bass_kernel_spmd

F32 = mybir.dt.float32
BF16 = mybir.dt.bfloat16
AF = mybir.ActivationFunctionType
ALU = mybir.AluOpType

D = 4096
S = 4096
L = 256
NTOK = S + L
DEPTH = 4
KC = 32
HID = 5632
NCORES = 4
NEG = -30000.0
SCALE = 128 ** -0.5
EPS = 1e-6

ENGS = ("pe", "act", "dve", "pool", "sp")


class Prog:
    ND = 40

    def __init__(self, nc, es):
        self.nc = nc
        self.esem = {e: es.enter_context(nc.semaphore("c_" + e)) for e in ENGS}
        self.ecnt = {e: 0 for e in ENGS}
        self.dsem = [es.enter_context(nc.semaphore("d%d" % i)) for i in range(self.ND)]
        self.dcnt = [0] * self.ND
        self.dnext = 0
        self.waited = {e: {} for e in ENGS}
        self.nops = 0
        self._reset()

    def _reset(self):
        self.streams = {e: [] for e in ENGS}
        self.lastw = {}
        self.readers = {}

    def op(self, eng, fn, reads=(), writes=(), dma=False):
        deps = []
        for k in reads:
            t = self.lastw.get(k)
            if t is not None:
                deps.append(t)
        for k in writes:
            t = self.lastw.get(k)
            if t is not None:
                deps.append(t)
            r = self.readers.get(k)
            if r:
                deps.extend(r.values())
        if dma:
            s = self.dnext
            self.dnext = (self.dnext + 1) % self.ND
            if self.dcnt[s] > 0:
                deps.append(("d%d" % s, self.dsem[s], self.dcnt[s]))
            self.dcnt[s] += 16
            tok = ("d%d" % s, self.dsem[s], self.dcnt[s])
            inc = (self.dsem[s], 16)
        else:
            self.ecnt[eng] += 1
            tok = (eng, self.esem[eng], self.ecnt[eng])
            inc = (self.esem[eng], 1)
        need = {}
        w = self.waited[eng]
        for (sid, sem, val) in deps:
            if sid == "pe" and eng == "pe" and not dma:
                continue
            if w.get(sid, 0) >= val:
                continue
            if sid not in need or need[sid][1] < val:
                need[sid] = (sem, val)
        for sid, (sem, val) in need.items():
            w[sid] = val
        self.streams[eng].append((list(need.values()), fn, inc))
        for k in writes:
            self.lastw[k] = tok
            self.readers[k] = {}
        for k in reads:
            if k in writes:
                continue
            r = self.readers.setdefault(k, {})
            old = r.get(tok[0])
            if old is None or old[2] < tok[2]:
                r[tok[0]] = tok
        self.nops += 1
        return tok

    def emit(self, final=False):
        nc = self.nc
        pre = self.pre if hasattr(self, "pre") else []
        fin = []
        for s in range(self.ND):
            if self.dcnt[s] > 0:
                fin.append(("d%d" % s, self.dsem[s], self.dcnt[s]))
        for e in ENGS:
            if self.ecnt[e] > 0:
                fin.append((e, self.esem[e], self.ecnt[e]))
        streams = self.streams

        def run(e, name):
            for sem, val in pre:
                e.wait_ge(sem, val)
            for waits, fn, inc in streams[name]:
                for sem, val in waits:
                    e.wait_ge(sem, val)
                ins = fn(e)
                ins.then_inc(inc[0], inc[1])
            if final and name == "sp":
                for sid, sem, val in fin:
                    e.wait_ge(sem, val)

        with nc.Block() as block:
            @block.tensor
            def _(e):
                run(e, "pe")

            @block.scalar
            def _(e):
                run(e, "act")

            @block.vector
            def _(e):
                run(e, "dve")

            @block.gpsimd
            def _(e):
                run(e, "pool")

            @block.sync
            def _(e):
                run(e, "sp")
        self.pre = [(sem, val) for (sid, sem, val) in fin]
        for e in ENGS:
            for sid, sem, val in fin:
                self.waited[e][sid] = val
        self._reset()


class Rot:
    def __init__(self, n):
        self.n = n
        self.i = 0

    def next(self):
        v = self.i
        self.i = (self.i + 1) % self.n
        return v


def gemm(P, ps, wb, nslot, slotrot, W, kcn, groups, acts, epi, bankrot, MG=256, tag="w", wq=("pool",)):
    Wv = W.rearrange("(c p) m -> p c m", p=128)
    nq = 4
    qs = [(q * kcn // nq, (q + 1) * kcn // nq) for q in range(nq)]

    def qof(c):
        for qi, (a, b) in enumerate(qs):
            if a <= c < b:
                return qi

    for gi, (col0, mode) in enumerate(groups):
        slot = slotrot.next()
        for qi, (a, b) in enumerate(qs):
            P.op(wq[qi % len(wq)], lambda e, slot=slot, a=a, b=b, col0=col0: e.dma_start(
                out=wb[:, slot, a:b, :], in_=Wv[:, a:b, col0:col0 + MG]),
                writes=[(tag, slot, qi)], dma=True)
        if mode == "f":
            for m in range(MG // 128):
                for j, A in enumerate(acts):
                    b = bankrot.next()
                    T = A["T"]
                    for c in range(kcn):
                        P.op("pe", lambda e, b=b, slot=slot, c=c, m=m, A=A, T=T: e.matmul(
                            ps[b][:, :T], lhsT=wb[:, slot, c, m * 128:(m + 1) * 128], rhs=A["ap"][:, c, :T],
                            start=(c == 0), stop=(c == kcn - 1)),
                            reads=[(tag, slot, qof(c)), A["key"] + (c,)], writes=[("ps", b)])
                    epi(col0 // 128 + m, j, A, b)
        else:
            for j, A in enumerate(acts):
                T = A["T"]
                for s in range(T // 128):
                    b = bankrot.next()
                    for c in range(kcn):
                        P.op("pe", lambda e, b=b, slot=slot, c=c, s=s, A=A: e.matmul(
                            ps[b][:, :MG], lhsT=A["ap"][:, c, s * 128:(s + 1) * 128], rhs=wb[:, slot, c, :],
                            start=(c == 0), stop=(c == kcn - 1)),
                            reads=[(tag, slot, qof(c)), A["key"] + (c,)], writes=[("ps", b)])
                    epi(col0, j, A, b, s)


def build_program():
    nc = bass.Bass("TRN2", target_bir_lowering=False)

    _uid = [0]

    def SBT(name, shape, dt):
        _uid[0] += 1
        return nc.sbuf_tensor("%s_%d" % (name, _uid[0]), shape, dt)

    def din(name, shape, dt=F32):
        return nc.dram_tensor(name, list(shape), dt, kind="ExternalInput").ap()

    def dscr(name, shape, dt):
        if _DEBUG[0]:
            return nc.dram_tensor(name, list(shape), dt, kind="ExternalOutput").ap()
        return nc.dram_tensor(name, list(shape), dt).ap()

    xin = din("xin", [D, NTOK])
    cvec = din("cvec", [128, KC, 2])
    w_mod = din("w_mod", [DEPTH, D, 6 * D])
    bmod = din("bmod", [128, DEPTH, 192])
    gvec = din("gvec", [128, 4, DEPTH, KC])
    even_w_in = din("even_w_in", [2, D, 9216])
    even_w_out = din("even_w_out", [2, D, D])
    odd_w_in = din("odd_w_in", [2, D, 8192])
    odd_w_out = din("odd_w_out", [2, D, D])
    ffn_w_up = din("ffn_w_up", [DEPTH, D, 2 * HID])
    ffn_w_down = din("ffn_w_down", [DEPTH, HID, D])
    nbias = din("nbias", [2, 16, 128, 14, 64])
    sinkb = din("sinkb", [128, 2, 16])
    oconv = din("oconv", [128, 2, 16, 3])
    fconv = din("fconv", [128, DEPTH, 88, 3])
    ropec = din("ropec", [128, S])
    ropes = din("ropes", [128, S])
    rotm = din("rotm", [128, 128])
    wmask = din("wmask", [128, 256])
    cdsd = din("cdsd", [128, 256], BF16)
    dftc = din("dftc", [S, S], BF16)
    dfts = din("dfts", [S, S], BF16)
    dft256 = din("dft256", [2, L, L], BF16)
    yout = nc.dram_tensor("yout", [D, S], F32, kind="ExternalOutput").ap()

    xB = dscr("xB", [D, NTOK], F32)
    xC = dscr("xC", [D, NTOK], F32)
    PT = dscr("PT", [9216, NTOK], BF16)
    VT = dscr("VT", [NTOK, 2560], BF16)
    OT = dscr("OT", [D, NTOK], BF16)
    YT = dscr("YT", [D, NTOK], F32)
    HPW = NTOK + 4
    HP = dscr("HP", [2 * HID, HPW], F32)

    def hpcol(tok0):
        return tok0 + 1 if tok0 < S else tok0 + 3

    MAIN_TILES = [(512 * j, 512, 0) for j in range(8)]
    CTX_TILE = (S, L, 1)

    def tile_groups(with_ctx, per=2):
        g = [list(MAIN_TILES[i:i + per]) for i in range(0, 8, per)]
        if with_ctx:
            if per == 2:
                g[-1].append(CTX_TILE)
            else:
                g.append([CTX_TILE])
        return g

    with contextlib.ExitStack() as ges:
        P = Prog(nc, ges)
        ps = [ges.enter_context(nc.psum_tensor("ps%d" % i, [128, 512], F32)) for i in range(8)]
        ones32 = ges.enter_context(SBT("ones32", [128, 128], F32))
        onesbf = ges.enter_context(SBT("onesbf", [128, 128], BF16))
        tab = ges.enter_context(SBT("tab", [128, DEPTH, 6, 2, KC], F32))
        fcw = ges.enter_context(SBT("fcw", [128, DEPTH, 88, 3], F32))
        ocw = ges.enter_context(SBT("ocw", [128, 2, 16, 3], F32))
        esink = ges.enter_context(SBT("esink", [128, 2, 16], F32))
        epsb = ges.enter_context(SBT("epsb", [128, 1], F32))

        with contextlib.ExitStack() as es:
            cv = es.enter_context(SBT("cv", [128, KC, 2], F32))
            cact = es.enter_context(SBT("cact", [128, KC, 2], BF16))
            bm = es.enter_context(SBT("bm", [128, DEPTH, 192], F32))
            gv = es.enter_context(SBT("gv", [128, 4, DEPTH, KC], F32))
            modt = es.enter_context(SBT("modt", [128, DEPTH, 192, 2], F32))
            zt = es.enter_context(SBT("zt", [128, 88, 1], F32))
            wbm = es.enter_context(SBT("wbm", [128, 3, KC, 256], BF16))
            P.op("dve", lambda e: e.memset(ones32[:], 1.0), writes=["ones32"])
            P.op("dve", lambda e: e.memset(onesbf[:], 1.0), writes=["onesbf"])
            P.op("dve", lambda e: e.memset(epsb[:], EPS), writes=["epsb"])
            P.op("dve", lambda e: e.memset(zt[:], 0.0), writes=["zt"])
            HPv = HP.rearrange("(c p) t -> p c t", p=128)
            for pc in (0, S + 1, S + 2, NTOK + 3):
                P.op("sp", lambda e, pc=pc: e.dma_start(out=HPv[:, :, pc:pc + 1], in_=zt[:], allow_slow_non_contiguous=True),
                     reads=["zt"], dma=True)
            P.op("sp", lambda e: e.dma_start(out=cv[:], in_=cvec), writes=["cv"], dma=True)
            P.op("sp", lambda e: e.dma_start(out=bm[:], in_=bmod), writes=["bm"], dma=True)
            P.op("sp", lambda e: e.dma_start(out=gv[:], in_=gvec), writes=["gv"], dma=True)
            P.op("sp", lambda e: e.dma_start(out=fcw[:], in_=fconv), writes=["fcw"], dma=True)
            P.op("sp", lambda e: e.dma_start(out=ocw[:], in_=oconv), writes=["ocw"], dma=True)
            P.op("sp", lambda e: e.dma_start(out=esink[:], in_=sinkb), writes=["esink"], dma=True)
            P.op("act", lambda e: e.activation(out=esink[:], in_=esink[:], func=AF.Exp), writes=["esink"])
            P.op("act", lambda e: e.activation(out=cact[:], in_=cv[:], func=AF.Silu), reads=["cv"],
                 writes=[("cact", c) for c in range(KC)])
            bankrot = Rot(6)
            slotrot = Rot(3)
            for i in range(DEPTH):
                def epi(m, j, A, b, i=i):
                    P.op("dve", lambda e, m=m, b=b, i=i: e.tensor_scalar(
                        out=modt[:, i, m, :], in0=ps[b][:, 0:2], scalar1=bm[:, i, m:m + 1], scalar2=None, op0=ALU.add),
                        reads=[("ps", b), "bm"], writes=[("modt", i, m)])
                groups = [(c0, "f") for c0 in range(0, 6 * D, 256)]
                gemm(P, ps, wbm, 3, slotrot, w_mod[i], KC, groups, [dict(ap=cact, T=2, key=("cact",))], epi, bankrot, tag="wm")
                allm = [("modt", i, m) for m in range(192)]
                for w in range(2):
                    def mk(kind, mlo, gidx, mode, i=i, w=w):
                        if mode == "a":
                            P.op("dve", lambda e: e.scalar_tensor_tensor(
                                out=tab[:, i, kind, w, :], in0=modt[:, i, mlo:mlo + 32, w], scalar=1.0, in1=gv[:, gidx, i, :],
                                op0=ALU.add, op1=ALU.mult), reads=allm + ["gv"], writes=[("tab", i, kind, w)])
                        elif mode == "c":
                            P.op("dve", lambda e: e.tensor_tensor(
                                out=tab[:, i, kind, w, :], in0=modt[:, i, mlo:mlo + 32, w], in1=gv[:, gidx, i, :], op=ALU.mult),
                                reads=allm + ["gv"], writes=[("tab", i, kind, w)])
                        else:
                            P.op("dve", lambda e: e.tensor_copy(out=tab[:, i, kind, w, :], in_=modt[:, i, mlo:mlo + 32, w]),
                                 reads=allm, writes=[("tab", i, kind, w)])
                    mk(0, 32, 0, "a")
                    mk(1, 0, 0, "b")
                    mk(2, 64, 1, "c")
                    mk(3, 128, 2, "a")
                    mk(4, 96, 0, "b")
                    mk(5, 160, 3, "c")
            P.emit()

        _ph = [0]

        def _gate(f):
            def g(*a, **k):
                _ph[0] += 1
                if _STOP[0] is not None and _ph[0] > _STOP[0]:
                    return
                return f(*a, **k)
            return g

        def alloc_norm_bufs(es):
            xs = es.enter_context(SBT("xs", [128, 4, 512], F32))
            sqs = es.enter_context(SBT("sqs", [128, 4, 512], F32))
            rstd = es.enter_context(SBT("rstd", [128, 3, 512], F32))
            tmpn = es.enter_context(SBT("tmpn", [128, 4, 512], F32))
            return xs, sqs, rstd, tmpn, Rot(4)

        def rstd_from_ps(bank, rstd, j, T):
            P.op("act", lambda e: e.activation(out=rstd[:, j, :T], in_=ps[bank][:, :T], func=AF.Sqrt,
                                               bias=epsb[:, 0:1], scale=1.0 / D),
                 reads=[("ps", bank)], writes=[("rstd", j)])
            P.op("dve", lambda e: e.reciprocal(out=rstd[:, j, :T], in_=rstd[:, j, :T]), writes=[("rstd", j)])

        def norm_prologue(bufs, Xsrc, tiles, layer, kA, kB, act):
            xs, sqs, rstd, tmpn, xrot = bufs
            acts = []
            for j, (tok0, T, w) in enumerate(tiles):
                for c in range(KC):
                    sl = xrot.next()
                    P.op("sp", lambda e, sl=sl, c=c, tok0=tok0, T=T: e.dma_start(
                        out=xs[:, sl, :T], in_=Xsrc[c * 128:(c + 1) * 128, tok0:tok0 + T]), writes=[("xs", sl)], dma=True)
                    P.op("act", lambda e, sl=sl, T=T: e.activation(out=sqs[:, sl, :T], in_=xs[:, sl, :T], func=AF.Square),
                         reads=[("xs", sl)], writes=[("sqs", sl)])
                    P.op("pe", lambda e, sl=sl, T=T, c=c: e.matmul(ps[6][:, :T], lhsT=ones32[:], rhs=sqs[:, sl, :T],
                                                                  start=(c == 0), stop=(c == KC - 1)),
                         reads=[("sqs", sl), "ones32"], writes=[("ps", 6)])
                rstd_from_ps(6, rstd, j, T)
                for c in range(KC):
                    sl = xrot.next()
                    P.op("sp", lambda e, sl=sl, c=c, tok0=tok0, T=T: e.dma_start(
                        out=xs[:, sl, :T], in_=Xsrc[c * 128:(c + 1) * 128, tok0:tok0 + T]), writes=[("xs", sl)], dma=True)
                    P.op("dve", lambda e, sl=sl, j=j, T=T: e.tensor_tensor(out=tmpn[:, sl, :T], in0=xs[:, sl, :T], in1=rstd[:, j, :T], op=ALU.mult),
                         reads=[("xs", sl), ("rstd", j)], writes=[("tmpn", sl)])
                    P.op("act", lambda e, sl=sl, j=j, c=c, T=T, w=w: e.activation(
                        out=act[j][:, c, :T], in_=tmpn[:, sl, :T], func=AF.Identity,
                        bias=tab[:, layer, kB, w, c:c + 1], scale=tab[:, layer, kA, w, c:c + 1]),
                        reads=[("tmpn", sl)], writes=[("act", j, c)])
                acts.append(dict(ap=act[j], T=T, key=("act", j), tok0=tok0, w=w))
            return acts

        class Resid:
            def __init__(self, es, nslot=2, sbase=6):
                self.n = nslot
                self.sbase = sbase
                self.ysb = es.enter_context(SBT("r_ysb", [128, nslot, 512], F32))
                self.sq = es.enter_context(SBT("r_sq", [128, nslot, 512], F32))
                self.rstd = es.enter_context(SBT("r_rstd", [128, 3, 512], F32))
                self.ys = es.enter_context(SBT("r_ys", [128, nslot, 512], F32))
                self.xs = es.enter_context(SBT("r_xs", [128, nslot, 512], F32))
                self.xo = es.enter_context(SBT("r_xo", [128, nslot, 512], F32))
                self.tm = es.enter_context(SBT("r_tm", [128, nslot, 512], F32))
                self.yrot = Rot(nslot)
                self.srot = Rot(nslot)
                self.prot = Rot(nslot)
                self.pending = []

            def flush(self, keep=0):
                while len(self.pending) > keep:
                    self.pending.pop(0)()

            def epi(self, m, j, A, b):
                T = A["T"]
                tok0 = A["tok0"]
                self.flush(0)
                o = self.yrot.next()
                s = self.srot.next()
                ysb, sq = self.ysb, self.sq
                P.op("act", lambda e: e.activation(out=ysb[:, o, :T], in_=ps[b][:, :T], func=AF.Copy),
                     reads=[("ps", b)], writes=[("r_ysb", o)])
                P.op("act", lambda e: e.activation(out=sq[:, s, :T], in_=ps[b][:, :T], func=AF.Square),
                     reads=[("ps", b)], writes=[("r_sq", s)])
                P.op("act", lambda e: e.dma_start(out=YT[m * 128:(m + 1) * 128, tok0:tok0 + T], in_=ysb[:, o, :T]),
                     reads=[("r_ysb", o)], writes=[("YT", m, j)], dma=True)

                sb_ = self.sbase + j

                def stat():
                    P.op("pe", lambda e: e.matmul(ps[sb_][:, :T], lhsT=ones32[:], rhs=sq[:, s, :T],
                                                  start=(m == 0), stop=(m == KC - 1)),
                         reads=[("r_sq", s), "ones32"], writes=[("ps", sb_)])
                self.pending.append(stat)

            def post(self, acts, Xprev, Xnext, layer, kC):
                self.flush(0)
                for j, A in enumerate(acts):
                    self._post_tile(j, A, Xprev, Xnext, layer, kC)

            def _post_tile(self, j, A, Xprev, Xnext, layer, kC):
                if True:
                    T = A["T"]
                    tok0 = A["tok0"]
                    w = A["w"]
                    rstd_from_ps(self.sbase + j, self.rstd, j, T)
                    for c in range(KC):
                        p = self.prot.next()
                        ys, xs, xo, tm, rstd = self.ys, self.xs, self.xo, self.tm, self.rstd
                        P.op("sp", lambda e, p=p, c=c: e.dma_start(out=ys[:, p, :T], in_=YT[c * 128:(c + 1) * 128, tok0:tok0 + T]),
                             reads=[("YT", c, j)], writes=[("r_ys", p)], dma=True)
                        P.op("sp", lambda e, p=p, c=c: e.dma_start(out=xs[:, p, :T], in_=Xprev[c * 128:(c + 1) * 128, tok0:tok0 + T]),
                             writes=[("r_xs", p)], dma=True)
                        P.op("dve", lambda e, p=p: e.tensor_tensor(out=tm[:, p, :T], in0=ys[:, p, :T], in1=rstd[:, j, :T], op=ALU.mult),
                             reads=[("r_ys", p), ("rstd", j)], writes=[("r_tm", p)])
                        P.op("dve", lambda e, p=p, c=c: e.scalar_tensor_tensor(
                            out=xo[:, p, :T], in0=tm[:, p, :T], scalar=tab[:, layer, kC, w, c:c + 1], in1=xs[:, p, :T],
                            op0=ALU.mult, op1=ALU.add), reads=[("r_tm", p), ("r_xs", p)], writes=[("r_xo", p)])
                        P.op("sp", lambda e, p=p, c=c: e.dma_start(out=Xnext[c * 128:(c + 1) * 128, tok0:tok0 + T], in_=xo[:, p, :T]),
                             reads=[("r_xo", p)], dma=True)

        @_gate
        def phase_s1(layer, Xsrc, with_ctx):
            even = layer % 2 == 0
            W = (even_w_in if even else odd_w_in)[layer // 2]
            with contextlib.ExitStack() as es:
                bufs = alloc_norm_bufs(es)
                act_ = es.enter_context(SBT("act", [128, 2, KC, 512], BF16))
                actc = es.enter_context(SBT("actc", [128, KC, 256], BF16))
                act = [act_[:, 0], act_[:, 1], actc]
                wb = es.enter_context(SBT("wb", [128, 3, KC, 256], BF16))
                obf = es.enter_context(SBT("obf", [128, 4, 512], BF16))
                q32 = es.enter_context(SBT("q32", [128, 2, 512], F32))
                rt1 = es.enter_context(SBT("rt1", [128, 2, 512], F32))
                rt2 = es.enter_context(SBT("rt2", [128, 2, 512], F32))
                rc = es.enter_context(SBT("rc", [128, 2, 512], F32))
                rs_ = es.enter_context(SBT("rs", [128, 2, 512], F32))
                rm = es.enter_context(SBT("rm", [128, 128], F32))
                P.op("sp", lambda e: e.dma_start(out=rm[:], in_=rotm), writes=["rm"], dma=True)
                bankrot = Rot(6)
                slotrot = Rot(3)
                orot = Rot(4)
                qrot = Rot(2)
                if even:
                    groups = []
                    for c0 in range(0, 9216, 256):
                        m = c0 // 128
                        mode = "t" if (32 <= m < 48 or m >= 68) else "f"
                        groups.append((c0, mode))
                else:
                    groups = [(c0, "f") for c0 in range(0, 8192, 256)]
                for tiles in tile_groups(with_ctx):
                    acts = norm_prologue(bufs, Xsrc, tiles, layer, 0, 1, act)
                    if even:
                        for j, A in enumerate(acts):
                            if A["w"] == 0:
                                P.op("sp", lambda e, j=j, A=A: e.dma_start(out=rc[:, j, :], in_=ropec[:, A["tok0"]:A["tok0"] + 512]),
                                     writes=[("rc", j)], dma=True)
                                P.op("sp", lambda e, j=j, A=A: e.dma_start(out=rs_[:, j, :], in_=ropes[:, A["tok0"]:A["tok0"] + 512]),
                                     writes=[("rs", j)], dma=True)

                    def epi(m, j, A, b, s=None):
                        T = A["T"]
                        tok0 = A["tok0"]
                        if s is not None:
                            vcol = (m - 4096) if m < 8192 else (m - 8704 + 2048)
                            o = orot.next()
                            P.op("act", lambda e: e.activation(out=obf[:, o, :256], in_=ps[b][:, :256], func=AF.Copy),
                                 reads=[("ps", b)], writes=[("obf", o)])
                            P.op("act", lambda e: e.dma_start(
                                out=VT[tok0 + s * 128: tok0 + (s + 1) * 128, vcol:vcol + 256], in_=obf[:, o, :256]),
                                reads=[("obf", o)], dma=True)
                            return
                        rope = even and A["w"] == 0 and (48 <= m < 68)
                        o = orot.next()
                        if not rope:
                            P.op("act", lambda e: e.activation(out=obf[:, o, :T], in_=ps[b][:, :T], func=AF.Copy),
                                 reads=[("ps", b)], writes=[("obf", o)])
                        else:
                            q = qrot.next()
                            P.op("act", lambda e: e.activation(out=q32[:, q, :], in_=ps[b][:, :], func=AF.Copy),
                                 reads=[("ps", b)], writes=[("q32", q)])
                            P.op("pe", lambda e: e.matmul(ps[7][:, :], lhsT=rm[:], rhs=q32[:, q, :], start=True, stop=True),
                                 reads=[("q32", q), "rm"], writes=[("ps", 7)])
                            P.op("dve", lambda e: e.tensor_tensor(out=rt1[:, q, :], in0=q32[:, q, :], in1=rc[:, j, :], op=ALU.mult),
                                 reads=[("q32", q), ("rc", j)], writes=[("rt1", q)])
                            P.op("dve", lambda e: e.tensor_tensor(out=rt2[:, q, :], in0=ps[7][:, :], in1=rs_[:, j, :], op=ALU.mult),
                                 reads=[("ps", 7), ("rs", j)], writes=[("rt2", q)])
                            P.op("dve", lambda e: e.tensor_tensor(out=obf[:, o, :], in0=rt1[:, q, :], in1=rt2[:, q, :], op=ALU.add),
                                 reads=[("rt1", q), ("rt2", q)], writes=[("obf", o)])
                        P.op("act", lambda e: e.dma_start(
                            out=PT[m * 128:(m + 1) * 128, tok0:tok0 + T], in_=obf[:, o, :T]), reads=[("obf", o)], dma=True)

                    gemm(P, ps, wb, 3, slotrot, W, KC, groups, acts, epi, bankrot, tag="w")
                P.emit()

        @_gate
        def phase_s2(layer, Xprev, Xnext, with_ctx):
            even = layer % 2 == 0
            W = (even_w_out if even else odd_w_out)[layer // 2]
            OTv = OT.rearrange("(c p) t -> p c t", p=128)
            with contextlib.ExitStack() as es:
                act_ = es.enter_context(SBT("act", [128, 2, KC, 512], BF16))
                actc = es.enter_context(SBT("actc", [128, KC, 256], BF16))
                act = [act_[:, 0], act_[:, 1], actc]
                wb = es.enter_context(SBT("wb", [128, 3, KC, 256], BF16))
                R = Resid(es, 3, sbase=5)
                bankrot = Rot(5)
                slotrot = Rot(3)
                groups = [(c0, "f") for c0 in range(0, D, 256)]
                for tiles in tile_groups(with_ctx):
                    acts = []
                    for j, (tok0, T, w) in enumerate(tiles):
                        for q in range(4):
                            P.op("sp", lambda e, j=j, q=q, tok0=tok0, T=T: e.dma_start(
                                out=act[j][:, q * 8:(q + 1) * 8, :T], in_=OTv[:, q * 8:(q + 1) * 8, tok0:tok0 + T]),
                                writes=[("act", j, c) for c in range(q * 8, (q + 1) * 8)], dma=True)
                        acts.append(dict(ap=act[j], T=T, key=("act", j), tok0=tok0, w=w))
                    gemm(P, ps, wb, 3, slotrot, W, KC, groups, acts, R.epi, bankrot, tag="w")
                    R.post(acts, Xprev, Xnext, layer, 2)
                P.emit()

        @_gate
        def phase_s3(layer, Xsrc, with_ctx):
            W = ffn_w_up[layer]
            with contextlib.ExitStack() as es:
                bufs = alloc_norm_bufs(es)
                act_ = es.enter_context(SBT("act", [128, 2, KC, 512], BF16))
                actc = es.enter_context(SBT("actc", [128, KC, 256], BF16))
                act = [act_[:, 0], act_[:, 1], actc]
                wb = es.enter_context(SBT("wb", [128, 3, KC, 256], BF16))
                of = es.enter_context(SBT("of", [128, 4, 512], F32))
                bankrot = Rot(6)
                slotrot = Rot(3)
                orot = Rot(4)
                groups = [(c0, "f") for c0 in range(0, 2 * HID, 256)]
                for tiles in tile_groups(with_ctx):
                    acts = norm_prologue(bufs, Xsrc, tiles, layer, 3, 4, act)

                    def epi(m, j, A, b):
                        T = A["T"]
                        col = hpcol(A["tok0"])
                        o = orot.next()
                        P.op("act", lambda e: e.activation(out=of[:, o, :T], in_=ps[b][:, :T], func=AF.Copy),
                             reads=[("ps", b)], writes=[("of", o)])
                        P.op("act", lambda e: e.dma_start(out=HP[m * 128:(m + 1) * 128, col:col + T], in_=of[:, o, :T]),
                             reads=[("of", o)], dma=True)
                    gemm(P, ps, wb, 3, slotrot, W, KC, groups, acts, epi, bankrot, tag="w")
                P.emit()

        @_gate
        def phase_s4(layer, Xprev, Xnext, with_ctx):
            W = ffn_w_down[layer]
            KD = HID // 128
            with contextlib.ExitStack() as es:
                act = es.enter_context(SBT("act", [128, 2, KD, 512], BF16))
                wb = es.enter_context(SBT("wb", [128, 2, KD, 256], BF16))
                gs = es.enter_context(SBT("gs", [128, 2, 514], F32))
                vs = es.enter_context(SBT("vs", [128, 2, 514], F32))
                ca = es.enter_context(SBT("ca", [128, 2, 512], F32))
                cb = es.enter_context(SBT("cb", [128, 2, 512], F32))
                cc_ = es.enter_context(SBT("cc", [128, 2, 512], F32))
                cd_ = es.enter_context(SBT("cd", [128, 2, 512], F32))
                sg = es.enter_context(SBT("sg", [128, 2, 512], F32))
                R = Resid(es, 2)
                bankrot = Rot(6)
                slotrot = Rot(2)
                crot = Rot(2)
                groups = [(c0, "f") for c0 in range(0, D, 256)]
                if True:
                    def conv_tile(j, tok0, T, w):
                        col = hpcol(tok0)
                        for c in range(KD):
                            x = crot.next()
                            P.op("sp", lambda e, x=x, c=c: e.dma_start(out=gs[:, x, :T + 2], in_=HP[c * 128:(c + 1) * 128, col - 1:col + T + 1]),
                                 writes=[("gs", x)], dma=True)
                            P.op("sp", lambda e, x=x, c=c: e.dma_start(out=vs[:, x, :T + 2], in_=HP[(KD + c) * 128:(KD + c + 1) * 128, col - 1:col + T + 1]),
                                 writes=[("vs", x)], dma=True)
                            P.op("act", lambda e, x=x, c=c: e.activation(out=ca[:, x, :T], in_=gs[:, x, 1:T + 1], func=AF.Identity, scale=fcw[:, layer, c, 1:2]),
                                 reads=[("gs", x)], writes=[("ca", x)])
                            P.op("dve", lambda e, x=x, c=c: e.scalar_tensor_tensor(out=cb[:, x, :T], in0=gs[:, x, 0:T], scalar=fcw[:, layer, c, 0:1], in1=ca[:, x, :T], op0=ALU.mult, op1=ALU.add),
                                 reads=[("gs", x), ("ca", x)], writes=[("cb", x)])
                            P.op("dve", lambda e, x=x, c=c: e.scalar_tensor_tensor(out=ca[:, x, :T], in0=gs[:, x, 2:T + 2], scalar=fcw[:, layer, c, 2:3], in1=cb[:, x, :T], op0=ALU.mult, op1=ALU.add),
                                 reads=[("gs", x), ("cb", x)], writes=[("ca", x)])
                            P.op("act", lambda e, x=x: e.activation(out=sg[:, x, :T], in_=ca[:, x, :T], func=AF.Silu),
                                 reads=[("ca", x)], writes=[("sg", x)])
                            P.op("act", lambda e, x=x, c=c: e.activation(out=cc_[:, x, :T], in_=vs[:, x, 1:T + 1], func=AF.Identity, scale=fcw[:, layer, KD + c, 1:2]),
                                 reads=[("vs", x)], writes=[("cc", x)])
                            P.op("dve", lambda e, x=x, c=c: e.scalar_tensor_tensor(out=cd_[:, x, :T], in0=vs[:, x, 0:T], scalar=fcw[:, layer, KD + c, 0:1], in1=cc_[:, x, :T], op0=ALU.mult, op1=ALU.add),
                                 reads=[("vs", x), ("cc", x)], writes=[("cd", x)])
                            P.op("dve", lambda e, x=x, c=c: e.scalar_tensor_tensor(out=cc_[:, x, :T], in0=vs[:, x, 2:T + 2], scalar=fcw[:, layer, KD + c, 2:3], in1=cd_[:, x, :T], op0=ALU.mult, op1=ALU.add),
                                 reads=[("vs", x), ("cd", x)], writes=[("cc", x)])
                            P.op("dve", lambda e, x=x, c=c, j=j: e.tensor_tensor(out=act[:, j, c, :T], in0=sg[:, x, :T], in1=cc_[:, x, :T], op=ALU.mult),
                                 reads=[("sg", x), ("cc", x)], writes=[("act", j, c)])
                        return dict(ap=act[:, j], T=T, key=("act", j), tok0=tok0, w=w)
                    tl = [t[0] for t in tile_groups(with_ctx, per=1)]
                    nxt = conv_tile(0, *tl[0])
                    for gi in range(len(tl)):
                        cur = nxt
                        if gi + 1 < len(tl):
                            nxt = conv_tile((gi + 1) % 2, *tl[gi + 1])
                        gemm(P, ps, wb, 2, slotrot, W, KD, groups, [cur], R.epi, bankrot, tag="w")
                        R.post([cur], Xprev, Xnext, layer, 5)
                P.emit()

        @_gate
        def phase_even(layer, ctx_out):
            ei = layer // 2
            ncols = NTOK if ctx_out else S
            with contextlib.ExitStack() as es:
                qT = es.enter_context(SBT("qT", [128, 2, NTOK], BF16))
                kT = es.enter_context(SBT("kT", [128, 2, NTOK], BF16))
                vE = es.enter_context(SBT("vE", [128, 2, 34, 128], BF16))
                vO = es.enter_context(SBT("vO", [128, 2, 31, 128], BF16))
                tbl = es.enter_context(SBT("tbl", [128, 2, 14, 64], F32))
                oT = es.enter_context(SBT("oT", [128, 2, NTOK], BF16))
                sb = es.enter_context(SBT("sb", [128, 2, 4, 64], F32))
                pT = es.enter_context(SBT("pT", [128, 2, 640], BF16))
                pc = es.enter_context(SBT("pc", [128, 512], BF16))
                rec = es.enter_context(SBT("rec", [128, 2, 256], F32))
                dsb = es.enter_context(SBT("dsb", [128, 2, 256], F32))
                wm = es.enter_context(SBT("wm", [128, 256], F32))
                P.op("sp", lambda e: e.dma_start(out=wm[:], in_=wmask), writes=["wm"], dma=True)
                srot = Rot(4)
                orot = Rot(2)
                drot = Rot(2)
                xrot = Rot(2)
                VTa = VT[0:NTOK, :].rearrange("(n p) d -> p n d", p=128)
                VTo = VT[64:64 + 31 * 128, :].rearrange("(n p) d -> p n d", p=128)

                def load_v(dst_key, dst, slot, vcol, odd):
                    src = VTo if odd else VTa
                    n = 31 if odd else 34
                    half = n // 2
                    for (a, b_) in ((0, half), (half, n)):
                        P.op("sp", lambda e, a=a, b_=b_: e.dma_start(out=dst[:, slot, a:b_, :], in_=src[:, a:b_, vcol:vcol + 128]),
                             writes=[(dst_key, slot, a)], dma=True)
                    return [(dst_key, slot, 0), (dst_key, slot, half)]

                def ctx_attn(hb, kb, vb, vkeys, sink_ap):
                    sbk = srot.next()
                    for cc in range(2):
                        P.op("pe", lambda e, cc=cc: e.matmul(ps[sbk][:, cc * 256:(cc + 1) * 256],
                                                          lhsT=kT[:, kb, S + cc * 128:S + (cc + 1) * 128], rhs=qT[:, hb, S:NTOK],
                                                          start=True, stop=True),
                             reads=[("qT", hb), ("kT", kb)], writes=[("ps", sbk)])
                    P.op("act", lambda e: e.activation(out=pc[:, :], in_=ps[sbk][:, :], func=AF.Exp, scale=SCALE),
                         reads=[("ps", sbk)], writes=["pc"])
                    ob = 4 + orot.next()
                    db = 6 + drot.next()
                    for cc in range(2):
                        P.op("pe", lambda e, cc=cc: e.matmul(ps[ob][:, 0:256], lhsT=vE[:, vb, 32 + cc, :], rhs=pc[:, cc * 256:(cc + 1) * 256],
                                                          start=(cc == 0), stop=(cc == 1)),
                             reads=["pc"] + vkeys, writes=[("ps", ob)])
                    for cc in range(2):
                        P.op("pe", lambda e, cc=cc: e.matmul(ps[db][:, 0:256], lhsT=onesbf[:], rhs=pc[:, cc * 256:(cc + 1) * 256],
                                                          start=(cc == 0), stop=(cc == 1)),
                             reads=["pc", "onesbf"], writes=[("ps", db)])
                    x = xrot.next()
                    if sink_ap is not None:
                        P.op("dve", lambda e: e.tensor_scalar(out=dsb[:, x, :], in0=ps[db][:, 0:256], scalar1=sink_ap, scalar2=None, op0=ALU.add),
                             reads=[("ps", db), "esink"], writes=[("dsb", x)])
                        P.op("dve", lambda e: e.reciprocal(out=rec[:, x, :], in_=dsb[:, x, :]), reads=[("dsb", x)], writes=[("rec", x)])
                    else:
                        P.op("dve", lambda e: e.reciprocal(out=rec[:, x, :], in_=ps[db][:, 0:256]), reads=[("ps", db)], writes=[("rec", x)])
                    P.op("dve", lambda e: e.tensor_tensor(out=oT[:, hb, S:NTOK], in0=ps[ob][:, 0:256], in1=rec[:, x, :], op=ALU.mult),
                         reads=[("ps", ob), ("rec", x)], writes=[("oT", hb, "ctx")])

                for h in range(16):
                    hb = h % 2
                    P.op("sp", lambda e, h=h, hb=hb: e.dma_start(out=qT[:, hb, :], in_=PT[h * 128:(h + 1) * 128, :]), writes=[("qT", hb)], dma=True)
                    P.op("sp", lambda e, h=h, hb=hb: e.dma_start(out=kT[:, hb, :], in_=PT[2048 + h * 128:2048 + (h + 1) * 128, :]), writes=[("kT", hb)], dma=True)
                    vek = load_v("vE", vE, hb, h * 128, False)
                    vok = load_v("vO", vO, hb, h * 128, True)
                    P.op("sp", lambda e, h=h, hb=hb: e.dma_start(out=tbl[:, hb], in_=nbias[ei, h]), writes=[("tbl", hb)], dma=True)
                    okeys = []
                    def _it(r):
                        rs = min(max(r - 4, 0), 56)
                        delta = rs - r + 7
                        sbk = srot.next()
                        for jc in range(4):
                            k0 = (rs + 2 * jc) * 64
                            P.op("pe", lambda e, jc=jc, k0=k0, r=r, sbk=sbk, hb=hb: e.matmul(
                                ps[sbk][:, jc * 64:(jc + 1) * 64], lhsT=kT[:, hb, k0:k0 + 128], rhs=qT[:, hb, r * 64:(r + 1) * 64],
                                start=True, stop=True), reads=[("qT", hb), ("kT", hb)], writes=[("ps", sbk)])
                        for cc in range(2):
                            P.op("pe", lambda e, cc=cc, r=r, sbk=sbk, hb=hb: e.matmul(
                                ps[sbk][:, 256 + cc * 64:256 + (cc + 1) * 64], lhsT=kT[:, hb, S + cc * 128:S + (cc + 1) * 128],
                                rhs=qT[:, hb, r * 64:(r + 1) * 64], start=True, stop=True),
                                reads=[("qT", hb), ("kT", hb)], writes=[("ps", sbk)])
                        yield
                        x = xrot.next()
                        P.op("dve", lambda e, x=x, sbk=sbk, hb=hb, delta=delta: e.scalar_tensor_tensor(
                            out=sb[:, x], in0=ps[sbk][:, 0:256].rearrange("p (a b) -> p a b", b=64), scalar=SCALE,
                            in1=tbl[:, hb, delta:delta + 7:2, :], op0=ALU.mult, op1=ALU.add),
                            reads=[("ps", sbk), ("tbl", hb)], writes=[("sb", x)])
                        P.op("act", lambda e, x=x: e.activation(out=pT[:, x, 0:256], in_=sb[:, x].rearrange("p a b -> p (a b)"), func=AF.Exp),
                             reads=[("sb", x)], writes=[("pT", x, 0)])
                        P.op("act", lambda e, x=x, sbk=sbk: e.activation(out=pT[:, x, 256:384], in_=ps[sbk][:, 256:384], func=AF.Exp, scale=SCALE),
                             reads=[("ps", sbk)], writes=[("pT", x, 1)])
                        ob = 4 + orot.next()
                        db = 6 + drot.next()
                        for c in range(6):
                            if c < 4:
                                if rs % 2 == 0:
                                    lh = vE[:, hb, rs // 2 + c, :]
                                    vk = vek
                                else:
                                    lh = vO[:, hb, (rs - 1) // 2 + c, :]
                                    vk = vok
                            else:
                                lh = vE[:, hb, 32 + (c - 4), :]
                                vk = vek
                            P.op("pe", lambda e, c=c, lh=lh, x=x, ob=ob: e.matmul(ps[ob][:, 0:64], lhsT=lh, rhs=pT[:, x, c * 64:(c + 1) * 64],
                                                                               start=(c == 0), stop=(c == 5)),
                                 reads=[("pT", x, 0), ("pT", x, 1)] + vk, writes=[("ps", ob)])
                        for c in range(6):
                            P.op("pe", lambda e, c=c, x=x, db=db: e.matmul(ps[db][:, 0:64], lhsT=onesbf[:], rhs=pT[:, x, c * 64:(c + 1) * 64],
                                                                         start=(c == 0), stop=(c == 5)),
                                 reads=[("pT", x, 0), ("pT", x, 1), "onesbf"], writes=[("ps", db)])
                        P.op("dve", lambda e, x=x, db=db: e.reciprocal(out=rec[:, x, 0:64], in_=ps[db][:, 0:64]), reads=[("ps", db)], writes=[("rec", x)])
                        P.op("dve", lambda e, x=x, ob=ob, hb=hb, r=r: e.tensor_tensor(out=oT[:, hb, r * 64:(r + 1) * 64], in0=ps[ob][:, 0:64], in1=rec[:, x, 0:64], op=ALU.mult),
                             reads=[("ps", ob), ("rec", x)], writes=[("oT", hb, r)])
                        okeys.append(("oT", hb, r))
                    _gens = [_it(_v) for _v in range(64)]
                    next(_gens[0])
                    for _v in range(64):
                        if _v + 1 < 64:
                            next(_gens[_v + 1])
                        for _ in _gens[_v]:
                            pass
                    if ctx_out:
                        ctx_attn(hb, hb, hb, vek, None)
                        okeys.append(("oT", hb, "ctx"))
                    P.op("sp", lambda e, h=h, hb=hb: e.dma_start(out=OT[h * 128:(h + 1) * 128, 0:ncols], in_=oT[:, hb, 0:ncols]),
                         reads=okeys, dma=True)

                for g in range(4):
                    gb = g % 2
                    P.op("sp", lambda e, g=g, gb=gb: e.dma_start(out=kT[:, gb, :], in_=PT[8192 + g * 128:8192 + (g + 1) * 128, :]), writes=[("kT", gb)], dma=True)
                    vek = load_v("vE", vE, gb, 2048 + g * 128, False)
                    for hh in range(4):
                        h = 4 * g + hh
                        hb = h % 2
                        P.op("sp", lambda e, h=h, hb=hb: e.dma_start(out=qT[:, hb, :], in_=PT[6144 + h * 128:6144 + (h + 1) * 128, :]), writes=[("qT", hb)], dma=True)
                        okeys = []
                        def _it(n):
                            pb = srot.next() % 2
                            b0, b1 = 2 * pb, 2 * pb + 1
                            chunks = []
                            if n > 0:
                                chunks.append((b0, 0, n - 1))
                            if n < 31:
                                chunks.append((b0, 128, n + 1))
                            chunks += [(b0, 256, n), (b0, 384, 32), (b1, 0, 33)]
                            for (bk, col, kbk) in chunks:
                                P.op("pe", lambda e, bk=bk, col=col, kbk=kbk, n=n, hb=hb, gb=gb: e.matmul(
                                    ps[bk][:, col:col + 128], lhsT=kT[:, gb, kbk * 128:(kbk + 1) * 128], rhs=qT[:, hb, n * 128:(n + 1) * 128],
                                    start=True, stop=True), reads=[("qT", hb), ("kT", gb)], writes=[("ps", bk)])
                            yield
                            x = xrot.next()
                            lo = 0 if n > 0 else 128
                            hi = 256 if n < 31 else 128
                            sbf = sb[:, x].rearrange("p a b -> p (a b)")
                            P.op("dve", lambda e, x=x, b0=b0, lo=lo, hi=hi, sbf=sbf: e.scalar_tensor_tensor(
                                out=sbf[:, lo:hi], in0=ps[b0][:, lo:hi], scalar=SCALE, in1=wm[:, lo:hi], op0=ALU.mult, op1=ALU.add),
                                reads=[("ps", b0), "wm"], writes=[("sb", x)])
                            P.op("act", lambda e, x=x, lo=lo, hi=hi, sbf=sbf: e.activation(out=pT[:, x, lo:hi], in_=sbf[:, lo:hi], func=AF.Exp),
                                 reads=[("sb", x)], writes=[("pT", x, 0)])
                            P.op("act", lambda e, x=x, b0=b0: e.activation(out=pT[:, x, 256:512], in_=ps[b0][:, 256:512], func=AF.Exp, scale=SCALE),
                                 reads=[("ps", b0)], writes=[("pT", x, 1)])
                            P.op("act", lambda e, x=x, b1=b1: e.activation(out=pT[:, x, 512:640], in_=ps[b1][:, 0:128], func=AF.Exp, scale=SCALE),
                                 reads=[("ps", b1)], writes=[("pT", x, 2)])
                            ob = 4 + orot.next()
                            db = 6 + drot.next()
                            pcols = [(col if bk == b0 else 512, kbk) for (bk, col, kbk) in chunks]
                            nch = len(pcols)
                            for ci, (pcol, kbk) in enumerate(pcols):
                                P.op("pe", lambda e, ci=ci, pcol=pcol, kbk=kbk, x=x, ob=ob, gb=gb, nch=nch: e.matmul(
                                    ps[ob][:, 0:128], lhsT=vE[:, gb, kbk, :], rhs=pT[:, x, pcol:pcol + 128], start=(ci == 0), stop=(ci == nch - 1)),
                                    reads=[("pT", x, 0), ("pT", x, 1), ("pT", x, 2)] + vek, writes=[("ps", ob)])
                            for ci, (pcol, kbk) in enumerate(pcols):
                                P.op("pe", lambda e, ci=ci, pcol=pcol, x=x, db=db, nch=nch: e.matmul(
                                    ps[db][:, 0:128], lhsT=onesbf[:], rhs=pT[:, x, pcol:pcol + 128], start=(ci == 0), stop=(ci == nch - 1)),
                                    reads=[("pT", x, 0), ("pT", x, 1), ("pT", x, 2), "onesbf"], writes=[("ps", db)])
                            P.op("dve", lambda e, x=x, db=db, h=h: e.tensor_scalar(out=dsb[:, x, 0:128], in0=ps[db][:, 0:128], scalar1=esink[:, ei, h:h + 1], scalar2=None, op0=ALU.add),
                                 reads=[("ps", db), "esink"], writes=[("dsb", x)])
                            P.op("dve", lambda e, x=x: e.reciprocal(out=rec[:, x, 0:128], in_=dsb[:, x, 0:128]), reads=[("dsb", x)], writes=[("rec", x)])
                            P.op("dve", lambda e, x=x, ob=ob, hb=hb, n=n: e.tensor_tensor(out=oT[:, hb, n * 128:(n + 1) * 128], in0=ps[ob][:, 0:128], in1=rec[:, x, 0:128], op=ALU.mult),
                                 reads=[("ps", ob), ("rec", x)], writes=[("oT", hb, n)])
                            okeys.append(("oT", hb, n))
                        _gens = [_it(_v) for _v in range(32)]
                        next(_gens[0])
                        for _v in range(32):
                            if _v + 1 < 32:
                                next(_gens[_v + 1])
                            for _ in _gens[_v]:
                                pass
                        if ctx_out:
                            ctx_attn(hb, gb, gb, vek, esink[:, ei, h:h + 1])
                            okeys.append(("oT", hb, "ctx"))
                        P.op("sp", lambda e, h=h, hb=hb: e.dma_start(out=OT[2048 + h * 128:2048 + (h + 1) * 128, 0:ncols], in_=oT[:, hb, 0:ncols]),
                             reads=okeys, dma=True)
                P.emit()

        @_gate
        def phase_fnet(layer, ctx_out):
            ntile = 34 if ctx_out else 32
            ncols = NTOK if ctx_out else S
            sc_main = float((S * 128) ** -0.5)
            sc_ctx = float((L * 128) ** -0.5)
            dcv = dftc.rearrange("(c p) k -> p c k", p=128)
            dsv = dfts.rearrange("(c p) k -> p c k", p=128)
            with contextlib.ExitStack() as es:
                uT = es.enter_context(SBT("uT", [128, 2, NTOK], BF16))
                cd = es.enter_context(SBT("cdm", [128, 256], BF16))
                ucs = es.enter_context(SBT("ucs", [128, 4, 34, 256], BF16))
                dft = es.enter_context(SBT("dft", [128, 2, 2, 32, 256], BF16))
                d256 = es.enter_context(SBT("d256", [128, 2, 2, 256], BF16))
                fo = es.enter_context(SBT("fo", [128, 4, 256], BF16))
                P.op("sp", lambda e: e.dma_start(out=cd[:], in_=cdsd), writes=["cd"], dma=True)
                P.op("sp", lambda e: e.dma_start(out=d256[:], in_=dft256.rearrange("w (c p) k -> p w c k", p=128)), writes=["d256"], dma=True)
                bankrot = Rot(8)
                forot = Rot(4)
                dslot = Rot(2)
                for gb in range(4):
                    for gi in range(4):
                        g = 4 * gb + gi
                        ub = g % 2
                        P.op("sp", lambda e, g=g, ub=ub: e.dma_start(out=uT[:, ub, 0:ncols], in_=PT[g * 128:(g + 1) * 128, 0:ncols]), writes=[("uT", ub)], dma=True)
                        for t in range(ntile):
                            b = bankrot.next()
                            P.op("pe", lambda e, b=b, t=t, ub=ub: e.matmul(ps[b][:, 0:256], lhsT=uT[:, ub, t * 128:(t + 1) * 128], rhs=cd[:, :], start=True, stop=True),
                                 reads=[("uT", ub), "cd"], writes=[("ps", b)])
                            eng = "act" if t % 2 == 0 else "dve"
                            if eng == "act":
                                P.op("act", lambda e, b=b, t=t, gi=gi: e.activation(out=ucs[:, gi, t, :], in_=ps[b][:, 0:256], func=AF.Copy),
                                     reads=[("ps", b)], writes=[("ucs", gi, t)])
                            else:
                                P.op("dve", lambda e, b=b, t=t, gi=gi: e.tensor_copy(out=ucs[:, gi, t, :], in_=ps[b][:, 0:256]),
                                     reads=[("ps", b)], writes=[("ucs", gi, t)])
                    for kt in range(16):
                        sl = dslot.next()
                        for q in range(4):
                            P.op("sp", lambda e, sl=sl, q=q, kt=kt: e.dma_start(out=dft[:, sl, 0, q * 8:(q + 1) * 8, :], in_=dcv[:, q * 8:(q + 1) * 8, kt * 256:(kt + 1) * 256]),
                                 writes=[("dft", sl, 0, q)], dma=True)
                            P.op("sp", lambda e, sl=sl, q=q, kt=kt: e.dma_start(out=dft[:, sl, 1, q * 8:(q + 1) * 8, :], in_=dsv[:, q * 8:(q + 1) * 8, kt * 256:(kt + 1) * 256]),
                                 writes=[("dft", sl, 1, q)], dma=True)
                        for gi in range(4):
                            g = 4 * gb + gi
                            b = bankrot.next()
                            for tc in range(32):
                                P.op("pe", lambda e, b=b, tc=tc, gi=gi, sl=sl: e.matmul(ps[b][:, 0:256], lhsT=ucs[:, gi, tc, 0:128], rhs=dft[:, sl, 0, tc, :], start=(tc == 0), stop=False),
                                     reads=[("ucs", gi, tc), ("dft", sl, 0, tc // 8)], writes=[("ps", b)])
                                P.op("pe", lambda e, b=b, tc=tc, gi=gi, sl=sl: e.matmul(ps[b][:, 0:256], lhsT=ucs[:, gi, tc, 128:256], rhs=dft[:, sl, 1, tc, :], start=False, stop=(tc == 31)),
                                     reads=[("ucs", gi, tc), ("dft", sl, 1, tc // 8)], writes=[("ps", b)])
                            o = forot.next()
                            P.op("act", lambda e, b=b, o=o: e.activation(out=fo[:, o, :], in_=ps[b][:, 0:256], func=AF.Copy, scale=sc_main),
                                 reads=[("ps", b)], writes=[("fo", o)])
                            P.op("act", lambda e, o=o, g=g, kt=kt: e.dma_start(out=OT[g * 128:(g + 1) * 128, kt * 256:(kt + 1) * 256], in_=fo[:, o, :]),
                                 reads=[("fo", o)], dma=True)
                    if ctx_out:
                        for gi in range(4):
                            g = 4 * gb + gi
                            b = bankrot.next()
                            for tc in range(2):
                                P.op("pe", lambda e, b=b, tc=tc, gi=gi: e.matmul(ps[b][:, 0:256], lhsT=ucs[:, gi, 32 + tc, 0:128], rhs=d256[:, 0, tc, :], start=(tc == 0), stop=False),
                                     reads=[("ucs", gi, 32 + tc), "d256"], writes=[("ps", b)])
                                P.op("pe", lambda e, b=b, tc=tc, gi=gi: e.matmul(ps[b][:, 0:256], lhsT=ucs[:, gi, 32 + tc, 128:256], rhs=d256[:, 1, tc, :], start=False, stop=(tc == 1)),
                                     reads=[("ucs", gi, 32 + tc), "d256"], writes=[("ps", b)])
                            o = forot.next()
                            P.op("act", lambda e, b=b, o=o: e.activation(out=fo[:, o, :], in_=ps[b][:, 0:256], func=AF.Copy, scale=sc_ctx),
                                 reads=[("ps", b)], writes=[("fo", o)])
                            P.op("sp", lambda e, o=o, g=g: e.dma_start(out=OT[g * 128:(g + 1) * 128, S:NTOK], in_=fo[:, o, :]),
                                 reads=[("fo", o)], dma=True)
                P.emit()

        @_gate
        def phase_sconv(layer, ctx_out):
            oi = layer // 2
            ncols = NTOK if ctx_out else S
            MW = NTOK + 6
            segs = [(1, 0, S)]
            if ctx_out:
                segs.append((S + 3, S, L))
            with contextlib.ExitStack() as es:
                bgT = es.enter_context(SBT("bgT", [128, 2, NTOK], BF16))
                cgT = es.enter_context(SBT("cgT", [128, 2, NTOK], BF16))
                hvT = es.enter_context(SBT("hvT", [128, 2, NTOK], BF16))
                mm = es.enter_context(SBT("mm", [128, 2, MW], F32))
                a0 = es.enter_context(SBT("a0", [128, NTOK], F32))
                a1 = es.enter_context(SBT("a1", [128, NTOK], F32))
                so = es.enter_context(SBT("so", [128, 2, NTOK], BF16))
                for s_ in range(2):
                    P.op("pool", lambda e, s_=s_: e.memset(mm[:, s_, :], 0.0), writes=[("mm", s_)])
                for c in range(16):
                    sl = c % 2
                    for (buf, key, row0) in ((bgT, "bg", 2048), (cgT, "cg", 4096), (hvT, "hv", 6144)):
                        P.op("sp", lambda e, buf=buf, row0=row0, c=c, sl=sl: e.dma_start(out=buf[:, sl, 0:ncols], in_=PT[row0 + c * 128:row0 + (c + 1) * 128, 0:ncols]),
                             writes=[(key, sl)], dma=True)
                    for (mo, to, n) in segs:
                        P.op("dve", lambda e, sl=sl, mo=mo, to=to, n=n: e.tensor_tensor(out=mm[:, sl, mo:mo + n], in0=cgT[:, sl, to:to + n], in1=hvT[:, sl, to:to + n], op=ALU.mult),
                             reads=[("cg", sl), ("hv", sl)], writes=[("mm", sl)])
                    for (mo, to, n) in segs:
                        P.op("act", lambda e, sl=sl, mo=mo, to=to, n=n, c=c: e.activation(out=a0[:, to:to + n], in_=mm[:, sl, mo:mo + n], func=AF.Identity, scale=ocw[:, oi, c, 1:2]),
                             reads=[("mm", sl)], writes=[("a0", to)])
                        P.op("dve", lambda e, sl=sl, mo=mo, to=to, n=n, c=c: e.scalar_tensor_tensor(out=a1[:, to:to + n], in0=mm[:, sl, mo - 1:mo - 1 + n], scalar=ocw[:, oi, c, 0:1], in1=a0[:, to:to + n], op0=ALU.mult, op1=ALU.add),
                             reads=[("mm", sl), ("a0", to)], writes=[("a1", to)])
                        P.op("dve", lambda e, sl=sl, mo=mo, to=to, n=n, c=c: e.scalar_tensor_tensor(out=a0[:, to:to + n], in0=mm[:, sl, mo + 1:mo + 1 + n], scalar=ocw[:, oi, c, 2:3], in1=a1[:, to:to + n], op0=ALU.mult, op1=ALU.add),
                             reads=[("mm", sl), ("a1", to)], writes=[("a0", to)])
                        P.op("dve", lambda e, sl=sl, to=to, n=n: e.tensor_tensor(out=so[:, sl, to:to + n], in0=a0[:, to:to + n], in1=bgT[:, sl, to:to + n], op=ALU.mult),
                             reads=[("a0", to), ("bg", sl)], writes=[("so", sl, to)])
                    P.op("sp", lambda e, sl=sl, c=c: e.dma_start(out=OT[2048 + c * 128:2048 + (c + 1) * 128, 0:ncols], in_=so[:, sl, 0:ncols]),
                         reads=[("so", sl, to) for (mo, to, n) in segs], dma=True)
                P.emit()

        Xa = xin
        for layer in range(N_LAYERS_RUN):
            ctx_live = any(j % 2 == 0 for j in range(layer + 1, DEPTH))
            need_hc = (layer % 2 == 0) or ctx_live
            phase_s1(layer, Xa, need_hc)
            if layer % 2 == 0:
                phase_even(layer, ctx_live)
            else:
                phase_fnet(layer, ctx_live)
                phase_sconv(layer, ctx_live)
            phase_s2(layer, Xa, xB, ctx_live)
            phase_s3(layer, xB, ctx_live)
            Xn = yout if layer == N_LAYERS_RUN - 1 else xC
            phase_s4(layer, xB, Xn, ctx_live and (layer != N_LAYERS_RUN - 1))
            Xa = xC
        P.emit(final=True)
        nops = P.nops
    return nc, nops


N_LAYERS_RUN = DEPTH
_NC = [NCORES]
_RUNKW = {}
_LAST = [None]
_DEBUG = [False]
_STOP = [None]


def _host_consts():
    t = np.arange(S)
    row = (t // 64).astype(np.float32)
    col = (t % 64).astype(np.float32)
    nf = 32
    inv = (10000.0 ** (-np.arange(nf, dtype=np.float32) / nf)).astype(np.float32)
    ropec = np.zeros((128, S), np.float32)
    ropes = np.zeros((128, S), np.float32)
    rotm = np.zeros((128, 128), np.float32)
    for a, pos in enumerate((row, col)):
        ang = pos[None, :] * inv[:, None]
        for j in range(2):
            d0 = a * 64 + j * 32
            ropec[d0:d0 + 32] = np.cos(ang)
            ropes[d0:d0 + 32] = -np.sin(ang) if j == 0 else np.sin(ang)
        for f in range(nf):
            rotm[a * 64 + f, a * 64 + 32 + f] = 1.0
            rotm[a * 64 + 32 + f, a * 64 + f] = 1.0
    kl = np.arange(128)[:, None]
    ql = np.arange(128)[None, :]
    wmask = np.zeros((128, 256), np.float32)
    wmask[:, 0:128] = np.where(kl >= ql, 0.0, NEG)
    wmask[:, 128:256] = np.where(kl <= ql, 0.0, NEG)
    d = np.arange(128)
    angd = 2 * np.pi * np.outer(d, d) / 128.0
    cdsd = np.concatenate([np.cos(angd), np.sin(angd)], axis=1).astype(ml_dtypes.bfloat16)
    tt = np.arange(S, dtype=np.int64)
    prod = (np.outer(tt, tt) % S).astype(np.float64) * (2 * np.pi / S)
    dftc = np.cos(prod).astype(ml_dtypes.bfloat16)
    dfts = (-np.sin(prod)).astype(ml_dtypes.bfloat16)
    t2 = np.arange(L, dtype=np.int64)
    p2 = (np.outer(t2, t2) % L).astype(np.float64) * (2 * np.pi / L)
    dft256 = np.stack([np.cos(p2), -np.sin(p2)]).astype(ml_dtypes.bfloat16)
    return dict(ropec=ropec, ropes=ropes, rotm=rotm, wmask=wmask, cdsd=cdsd, dftc=dftc, dfts=dfts, dft256=dft256)


def _nbias_table(rpb):
    E = rpb.shape[0]
    kc = np.arange(64)[:, None]
    c = np.arange(64)[None, :]
    cstart = np.clip(c - 8, 0, 48)
    ok = (kc >= cstart) & (kc < cstart + 16)
    rel = np.clip(kc - c, -15, 15) + 15
    out = np.full((E, 16, 128, 14, 64), NEG, np.float32)
    for jpar in range(2):
        for m in range(14):
            rr = m + jpar
            if rr > 14:
                continue
            g = rpb[:, :, rr, :][:, :, rel]
            out[:, :, jpar * 64:(jpar + 1) * 64, m, :] = np.where(ok[None, None], g, np.float32(NEG))
    return out


def kernel(x, c, ctx, c_ctx, w_mod, b_mod, g_mix_pre, g_mix_post, g_ffn_pre, g_ffn_post,
           even_w_in, even_rpb, even_sink, even_w_out, odd_w_in, odd_conv, odd_w_out,
           ffn_w_up, ffn_conv, ffn_w_down):
    f32 = np.float32
    x = np.asarray(x, f32); c = np.asarray(c, f32); ctx = np.asarray(ctx, f32); c_ctx = np.asarray(c_ctx, f32)
    nc, _ = build_program()
    consts = _host_consts()

    def pl(v, n):
        v = np.asarray(v, f32)
        lead = v.shape[:-1]
        return np.ascontiguousarray(np.moveaxis(v.reshape(lead + (n, 128)), -1, 0))

    shared = dict(
        w_mod=np.asarray(w_mod, f32),
        bmod=pl(b_mod, 192),
        gvec=np.ascontiguousarray(np.stack([pl(g_mix_pre, KC), pl(g_mix_post, KC), pl(g_ffn_pre, KC), pl(g_ffn_post, KC)], axis=1)),
        even_w_in=np.asarray(even_w_in, f32), even_w_out=np.asarray(even_w_out, f32),
        odd_w_in=np.asarray(odd_w_in, f32), odd_w_out=np.asarray(odd_w_out, f32),
        ffn_w_up=np.asarray(ffn_w_up, f32), ffn_w_down=np.asarray(ffn_w_down, f32),
        nbias=_nbias_table(np.asarray(even_rpb, f32)),
        sinkb=np.ascontiguousarray(np.broadcast_to(np.asarray(even_sink, f32)[None], (128, 2, 16))),
        oconv=np.ascontiguousarray(np.transpose(np.asarray(odd_conv, f32).reshape(2, 3, 16, 128), (3, 0, 2, 1))),
        fconv=np.ascontiguousarray(np.transpose(np.asarray(ffn_conv, f32).reshape(DEPTH, 3, 88, 128), (3, 0, 2, 1))),
        **consts,
    )
    in_maps = []
    for b in range(_NC[0]):
        xin = np.ascontiguousarray(np.concatenate([x[b], ctx[b]], axis=0).T)
        cv = np.ascontiguousarray(np.stack([c[b].reshape(KC, 128).T, c_ctx.reshape(KC, 128).T], axis=-1))
        m = dict(shared)
        m["xin"] = xin
        m["cvec"] = cv
        in_maps.append(m)
    res = run_bass_kernel_spmd(nc, in_maps, core_ids=list(range(_NC[0])), **_RUNKW)
    _LAST[0] = res
    out = np.stack([np.ascontiguousarray(res.results[b]["yout"].T) for b in range(_NC[0])], axis=0)
    return out.astype(f32)
```
